# Optimizing a Trainium2 kernel written in Bass

```python
import jax, jax.numpy as jnp
from jax import lax
import numpy as np

D_MODEL = 1024
BATCH = 8
SEQ = 2048
DEPTH = 1

MEM_LEN = 256
CONV_W = D_MODEL
CONV_K = 3
SB_HEADS = 8
SB_HEAD_DIM = 128
SB_W = SB_HEADS * SB_HEAD_DIM
Q_BLOCK = 128
X_HEADS = 4
X_HEAD_DIM = D_MODEL // X_HEADS
D_FF = 2816
RMS_EPS = 1e-6
PROJ_SIZES = (CONV_W, CONV_W, CONV_W, SB_W, SB_W, SB_W, D_MODEL, D_MODEL)

kernel_name = 'hybrid_conv_stickbreak_macaron_layer'


def rms_norm(x, g):
    xf = x.astype(jnp.float32)
    y = xf * lax.rsqrt(jnp.mean(xf * xf, axis=-1, keepdims=True) + RMS_EPS)
    return (y * g.astype(jnp.float32)).astype(x.dtype)


def swiglu(x, w_gu, w_down):
    gate, up = jnp.split(x @ w_gu, 2, axis=-1)
    return (jax.nn.silu(gate) * up) @ w_down


def short_conv(x, w):
    c = x.shape[-1]
    return lax.conv_general_dilated(
        x, w[:, None, :].astype(x.dtype), window_strides=(1,), padding=[(CONV_K - 1, 0)],
        dimension_numbers=('NWC', 'WIO', 'NWC'), feature_group_count=c)


def stick_breaking_attention(q, k, v):
    seq = q.shape[2]
    scale = SB_HEAD_DIM ** -0.5
    outs = []
    for blk in range(seq // Q_BLOCK):
        start = blk * Q_BLOCK
        end = start + Q_BLOCK
        z = jnp.einsum('bhqd,bhkd->bhqk', q[:, :, start:end], k[:, :, :end]).astype(jnp.float32) * scale
        t_pos = start + jnp.arange(Q_BLOCK)[:, None]
        s_pos = jnp.arange(end)[None, :]
        causal = s_pos < t_pos
        log_1m_beta = jnp.where(causal, jax.nn.log_sigmoid(-z), 0.0)
        after = lax.cumsum(log_1m_beta, axis=3, reverse=True) - log_1m_beta
        a = jnp.where(causal, jnp.exp(jax.nn.log_sigmoid(z) + after), 0.0)
        outs.append(jnp.einsum('bhqk,bhkd->bhqd', a, v[:, :, :end].astype(jnp.float32)))
    return jnp.concatenate(outs, axis=2).astype(q.dtype)


def memory_cross_attention(hn, mn, w_cq, w_ckv, w_co):
    b, s, _ = hn.shape
    m = mn.shape[1]
    q = (hn @ w_cq).reshape(b, s, X_HEADS, X_HEAD_DIM)
    k, v = jnp.split(mn @ w_ckv, 2, axis=-1)
    k = k.reshape(b, m, X_HEADS, X_HEAD_DIM)
    v = v.reshape(b, m, X_HEADS, X_HEAD_DIM)
    scores = jnp.einsum('bshd,bmhd->bhsm', q, k).astype(jnp.float32) * (X_HEAD_DIM ** -0.5)
    p = jax.nn.softmax(scores, axis=-1)
    o = jnp.einsum('bhsm,bmhd->bshd', p, v.astype(jnp.float32)).astype(hn.dtype)
    return o.reshape(b, s, D_MODEL) @ w_co


def hybrid_mixer(u, w_in, b_gate, conv_w, w_conv_out, w_attn_out, w_o):
    b, s, _ = u.shape
    split_at = np.cumsum(PROJ_SIZES)[:6].tolist()
    cb, cc, cx, q, k, v, gates = jnp.split(u @ w_in, split_at, axis=-1)
    gate_pre = gates + b_gate
    g_conv, g_sb = jnp.split(jax.nn.sigmoid(gate_pre), 2, axis=-1)
    y_conv = cb * short_conv(cc * cx, conv_w)
    to_heads = lambda t: t.reshape(b, s, SB_HEADS, SB_HEAD_DIM).transpose(0, 2, 1, 3)
    y_sb = stick_breaking_attention(to_heads(q), to_heads(k), to_heads(v))
    y_sb = y_sb.transpose(0, 2, 1, 3).reshape(b, s, SB_W)
    merged = g_conv * (y_conv @ w_conv_out) + g_sb * (y_sb @ w_attn_out)
    return merged @ w_o


def setup_inputs(seed: int = 0) -> dict:
    key = jax.random.key(seed)
    ks = jax.random.split(key, 21)
    f32 = jnp.float32

    def dense(k, shape):
        return jax.random.normal(k, shape, f32) * (shape[0] ** -0.5)

    def gain(k):
        return 1.0 + 0.01 * jax.random.normal(k, (D_MODEL,), f32)

    return {
        'x': jax.random.normal(ks[0], (BATCH, SEQ, D_MODEL), f32),
        'mem': jax.random.normal(ks[1], (BATCH, MEM_LEN, D_MODEL), f32),
        'g_ffn1': gain(ks[2]),
        'w_ffn1_gu': dense(ks[3], (D_MODEL, 2 * D_FF)),
        'w_ffn1_down': dense(ks[4], (D_FF, D_MODEL)),
        'g_mix': gain(ks[5]),
        'w_in': dense(ks[6], (D_MODEL, sum(PROJ_SIZES))),
        'b_gate': 0.01 * jax.random.normal(ks[7], (2 * D_MODEL,), f32),
        'conv_w': jax.random.normal(ks[8], (CONV_K, CONV_W), f32) * (CONV_K ** -0.5),
        'w_conv_out': dense(ks[9], (CONV_W, D_MODEL)),
        'w_attn_out': dense(ks[10], (SB_W, D_MODEL)),
        'w_o': dense(ks[11], (D_MODEL, D_MODEL)),
        'g_cross': gain(ks[12]),
        'g_mem': gain(ks[13]),
        'w_cq': dense(ks[14], (D_MODEL, D_MODEL)),
        'w_ckv': dense(ks[15], (D_MODEL, 2 * D_MODEL)),
        'w_co': dense(ks[16], (D_MODEL, D_MODEL)),
        'g_ffn2': gain(ks[17]),
        'w_ffn2_gu': dense(ks[18], (D_MODEL, 2 * D_FF)),
        'w_ffn2_down': dense(ks[19], (D_FF, D_MODEL)),
        'g_final': gain(ks[20]),
    }


def reference(x, mem, g_ffn1, w_ffn1_gu, w_ffn1_down, g_mix, w_in, b_gate, conv_w,
              w_conv_out, w_attn_out, w_o, g_cross, g_mem, w_cq, w_ckv, w_co,
              g_ffn2, w_ffn2_gu, w_ffn2_down, g_final):
    h = x
    for _ in range(DEPTH):
        h = h + 0.5 * swiglu(rms_norm(h, g_ffn1), w_ffn1_gu, w_ffn1_down)
        h = h + hybrid_mixer(rms_norm(h, g_mix), w_in, b_gate, conv_w, w_conv_out, w_attn_out, w_o)
        h = h + memory_cross_attention(rms_norm(h, g_cross), rms_norm(mem, g_mem), w_cq, w_ckv, w_co)
        h = h + 0.5 * swiglu(rms_norm(h, g_ffn2), w_ffn2_gu, w_ffn2_down)
    return rms_norm(h, g_final)
```

```python
import os
import numpy as np
import ml_dtypes
import concourse.bass as bass
import concourse.mybir as mybir
from concourse.bass_utils import run_bass_kernel_spmd

F32, BF16 = mybir.dt.float32, mybir.dt.bfloat16
AF = mybir.ActivationFunctionType
ALU = mybir.AluOpType
AX = mybir.AxisListType

S = 2048
D = 1024
DFF = 2816
MEM = 256
NT = S // 128
EPS = 1e-6
ND = 8
NEG = -30000.0


class Op:
    __slots__ = ("eng", "fn", "deps", "needed", "sem", "val", "is_dma", "phase", "prev")


class Plan:
    def __init__(self):
        self.ops = []
        self.lastw = {}
        self.readers = {}
        self.phase = 0
        self.bank = 0
        self.reserved = set()

    def nextbank(self):
        while self.bank in self.reserved:
            self.bank = (self.bank + 1) % 8
        b = self.bank
        self.bank = (self.bank + 1) % 8
        return b

    def add(self, eng, fn, reads=(), writes=(), dma=False):
        op = Op()
        op.eng, op.fn, op.is_dma, op.needed, op.phase = eng, fn, dma, dma, self.phase
        op.prev = None
        deps, seen = [], set()

        def add_dep(d):
            if d is None or id(d) in seen:
                return
            if d.eng == "pe" and eng == "pe" and not d.is_dma and not dma:
                return
            seen.add(id(d))
            deps.append(d)
            d.needed = True

        for r in reads:
            add_dep(self.lastw.get(r))
        for w in writes:
            add_dep(self.lastw.get(w))
            rd = self.readers.get(w)
            if rd:
                for d in rd[0].values():
                    add_dep(d)
                for d in rd[1]:
                    add_dep(d)
        for w in writes:
            self.lastw[w] = op
            self.readers[w] = ({}, [])
        for r in reads:
            if r in writes:
                continue
            rd = self.readers.setdefault(r, ({}, []))
            if dma:
                rd[1].append(op)
            else:
                rd[0][eng] = op
        op.deps = deps
        self.ops.append(op)
        return op

    def next_phase(self):
        self.phase += 1


def build_program(stages=5, final_norm=True):
    nc = bass.Bass("TRN2", target_bir_lowering=False)
    P = Plan()

    def din(name, shape, dt=F32):
        return nc.dram_tensor(name, shape, dt, kind="ExternalInput").ap()

    x_d = din("x", [S, D])
    mem_d = din("mem", [MEM, D])
    gv_d = din("gv", [6, 128, D])
    w1gu_d = din("w_ffn1_gu", [D, 2 * DFF])
    w1d_d = din("w_ffn1_down", [DFF, D])
    win_d = din("w_in", [D, 8 * D])
    wco_d = din("w_conv_out", [D, D])
    wao_d = din("w_attn_out", [D, D])
    wo_d = din("w_o", [D, D])
    wcq_d = din("w_cq", [D, D])
    wckv_d = din("w_ckv", [D, 2 * D])
    wcout_d = din("w_co", [D, D])
    w2gu_d = din("w_ffn2_gu", [D, 2 * DFF])
    w2d_d = din("w_ffn2_down", [DFF, D])
    bg_d = din("bgate", [128, 16])
    cw_d = din("convw", [128, 8, 3])
    ident_d = din("ident", [128, 128], BF16)
    triA_d = din("triA", [128, 128], BF16)
    triB_d = din("triB", [128, 128], BF16)
    maskb_d = din("maskb", [128, 128], BF16)
    y_d = nc.dram_tensor("y", [S, D], F32, kind="ExternalOutput").ap()

    import contextlib

    gstack = contextlib.ExitStack()

    def salloc(stack, name, shape, dt):
        return stack.enter_context(nc.sbuf_tensor(name, shape, dt))

    H = salloc(gstack, "H", [128, NT, D], F32)
    H_ = H
    ident = salloc(gstack, "ident_s", [128, 128], BF16)
    triA = salloc(gstack, "triA_s", [128, 128], BF16)
    triB = salloc(gstack, "triB_s", [128, 128], BF16)
    maskb = salloc(gstack, "maskb_s", [128, 128], BF16)
    bgate = salloc(gstack, "bgate_s", [128, 16], F32)
    convw = salloc(gstack, "convw_s", [128, 8, 3], F32)
    gb = salloc(gstack, "gb", [128, D], F32)
    ss = salloc(gstack, "ss", [128, NT], F32)
    rstd = salloc(gstack, "rstd", [128, NT], F32)
    xn0 = salloc(gstack, "xn0", [128, D], BF16)
    XNB = [xn0]
    ps = [gstack.enter_context(nc.psum_tensor(f"ps{i}", [128, 512], F32)) for i in range(8)]

    sems = {}
    for e in ("pe", "act", "dve", "pool"):
        sems[e] = gstack.enter_context(nc.semaphore(f"s_{e}"))
    dsems = {}
    for q in ("sp", "pool", "act"):
        dsems[q] = [gstack.enter_context(nc.semaphore(f"d_{q}{i}")) for i in range(ND)]

    EM = Emitter(nc, P, sems, dsems)

    def dma(q, out, in_, reads=(), writes=()):
        return P.add(q, lambda e: e.dma_start(out=out, in_=in_), reads=reads, writes=writes, dma=True)

    def load_gb(idx):
        dma("sp", gb[:], gv_d[idx], writes=[("gb",)])

    BLK = {}

    def defblk(key, parts):
        BLK[key] = (len(BLK), parts)

    for blk in range(2):
        defblk(("k", blk), [(win_d[:, 4096 + blk * 512:4096 + (blk + 1) * 512], 0, 512)])
    for blk in range(2):
        defblk(("q", blk), [(win_d[:, 3072 + blk * 512:3072 + (blk + 1) * 512], 0, 512)])
    for blk in range(2):
        defblk(("v", blk), [(win_d[:, 5120 + blk * 512:5120 + (blk + 1) * 512], 0, 512)])
    for c in range(8):
        defblk(("conv", c), [(win_d[:, j * 1024 + c * 128:j * 1024 + (c + 1) * 128], j * 128, 128) for j in range(3)])
    for fc in range(8):
        defblk(("op", fc), [(wco_d[:, fc * 128:(fc + 1) * 128], 0, 128), (wao_d[:, fc * 128:(fc + 1) * 128], 128, 128),
                            (win_d[:, 6144 + fc * 128:6144 + (fc + 1) * 128], 256, 128),
                            (win_d[:, 7168 + fc * 128:7168 + (fc + 1) * 128], 384, 128)])
    for dh in range(2):
        defblk(("wo", dh), [(wo_d[:, dh * 512:(dh + 1) * 512], 0, 512)])
    wsc = nc.dram_tensor("wscratch", [len(BLK), 128, 8, 512], BF16).ap()
    conv_dmas = []
    for key, (b, parts) in BLK.items():
        for (src, c0, n) in parts:
            conv_dmas.append(lambda b=b, src=src, c0=c0, n=n: dma(
                "pool", wsc[b][:, :, c0:c0 + n], src.rearrange("(k p) n -> p k n", p=128),
                writes=[("wsc", b, c0 // 128 + j) for j in range(n // 128)]))

    xn_ctr = [0]

    def norm_pieces(tiles, XT, col_of_tile, evac_eng="act", SRC=None, skey="H", xkey="XT"):
        H = SRC if SRC is not None else H_
        xk = (lambda c: xkey + (c,)) if isinstance(xkey, tuple) else (lambda c: (xkey, c))
        lo, hi = min(tiles), max(tiles) + 1

        def stats():
            for i in tiles:
                P.add("act", lambda e, i=i: e.activation(out=XNB[0][:], in_=H[:, i, :], func=AF.Square,
                                                         accum_out=ss[:, i:i + 1]),
                      reads=[(skey, i)], writes=[("ss", i), ("xn", 0)])
            P.add("dve", lambda e: e.tensor_scalar(out=rstd[:, lo:hi], in0=ss[:, lo:hi], scalar1=1.0 / D, scalar2=EPS,
                                                   op0=ALU.mult, op1=ALU.add),
                  reads=[("ss", i) for i in tiles], writes=[("rstd",)])
            P.add("act", lambda e: e.activation(out=rstd[:, lo:hi], in_=rstd[:, lo:hi], func=AF.Ln),
                  reads=[("rstd",)], writes=[("rstd",)])
            P.add("act", lambda e: e.activation(out=rstd[:, lo:hi], in_=rstd[:, lo:hi], func=AF.Exp, scale=-0.5),
                  reads=[("rstd",)], writes=[("rstd",)])

        bsel = {}

        def mult(i):
            b = xn_ctr[0] % len(XNB)
            xn_ctr[0] += 1
            bsel[i] = b
            xnb = XNB[b]
            P.add("dve", lambda e: e.scalar_tensor_tensor(out=xnb[:], in0=H[:, i, :], scalar=rstd[:, i:i + 1], in1=gb[:],
                                                          op0=ALU.mult, op1=ALU.mult),
                  reads=[(skey, i), ("rstd",), ("gb",)], writes=[("xn", b)])

        def trn(i):
            b = bsel[i]
            xnb = XNB[b]
            bank = P.nextbank()
            pview = ps[bank][:].bitcast(BF16)

            def tr(e):
                for k in range(8):
                    ins = e.transpose(pview[:, k * 128:(k + 1) * 128], xnb[:, k * 128:(k + 1) * 128], ident[:])
                return ins

            P.add("pe", tr, reads=[("xn", b), ("ident",)], writes=[("ps", bank)])
            c0 = col_of_tile(i)
            src = pview.rearrange("p (k c) -> p k c", k=8)
            dst = XT[:, :, c0:c0 + 128]
            if evac_eng == "act":
                P.add("act", lambda e: e.copy(out=dst, in_=src), reads=[("ps", bank)], writes=[xk(c0 // 128)])
            else:
                P.add("dve", lambda e: e.tensor_copy(out=dst, in_=src), reads=[("ps", bank)], writes=[xk(c0 // 128)])

        if len(XNB) == 1:
            return [stats] + [(lambda i=i: (mult(i), trn(i))) for i in tiles]
        pieces = [stats, (lambda: mult(tiles[0]))]
        for j in range(1, len(tiles)):
            pieces.append(lambda j=j: (mult(tiles[j]), trn(tiles[j - 1])))
        pieces.append(lambda: trn(tiles[-1]))
        return pieces

    def norm_tiles(*a, **kw):
        for p_ in norm_pieces(*a, **kw):
            p_()

    for a in range(4):
        dma("sp", H[:, 4 * a:4 * a + 4, :], x_d[512 * a:512 * (a + 1), :].rearrange("(i p) d -> p i d", p=128),
            writes=[("H", 4 * a + j) for j in range(4)])
    dma("sp", ident[:], ident_d, writes=[("ident",)])
    dma("sp", triA[:], triA_d, writes=[("triA",)])
    dma("sp", triB[:], triB_d, writes=[("triB",)])
    dma("sp", maskb[:], maskb_d, writes=[("maskb",)])
    dma("sp", bgate[:], bg_d, writes=[("bgate",)])
    dma("sp", convw[:], cw_d, writes=[("convw",)])

    def ffn_phase(wgu_d, wd_d, gidx, tag, fin=False, extra=None):
        extra = list(extra or [])
        per_group = (len(extra) + 11) // 12
        EM.flush()
        st = contextlib.ExitStack()
        if fin:
            gb2 = salloc(st, "gb2", [128, D], F32)
            ob = [salloc(st, f"obf{i}", [128, D], F32) for i in range(2)]
            ss2 = salloc(st, "ss2", [128, NT], F32)
            rstd2 = salloc(st, "rstd2", [128, NT], F32)
            dma("sp", gb2[:], gv_d[5], writes=[("gb2",)])
            epsT = salloc(st, "epsT", [128, 1], F32)
            P.add("dve", lambda e: e.memset(epsT[:], EPS), writes=[("epsT",)])

            def fin_stats(t):
                P.add("act", lambda e: e.activation(out=XNB[0][:], in_=H[:, t, :], func=AF.Square, accum_out=ss2[:, t:t + 1]),
                      reads=[("H", t)], writes=[("ss2", t), ("xn", 0)])
                P.add("act", lambda e: e.activation(out=rstd2[:, t:t + 1], in_=ss2[:, t:t + 1], func=AF.Ln, scale=1.0 / D,
                                                    bias=epsT[:, 0:1]),
                      reads=[("ss2", t), ("epsT",)], writes=[("rstd2", t)])
                P.add("act", lambda e: e.activation(out=rstd2[:, t:t + 1], in_=rstd2[:, t:t + 1], func=AF.Exp, scale=-0.5),
                      reads=[("rstd2", t)], writes=[("rstd2", t)])

            def fin_out(t):
                b = t % 2
                P.add("dve", lambda e: e.scalar_tensor_tensor(out=ob[b][:], in0=H[:, t, :], scalar=rstd2[:, t:t + 1], in1=gb2[:],
                                                              op0=ALU.mult, op1=ALU.mult),
                      reads=[("H", t), ("rstd2", t), ("gb2",)], writes=[("obf", b)])
                dma("sp", y_d[t * 128:(t + 1) * 128, :], ob[b][:], reads=[("obf", b)], writes=[("y", t)])

            def final_tiles(tiles):
                lo, hi = min(tiles), max(tiles) + 1
                for i in tiles:
                    P.add("act", lambda e, i=i: e.activation(out=XNB[0][:], in_=H[:, i, :], func=AF.Square,
                                                             accum_out=ss2[:, i:i + 1]),
                          reads=[("H", i)], writes=[("ss2", i), ("xn", 0)])
                P.add("dve", lambda e: e.tensor_scalar(out=rstd2[:, lo:hi], in0=ss2[:, lo:hi], scalar1=1.0 / D, scalar2=EPS,
                                                       op0=ALU.mult, op1=ALU.add),
                      reads=[("ss2", i) for i in tiles], writes=[("rstd2", i) for i in tiles])
                P.add("act", lambda e: e.activation(out=rstd2[:, lo:hi], in_=rstd2[:, lo:hi], func=AF.Ln),
                      reads=[("rstd2", i) for i in tiles], writes=[("rstd2", i) for i in tiles])
                P.add("act", lambda e: e.activation(out=rstd2[:, lo:hi], in_=rstd2[:, lo:hi], func=AF.Exp, scale=-0.5),
                      reads=[("rstd2", i) for i in tiles], writes=[("rstd2", i) for i in tiles])
                for i in tiles:
                    b = i % 2
                    P.add("dve", lambda e, i=i, b=b: e.scalar_tensor_tensor(out=ob[b][:], in0=H[:, i, :],
                                                                            scalar=rstd2[:, i:i + 1], in1=gb2[:],
                                                                            op0=ALU.mult, op1=ALU.mult),
                          reads=[("H", i), ("rstd2", i), ("gb2",)], writes=[("obf", b)])
                    dma("sp", y_d[i * 128:(i + 1) * 128, :], ob[b][:], reads=[("obf", b)], writes=[("y", i)])
        XNB[:] = [xn0, salloc(st, f"xn1_{tag}", [128, D], BF16)]
        XTs = [salloc(st, f"XT{i}_{tag}", [128, 8, 1024], BF16) for i in range(2)]
        actT = salloc(st, f"actT_{tag}", [128, 22, 1024], BF16)
        wgu = [salloc(st, f"wgu{i}_{tag}", [128, 8, 2, 512], BF16) for i in range(2)]
        NWD = 4
        wd = [salloc(st, f"wd{i}_{tag}", [128, 2, 512], BF16) for i in range(NWD)]
        sg = [salloc(st, f"sg{i}_{tag}", [128, 512], F32) for i in range(2)]
        load_gb(gidx)
        gcnt = 0
        dcnt = 0
        scnt = 0
        nfill = []
        for hf in range(2):
            XT = XTs[hf]
            if hf == 0:
                norm_tiles(list(range(0, 8)), XT, lambda i: i * 128, xkey=("XT", 0))
                nfill = norm_pieces(list(range(8, 16)), XTs[1], lambda i: (i - 8) * 128, xkey=("XT", 1))
            if fin and hf == 1:
                final_tiles(list(range(0, 8)))
            for gi in range(6):
                ncols = 512 if gi < 5 else 256
                slot = gcnt % 2
                gcnt += 1
                for gu in range(2):
                    c0 = gu * DFF + gi * 512
                    dma("pool", wgu[slot][:, :, gu, 0:ncols],
                        wgu_d[:, c0:c0 + ncols].rearrange("(k p) n -> p k n", p=128),
                        writes=[("wgu", slot, gu)])
                if gcnt > 2:
                    for _ in range(2):
                        if extra:
                            extra.pop(0)()
                for jj in range(ncols // 128):
                    j = gi * 4 + jj
                    for nh in range(2):
                        banks = (P.nextbank(), P.nextbank())
                        for gu in range(2):
                            def mm(e, slot=slot, gu=gu, jj=jj, nh=nh, bank=banks[gu], XT=XT):
                                for k in range(8):
                                    ins = e.matmul(ps[bank][:], lhsT=wgu[slot][:, k, gu, jj * 128:(jj + 1) * 128],
                                                   rhs=XT[:, k, nh * 512:(nh + 1) * 512], start=(k == 0), stop=(k == 7))
                                return ins
                            P.add("pe", mm, reads=[("wgu", slot, gu)] + [("XT", hf, nh * 4 + c) for c in range(4)],
                                  writes=[("ps", banks[gu])])
                        s_ = scnt % 2
                        scnt += 1
                        P.add("act", lambda e, s_=s_, b=banks[0]: e.activation(out=sg[s_][:], in_=ps[b][:], func=AF.Silu),
                              reads=[("ps", banks[0])], writes=[("sg", s_)])
                        P.add("dve", lambda e, s_=s_, b=banks[1], j=j, nh=nh: e.tensor_tensor(
                            out=actT[:, j, nh * 512:(nh + 1) * 512], in0=sg[s_][:], in1=ps[b][:], op=ALU.mult),
                            reads=[("sg", s_), ("ps", banks[1])], writes=[("actT", j, nh)])
                    if hf == 0 and j >= 2 and nfill:
                        nfill.pop(0)()
            while hf == 0 and nfill:
                nfill.pop(0)()
            for dh in range(2):
                for kg in range(11):
                    slot = dcnt % NWD
                    dcnt += 1
                    dma("pool", wd[slot][:],
                        wd_d[kg * 256:(kg + 1) * 256, dh * 512:(dh + 1) * 512].rearrange("(k p) n -> p k n", p=128),
                        writes=[("wd", slot)])
                    if extra:
                        extra.pop(0)()
                    for ts in range(8):
                        def mm(e, slot=slot, kg=kg, ts=ts):
                            for kk in range(2):
                                ins = e.matmul(ps[ts][:], lhsT=actT[:, 2 * kg + kk, ts * 128:(ts + 1) * 128],
                                               rhs=wd[slot][:, kk, :], start=(kg == 0 and kk == 0),
                                               stop=(kg == 10 and kk == 1))
                            return ins
                        P.add("pe", mm, reads=[("wd", slot), ("actT", 2 * kg, ts // 4), ("actT", 2 * kg + 1, ts // 4)],
                              writes=[("ps", ts)])
                for ts in range(8):
                    t = 8 * hf + ts
                    P.add("dve", lambda e, ts=ts, t=t, dh=dh: e.scalar_tensor_tensor(
                        out=H[:, t, dh * 512:(dh + 1) * 512], in0=ps[ts][:], scalar=0.5,
                        in1=H[:, t, dh * 512:(dh + 1) * 512], op0=ALU.mult, op1=ALU.add),
                        reads=[("ps", ts), ("H", t)], writes=[("H", t)])
                    if fin and hf == 1 and dh == 1:
                        fin_stats(t)
                        if ts > 0:
                            fin_out(t - 1)
        if fin:
            fin_out(15)
            P.add("sp", None, reads=[("y", i) for i in range(NT)], writes=[("done",)])
        EM.flush()
        st.close()
        XNB[:] = [xn0]

    if stages >= 1:
        ffn_phase(w1gu_d, w1d_d, 0, "f1", extra=conv_dmas if stages >= 2 else None)


    def mixer_phase():
        EM.flush()
        st = contextlib.ExitStack()
        kT = salloc(st, "kT", [128, 8, S], BF16)
        V = salloc(st, "V", [128, NT, D], BF16)
        XTs = [salloc(st, f"XTm{i}", [128, 8, 512], BF16) for i in range(2)]
        qT = salloc(st, "qTm", [128, 8, 512], BF16)
        mT = salloc(st, "mTm", [128, 8, 512], BF16)
        ysT = qT
        ycT = salloc(st, "ycT", [128, 8, 512], BF16)
        wb = [salloc(st, f"wbm{i}", [128, 8, 512], BF16) for i in range(2)]
        ebuf = [salloc(st, f"e{i}", [128, 512], F32) for i in range(2)]
        ebuf += [mT[:, 2 * j:2 * j + 2, :].rearrange("p a b -> p (a b)").bitcast(F32) for j in range(2)]
        ekeys = {0: [("e", 0)], 1: [("e", 1)], 2: [("mT", 0), ("mT", 1)], 3: [("mT", 2), ("mT", 3)]}
        spb = [salloc(st, f"sp{i}", [128, 512], BF16) for i in range(2)]
        E1 = [salloc(st, f"E1{i}", [128, 512], F32) for i in range(2)]
        Ab = [salloc(st, f"A{i}", [128, 512], BF16) for i in range(2)]
        pbuf = salloc(st, "pbuf", [128, 514], F32)
        phalo = salloc(st, "phalo", [128, 8, 2], F32)
        tmpa = salloc(st, "tmpa", [128, 512], F32)
        load_gb(1)
        P.add("dve", lambda e: e.memset(phalo[:].rearrange("p a b -> p (a b)"), 0.0), writes=[("phalo", c) for c in range(8)])
        wcnt = [0]
        SCALE = float(128 ** -0.5)

        def wload(key):
            b, parts = BLK[key]
            ncols = max(c0 + n for (_, c0, n) in parts)
            slot = wcnt[0] % 2
            wcnt[0] += 1
            dma("sp", wb[slot][:, :, 0:ncols], wsc[b][:, :, 0:ncols],
                reads=[("wsc", b, j) for j in range(ncols // 128)],
                writes=[("wb", slot, j) for j in range(ncols // 128)])
            return slot

        def proj_fm(slot, cb, dst_fn, rkeys, XTsrc, xk):
            bank = P.nextbank()

            def mm(e):
                for k in range(8):
                    ins = e.matmul(ps[bank][:], lhsT=wb[slot][:, k, cb * 128:(cb + 1) * 128], rhs=XTsrc[:, k, :],
                                   start=(k == 0), stop=(k == 7))
                return ins
            P.add("pe", mm, reads=[("wb", slot, cb)] + rkeys, writes=[("ps", bank)])
            return bank

        def xkeys_of(T):
            return [("XTm", T % 2, c) for c in range(4)]

        def proj_fm_g(slot, cb, rkeys, XTsrc):
            bank = P.nextbank()
            P.reserved.add(bank)
            for kk in range(4):
                def mm(e, kk=kk):
                    for k in (2 * kk, 2 * kk + 1):
                        ins = e.matmul(ps[bank][:], lhsT=wb[slot][:, k, cb * 128:(cb + 1) * 128], rhs=XTsrc[:, k, :],
                                       start=(k == 0), stop=(k == 7))
                    return ins
                P.add("pe", mm, reads=[("wb", slot, cb)] + rkeys, writes=[("ps", bank)])
                if kk < 3:
                    yield "pe"
            return bank

        def norm_gen(T):
            XT = XTs[T % 2]
            tiles = list(range(4 * T, 4 * T + 4))
            lo, hi = tiles[0], tiles[-1] + 1
            for i in tiles:
                P.add("act", lambda e, i=i: e.activation(out=xn0[:], in_=H[:, i, :], func=AF.Square, accum_out=ss[:, i:i + 1]),
                      reads=[("H", i)], writes=[("ss", i), ("xn", 0)])
            P.add("dve", lambda e: e.tensor_scalar(out=rstd[:, lo:hi], in0=ss[:, lo:hi], scalar1=1.0 / D, scalar2=EPS,
                                                   op0=ALU.mult, op1=ALU.add),
                  reads=[("ss", i) for i in tiles], writes=[("rstd",)])
            P.add("act", lambda e: e.activation(out=rstd[:, lo:hi], in_=rstd[:, lo:hi], func=AF.Ln),
                  reads=[("rstd",)], writes=[("rstd",)])
            P.add("act", lambda e: e.activation(out=rstd[:, lo:hi], in_=rstd[:, lo:hi], func=AF.Exp, scale=-0.5),
                  reads=[("rstd",)], writes=[("rstd",)])
            yield "dve"
            for i in tiles:
                P.add("dve", lambda e, i=i: e.scalar_tensor_tensor(out=xn0[:], in0=H[:, i, :], scalar=rstd[:, i:i + 1], in1=gb[:],
                                                                   op0=ALU.mult, op1=ALU.mult),
                      reads=[("H", i), ("rstd",), ("gb",)], writes=[("xn", 0)])
                bank = P.nextbank()
                P.reserved.add(bank)
                pview = ps[bank][:].bitcast(BF16)
                for kk in range(4):
                    def tr(e, kk=kk, pview=pview):
                        for k in (2 * kk, 2 * kk + 1):
                            ins = e.transpose(pview[:, k * 128:(k + 1) * 128], xn0[:, k * 128:(k + 1) * 128], ident[:])
                        return ins
                    P.add("pe", tr, reads=[("xn", 0), ("ident",)], writes=[("ps", bank)])
                    if kk < 3:
                        yield "pe"
                yield "dve"
                c0 = (i - 4 * T) * 128
                P.add("dve", lambda e, c0=c0, pview=pview: e.tensor_copy(out=XT[:, :, c0:c0 + 128],
                                                                        in_=pview.rearrange("p (k c) -> p k c", k=8)),
                      reads=[("ps", bank)], writes=[("XTm", T % 2, c0 // 128)])
                P.reserved.discard(bank)
                yield "dve"

        def k_gen(T):
            XT, xkeys = XTs[T % 2], xkeys_of(T)
            for blk in range(2):
                slot = wload(("k", blk))
                for cb in range(4):
                    fc = blk * 4 + cb
                    bank = yield from proj_fm_g(slot, cb, xkeys, XT)
                    yield "dve"
                    P.add("dve", lambda e, fc=fc, bank=bank: e.tensor_copy(out=kT[:, fc, T * 512:(T + 1) * 512], in_=ps[bank][:]),
                          reads=[("ps", bank)], writes=[("kT", fc, T)])
                    P.reserved.discard(bank)
                    yield "pe"

        def v_gen(T):
            XT = XTs[T % 2]
            for blk in range(2):
                slot = wload(("v", blk))
                for ts in range(4):
                    bank = P.nextbank()
                    P.reserved.add(bank)
                    for kk in range(4):
                        def mm(e, kk=kk, slot=slot, ts=ts, bank=bank):
                            for k in (2 * kk, 2 * kk + 1):
                                ins = e.matmul(ps[bank][:], lhsT=XT[:, k, ts * 128:(ts + 1) * 128], rhs=wb[slot][:, k, :],
                                               start=(k == 0), stop=(k == 7))
                            return ins
                        P.add("pe", mm, reads=[("wb", slot, j) for j in range(4)] + [("XTm", T % 2, ts)], writes=[("ps", bank)])
                        if kk < 3:
                            yield "pe"
                    yield "dve"
                    P.add("dve", lambda e, bank=bank, ts=ts, blk=blk: e.tensor_copy(
                        out=V[:, 4 * T + ts, blk * 512:(blk + 1) * 512], in_=ps[bank][:]),
                        reads=[("ps", bank)], writes=[("V", 4 * T + ts, blk)])
                    P.reserved.discard(bank)
                    yield "pe"

        def drain(gens):
            for g in list(gens):
                for _ in g:
                    pass
            del gens[:]

        drain([norm_gen(0), k_gen(0), v_gen(0)])
        for T in range(4):
            XT, xkeys = XTs[T % 2], xkeys_of(T)
            for blk in range(2):
                slot = wload(("q", blk))
                for cb in range(4):
                    fc = blk * 4 + cb
                    bank = proj_fm(slot, cb, None, xkeys, XT, "XTm")
                    P.add("act", lambda e, fc=fc, bank=bank: e.activation(out=qT[:, fc, :], in_=ps[bank][:], func=AF.Copy,
                                                                         scale=SCALE),
                          reads=[("ps", bank)], writes=[("qT", fc)])
            def conv_gen(c, XT=XT, xkeys=xkeys):
                slot = wload(("conv", c))
                b_cc = yield from proj_fm_g(slot, 1, xkeys, XT)
                yield "dve"
                P.add("dve", lambda e: e.tensor_copy(out=tmpa[:], in_=ps[b_cc][:]), reads=[("ps", b_cc)], writes=[("tmpa",)])
                P.reserved.discard(b_cc)
                yield "pe"
                b_cx = yield from proj_fm_g(slot, 2, xkeys, XT)
                yield "dve"
                P.add("dve", lambda e: e.tensor_copy(out=pbuf[:, 0:2], in_=phalo[:, c, :]),
                      reads=[("phalo", c)], writes=[("pbuf",)])
                P.add("dve", lambda e: e.tensor_tensor(out=pbuf[:, 2:514], in0=tmpa[:], in1=ps[b_cx][:], op=ALU.mult),
                      reads=[("tmpa",), ("ps", b_cx), ("pbuf",)], writes=[("pbuf",)])
                P.reserved.discard(b_cx)
                P.add("dve", lambda e: e.tensor_copy(out=phalo[:, c, :], in_=pbuf[:, 512:514]),
                      reads=[("pbuf",)], writes=[("phalo", c)])
                yield "dve"
                P.add("dve", lambda e: e.tensor_scalar(out=tmpa[:], in0=pbuf[:, 2:514], scalar1=convw[:, c, 2:3], scalar2=None,
                                                       op0=ALU.mult),
                      reads=[("pbuf",), ("convw",)], writes=[("tmpa",)])
                P.add("dve", lambda e: e.scalar_tensor_tensor(out=tmpa[:], in0=pbuf[:, 1:513], scalar=convw[:, c, 1:2],
                                                              in1=tmpa[:], op0=ALU.mult, op1=ALU.add),
                      reads=[("pbuf",), ("tmpa",), ("convw",)], writes=[("tmpa",)])
                yield "dve"
                P.add("dve", lambda e: e.scalar_tensor_tensor(out=tmpa[:], in0=pbuf[:, 0:512], scalar=convw[:, c, 0:1],
                                                              in1=tmpa[:], op0=ALU.mult, op1=ALU.add),
                      reads=[("pbuf",), ("tmpa",), ("convw",)], writes=[("tmpa",)])
                yield "pe"
                b_cb = yield from proj_fm_g(slot, 0, xkeys, XT)
                yield "dve"
                P.add("dve", lambda e: e.tensor_tensor(out=ycT[:, c, :], in0=tmpa[:], in1=ps[b_cb][:], op=ALU.mult),
                      reads=[("tmpa",), ("ps", b_cb)], writes=[("ycT", c)])
                P.reserved.discard(b_cb)

            gens = [[conv_gen(c), "pe"] for c in range(8)]
            nmicro = 8 * 17
            if T < 3:
                gens += [[norm_gen(T + 1), "dve"], [k_gen(T + 1), "pe"], [v_gen(T + 1), "pe"]]
                nmicro += 21 + 40 + 40
            nch = 4 * T + 4
            npoints = 2 * 4 * nch
            fstate = [0, 0]

            def pump_n(n, allow_dve):
                while n > 0 and gens:
                    g, tag = gens[0]
                    if tag == "dve" and not allow_dve:
                        break
                    try:
                        gens[0][1] = next(g)
                        n -= 1
                        fstate[1] += 1
                    except StopIteration:
                        gens.pop(0)

            def step_done(allow_dve=False):
                fstate[0] += 1
                target = (nmicro * fstate[0] + npoints - 1) // npoints
                pump_n(target - fstate[1], allow_dve)

            tcnt = [0]
            for hp in range(4):
                heads = (2 * hp, 2 * hp + 1)
                Rb = [P.nextbank(), P.nextbank()]
                P.reserved.update(Rb)
                Ob = [P.nextbank(), P.nextbank()]
                P.reserved.update(Ob)
                zb = {}

                def emit_z(h, c):
                    bank = P.nextbank()
                    P.reserved.add(bank)
                    dd = c - 4 * T
                    c0 = max(dd, 0) * 128

                    def mm(e, h=h, c=c, bank=bank, dd=dd, c0=c0):
                        ins = e.matmul(ps[bank][:, c0:512], lhsT=kT[:, h, c * 128:(c + 1) * 128], rhs=qT[:, h, c0:512],
                                       start=True, stop=(dd < 0))
                        if dd >= 0:
                            ins = e.matmul(ps[bank][:, c0:c0 + 128], lhsT=ident[:], rhs=maskb[:], start=False, stop=True)
                        return ins
                    P.add("pe", mm, reads=[("kT", h, c // 4), ("qT", h), ("ident",), ("maskb",)], writes=[("ps", bank)])
                    zb[(h, c)] = bank

                def split_mm(e, out_bank, lhsT, rhs_buf, dd, c0, last):
                    if dd >= 0:
                        ins = e.matmul(ps[out_bank][:, c0:c0 + 128], lhsT=lhsT, rhs=rhs_buf[:, c0:c0 + 128], start=(dd == 3),
                                       stop=last, skip_group_check=True)
                        if dd < 3:
                            ins = e.matmul(ps[out_bank][:, c0 + 128:512], lhsT=lhsT, rhs=rhs_buf[:, c0 + 128:512],
                                           start=False, stop=last, skip_group_check=True)
                    else:
                        ins = e.matmul(ps[out_bank][:], lhsT=lhsT, rhs=rhs_buf[:], start=False, stop=last, skip_group_check=True)
                    return ins

                for h in heads:
                    emit_z(h, nch - 1)
                for c in range(nch - 1, -1, -1):
                    dd = c - 4 * T
                    c0 = max(dd, 0) * 128
                    slots = {}
                    eslot = {}
                    for i, h in enumerate(heads):
                        s_ = i
                        se = 2 * (tcnt[0] % 2) + i
                        slots[h] = s_
                        eslot[h] = se
                        bank = zb[(h, c)]
                        P.add("act", lambda e, se=se, bank=bank, c0=c0: e.activation(out=ebuf[se][:, c0:512], in_=ps[bank][:, c0:512],
                                                                                     func=AF.Exp),
                              reads=[("ps", bank)], writes=ekeys[se])
                        P.reserved.discard(bank)
                    tcnt[0] += 1
                    for h in heads:
                        s_ = slots[h]
                        se = eslot[h]
                        P.add("act", lambda e, s_=s_, se=se, c0=c0: e.activation(out=spb[s_][:, c0:512], in_=ebuf[se][:, c0:512],
                                                                                 func=AF.Ln, bias=1.0),
                              reads=ekeys[se], writes=[("sp", s_)])
                    for i, h in enumerate(heads):
                        s_ = slots[h]
                        P.add("pe", lambda e, s_=s_, rb=Rb[i], dd=dd, c0=c0: split_mm(e, rb, triA[:], spb[s_], dd, c0, True),
                              reads=[("triA",), ("sp", s_)], writes=[("ps", Rb[i])])
                    step_done(True)
                    if c > 0:
                        for h in heads:
                            emit_z(h, c - 1)
                    for i, h in enumerate(heads):
                        s_ = slots[h]
                        P.add("act", lambda e, s_=s_, rb=Rb[i], c0=c0: e.activation(out=E1[s_][:, c0:512], in_=ps[rb][:, c0:512],
                                                                                    func=AF.Exp),
                              reads=[("ps", Rb[i])], writes=[("E1", s_)])
                        if c > 0:
                            P.add("pe", lambda e, s_=s_, rb=Rb[i], c0=c0: e.matmul(ps[rb][:, c0:512], lhsT=triB[:], rhs=spb[s_][:, c0:512],
                                                                                start=False, stop=True, skip_group_check=True),
                                  reads=[("triB",), ("sp", s_)], writes=[("ps", Rb[i])])
                        se = eslot[h]
                        P.add("dve", lambda e, s_=s_, se=se, c0=c0: e.tensor_tensor(out=Ab[s_][:, c0:512], in0=ebuf[se][:, c0:512],
                                                                                    in1=E1[s_][:, c0:512], op=ALU.mult),
                              reads=ekeys[se] + [("E1", s_)], writes=[("A", s_)])
                        P.add("pe", lambda e, s_=s_, ob=Ob[i], h=h, c=c, dd=dd, c0=c0: split_mm(
                            e, ob, V[:, c, h * 128:(h + 1) * 128], Ab[s_], dd, c0, (c == 0)),
                            reads=[("V", c, h // 4), ("A", s_)], writes=[("ps", Ob[i])])
                    step_done(True)
                for i, h in enumerate(heads):
                    P.add("dve", lambda e, h=h, ob=Ob[i]: e.tensor_copy(out=ysT[:, h, :], in_=ps[ob][:]),
                          reads=[("ps", Ob[i])], writes=[("qT", h)])
                P.reserved.difference_update(Rb)
                P.reserved.difference_update(Ob)
            drain([g for g, _ in gens])
            del gens[:]
            for fc in range(8):
                slot = wload(("op", fc))
                bA = proj_fm(slot, 0, None, [("ycT", k) for k in range(8)], ycT, "ycT")
                bB = proj_fm(slot, 1, None, [("qT", k) for k in range(8)], ysT, "ysT")
                bGc = proj_fm(slot, 2, None, xkeys, XT, "XTm")
                bGs = proj_fm(slot, 3, None, xkeys, XT, "XTm")
                P.add("act", lambda e, b=bGc, fc=fc: e.activation(out=tmpa[:], in_=ps[b][:], func=AF.Sigmoid,
                                                                 bias=bgate[:, fc:fc + 1]),
                      reads=[("ps", bGc), ("bgate",)], writes=[("tmpa",)])
                P.add("dve", lambda e, b=bA: e.tensor_tensor(out=tmpa[:], in0=tmpa[:], in1=ps[b][:], op=ALU.mult),
                      reads=[("tmpa",), ("ps", bA)], writes=[("tmpa",)])
                P.add("act", lambda e, b=bGs, fc=fc: e.activation(out=pbuf[:, 0:512], in_=ps[b][:], func=AF.Sigmoid,
                                                                 bias=bgate[:, 8 + fc:9 + fc]),
                      reads=[("ps", bGs), ("bgate",)], writes=[("pbuf",)])
                P.add("dve", lambda e, b=bB: e.tensor_tensor(out=pbuf[:, 0:512], in0=pbuf[:, 0:512], in1=ps[b][:], op=ALU.mult),
                      reads=[("pbuf",), ("ps", bB)], writes=[("pbuf",)])
                P.add("dve", lambda e, fc=fc: e.tensor_tensor(out=mT[:, fc, :], in0=tmpa[:], in1=pbuf[:, 0:512], op=ALU.add),
                      reads=[("tmpa",), ("pbuf",)], writes=[("mT", fc)])
            for dh in range(2):
                slot = wload(("wo", dh))
                for ts in range(4):
                    bank = P.nextbank()

                    def mm(e, slot=slot, ts=ts, bank=bank):
                        for k in range(8):
                            ins = e.matmul(ps[bank][:], lhsT=mT[:, k, ts * 128:(ts + 1) * 128], rhs=wb[slot][:, k, :],
                                           start=(k == 0), stop=(k == 7))
                        return ins
                    P.add("pe", mm, reads=[("wb", slot, j) for j in range(4)] + [("mT", k) for k in range(8)],
                          writes=[("ps", bank)])
                    t = 4 * T + ts
                    P.add("dve", lambda e, bank=bank, t=t, dh=dh: e.tensor_tensor(
                        out=H[:, t, dh * 512:(dh + 1) * 512], in0=ps[bank][:], in1=H[:, t, dh * 512:(dh + 1) * 512], op=ALU.add),
                        reads=[("ps", bank), ("H", t)], writes=[("H", t)])
        EM.flush()
        st.close()
        XNB[:] = [xn0]

    def cross_phase():
        EM.flush()
        st = contextlib.ExitStack()
        XNB[:] = [xn0, salloc(st, "xn1_c", [128, D], BF16)]
        memt = salloc(st, "memt", [128, 2, D], F32)
        mnT = salloc(st, "mnT", [128, 8, 256], BF16)
        kcT = salloc(st, "kcT", [128, 8, 256], BF16)
        Vc = salloc(st, "Vc", [128, 2, D], BF16)
        XT = [salloc(st, f"XTc{i}", [128, 8, 512], BF16) for i in range(2)]
        qcT = [salloc(st, f"qcT{i}", [128, 8, 512], BF16) for i in range(2)]
        oT = [salloc(st, f"oT{i}", [128, 8, 512], BF16) for i in range(2)]
        wq = [salloc(st, f"wq{i}", [128, 8, 512], BF16) for i in range(2)]
        wo = [salloc(st, f"wo{i}", [128, 8, 512], BF16) for i in range(2)]
        wkv = [salloc(st, f"wkv{i}", [128, 8, 512], BF16) for i in range(2)]
        Pf = [salloc(st, f"Pf{i}", [128, 4, 256], F32) for i in range(2)]
        Pn = [salloc(st, f"Pn{i}", [128, 4, 256], BF16) for i in range(2)]
        PnT = [salloc(st, f"PnT{i}", [128, 8, 128], BF16) for i in range(2)]
        mx = [salloc(st, f"mx{i}", [128, 4], F32) for i in range(2)]
        sm = [salloc(st, f"sm{i}", [128, 4], F32) for i in range(2)]
        rs = [salloc(st, f"rs{i}", [128, 4], F32) for i in range(2)]

        dma("sp", memt[:], mem_d.rearrange("(i p) d -> p i d", p=128), writes=[("memt", 0), ("memt", 1)])
        load_gb(3)
        for blk in range(2):
            dma("pool", wkv[blk][:], wckv_d[:, blk * 512:(blk + 1) * 512].rearrange("(k p) n -> p k n", p=128),
                writes=[("wkv", blk)])
        norm_tiles([0, 1], mnT, lambda i: i * 128, SRC=memt, skey="memt", xkey="mnT")
        load_gb(2)
        for blk in range(2):
            for cb in range(4):
                fc = blk * 4 + cb
                bank = P.nextbank()

                def mm(e, blk=blk, cb=cb, bank=bank):
                    for k in range(8):
                        ins = e.matmul(ps[bank][:, 0:256], lhsT=wkv[blk][:, k, cb * 128:(cb + 1) * 128], rhs=mnT[:, k, :],
                                       start=(k == 0), stop=(k == 7))
                    return ins
                P.add("pe", mm, reads=[("wkv", blk), ("mnT", 0), ("mnT", 1)], writes=[("ps", bank)])
                P.add("act", lambda e, fc=fc, bank=bank: e.copy(out=kcT[:, fc, :], in_=ps[bank][:, 0:256]),
                      reads=[("ps", bank)], writes=[("kcT", fc)])
        for blk in range(2):
            dma("pool", wkv[blk][:], wckv_d[:, 1024 + blk * 512:1024 + (blk + 1) * 512].rearrange("(k p) n -> p k n", p=128),
                writes=[("wkv", blk)])
        for blk in range(2):
            dma("pool", wq[blk][:], wcq_d[:, blk * 512:(blk + 1) * 512].rearrange("(k p) n -> p k n", p=128),
                writes=[("wq", blk)])
        for blk in range(2):
            dma("pool", wo[blk][:], wcout_d[:, blk * 512:(blk + 1) * 512].rearrange("(k p) n -> p k n", p=128),
                writes=[("wo", blk)])
        for blk in range(2):
            for mc in range(2):
                bank = P.nextbank()

                def mm(e, blk=blk, mc=mc, bank=bank):
                    for k in range(8):
                        ins = e.matmul(ps[bank][:], lhsT=mnT[:, k, mc * 128:(mc + 1) * 128], rhs=wkv[blk][:, k, :],
                                       start=(k == 0), stop=(k == 7))
                    return ins
                P.add("pe", mm, reads=[("wkv", blk), ("mnT", mc)], writes=[("ps", bank)])
                P.add("dve", lambda e, mc=mc, blk=blk, bank=bank: e.tensor_copy(out=Vc[:, mc, blk * 512:(blk + 1) * 512],
                                                                                in_=ps[bank][:]),
                      reads=[("ps", bank)], writes=[("Vc", mc, blk)])

        def qproj(T, fc):
            par = T % 2
            blk, cb = fc // 4, fc % 4
            bank = P.nextbank()

            def mm(e):
                for k in range(8):
                    ins = e.matmul(ps[bank][:], lhsT=wq[blk][:, k, cb * 128:(cb + 1) * 128], rhs=XT[par][:, k, :],
                                   start=(k == 0), stop=(k == 7))
                return ins
            P.add("pe", mm, reads=[("wq", blk)] + [("XTc", par, c) for c in range(4)], writes=[("ps", bank)])
            P.add("act", lambda e: e.activation(out=qcT[par][:, fc, :], in_=ps[bank][:], func=AF.Copy, scale=1.0 / 16.0),
                  reads=[("ps", bank)], writes=[("qcT", par, fc)])

        def outproj(T, ts):
            par = T % 2
            for dh in range(2):
                bank = P.nextbank()

                def mm(e, dh=dh, bank=bank):
                    for k in range(8):
                        ins = e.matmul(ps[bank][:], lhsT=oT[par][:, k, ts * 128:(ts + 1) * 128], rhs=wo[dh][:, k, :],
                                       start=(k == 0), stop=(k == 7))
                    return ins
                P.add("pe", mm, reads=[("wo", dh), ("oT", par, ts, 0), ("oT", par, ts, 1)], writes=[("ps", bank)])
                t = 4 * T + ts
                P.add("dve", lambda e, bank=bank, t=t, dh=dh: e.tensor_tensor(
                    out=H[:, t, dh * 512:(dh + 1) * 512], in0=ps[bank][:], in1=H[:, t, dh * 512:(dh + 1) * 512], op=ALU.add),
                    reads=[("ps", bank), ("H", t)], writes=[("H", t)])

        def cnorm(T):
            par = T % 2
            norm_tiles(list(range(4 * T, 4 * T + 4)), XT[par], lambda i: (i - 4 * T) * 128, xkey=("XTc", par), evac_eng="dve")

        def A(n):
            T, ts = divmod(n, 4)
            par = T % 2
            sbk = [P.nextbank(), P.nextbank()]
            P.reserved.update(sbk)

            def mm(e):
                for h in range(4):
                    for c in range(2):
                        ins = e.matmul(ps[sbk[h // 2]][:, (h % 2) * 256:(h % 2 + 1) * 256],
                                       lhsT=qcT[par][:, 2 * h + c, ts * 128:(ts + 1) * 128],
                                       rhs=kcT[:, 2 * h + c, :], start=(c == 0), stop=(c == 1))
                return ins
            P.add("pe", mm, reads=[("qcT", par, k) for k in range(8)] + [("kcT", k) for k in range(8)],
                  writes=[("ps", sbk[0]), ("ps", sbk[1])])
            return sbk

        def B(n, sbk):
            b_ = n % 2
            for i in range(2):
                pv = ps[sbk[i]][:].rearrange("p (h m) -> p h m", h=2)
                P.add("dve", lambda e, pv=pv, i=i: e.tensor_reduce(out=mx[b_][:, 2 * i:2 * i + 2], in_=pv, axis=AX.X, op=ALU.max),
                      reads=[("ps", sbk[i])], writes=[("mx", b_, i)])
            P.add("dve", lambda e: e.tensor_scalar(out=mx[b_][:], in0=mx[b_][:], scalar1=-1.0, scalar2=None, op0=ALU.mult),
                  reads=[("mx", b_, 0), ("mx", b_, 1)], writes=[("mx", b_, 0), ("mx", b_, 1)])
            for h in range(4):
                P.add("act", lambda e, h=h: e.activation(
                    out=Pf[b_][:, h, :], in_=ps[sbk[h // 2]][:, (h % 2) * 256:(h % 2 + 1) * 256],
                    func=AF.Exp, bias=mx[b_][:, h:h + 1], accum_out=sm[b_][:, h:h + 1]),
                    reads=[("ps", sbk[h // 2]), ("mx", b_, h // 2)], writes=[("Pf", b_, h), ("sm", b_, h)])
            P.reserved.difference_update(sbk)
            P.add("dve", lambda e: e.reciprocal(out=rs[b_][:], in_=sm[b_][:]),
                  reads=[("sm", b_, h) for h in range(4)], writes=[("rs", b_)])
            P.add("dve", lambda e: e.tensor_tensor(out=Pn[b_][:], in0=Pf[b_][:],
                                                   in1=rs[b_][:].unsqueeze(2).to_broadcast([128, 4, 256]), op=ALU.mult),
                  reads=[("Pf", b_, h) for h in range(4)] + [("rs", b_)], writes=[("Pn", b_)])

        def C(n):
            b_ = n % 2
            bank2 = P.nextbank()
            pview = ps[bank2][:].bitcast(BF16)

            def tr(e):
                for h in range(4):
                    for mc in range(2):
                        j = h * 2 + mc
                        ins = e.transpose(pview[:, j * 128:(j + 1) * 128], Pn[b_][:, h, mc * 128:(mc + 1) * 128], ident[:])
                return ins
            P.add("pe", tr, reads=[("Pn", b_), ("ident",)], writes=[("ps", bank2)])
            P.add("act", lambda e: e.copy(out=PnT[b_][:], in_=pview.rearrange("p (j c) -> p j c", j=8)),
                  reads=[("ps", bank2)], writes=[("PnT", b_)])

        def Dd(n):
            T, ts = divmod(n, 4)
            par = T % 2
            b_ = n % 2
            obk = [P.nextbank(), P.nextbank()]

            def mm2(e):
                for h in range(4):
                    for c in range(2):
                        f = 2 * h + c
                        for mc in range(2):
                            ins = e.matmul(ps[obk[f // 4]][:, (f % 4) * 128:(f % 4 + 1) * 128],
                                           lhsT=Vc[:, mc, f * 128:(f + 1) * 128],
                                           rhs=PnT[b_][:, h * 2 + mc, :], start=(mc == 0), stop=(mc == 1))
                return ins
            P.add("pe", mm2, reads=[("PnT", b_)] + [("Vc", mc, b) for mc in range(2) for b in range(2)],
                  writes=[("ps", obk[0]), ("ps", obk[1])])
            for i in range(2):
                P.add("dve", lambda e, i=i: e.tensor_copy(
                    out=oT[par][:, 4 * i:4 * i + 4, ts * 128:(ts + 1) * 128],
                    in_=ps[obk[i]][:].rearrange("p (f t) -> p f t", f=4)),
                    reads=[("ps", obk[i])], writes=[("oT", par, ts, i)])

        cnorm(0)
        for fc in range(8):
            qproj(0, fc)
        cur = A(0)
        B(0, cur)
        qsched = {0: [0, 1, 2], 1: [3, 4, 5], 2: [6, 7], 3: []}
        for n in range(16):
            T, ts = divmod(n, 4)
            if ts == 0 and T < 3:
                cnorm(T + 1)
            nxt = A(n + 1) if n + 1 < 16 and (n + 1) % 4 != 0 else None
            C(n)
            if nxt is not None:
                B(n + 1, nxt)
            if T < 3:
                for fc in qsched[ts]:
                    qproj(T + 1, fc)
            if n + 1 < 16 and (n + 1) % 4 == 0:
                nxt = A(n + 1)
                B(n + 1, nxt)
            if n >= 1:
                outproj(*divmod(n - 1, 4))
            Dd(n)
        prev = (3, 3)
        outproj(*prev)
        EM.flush()
        st.close()
        XNB[:] = [xn0]

    if stages >= 2:
        mixer_phase()
    if stages >= 3:
        cross_phase()
    if stages >= 4:
        ffn_phase(w2gu_d, w2d_d, 4, "f2", fin=final_norm)
    if stages >= 4 and final_norm:
        EM.flush()
        gstack.close()
        return nc

    EM.flush()
    st = contextlib.ExitStack()
    ob = [salloc(st, f"ob{i}", [128, D], F32) for i in range(2)]
    if final_norm:
        load_gb(5)
        for i in range(NT):
            P.add("act", lambda e, i=i: e.activation(out=XNB[0][:], in_=H[:, i, :], func=AF.Square,
                                                     accum_out=ss[:, i:i + 1]),
                  reads=[("H", i)], writes=[("ss", i), ("xn", 0)])
        P.add("dve", lambda e: e.tensor_scalar(out=rstd[:], in0=ss[:], scalar1=1.0 / D, scalar2=EPS,
                                               op0=ALU.mult, op1=ALU.add),
              reads=[("ss", i) for i in range(NT)], writes=[("rstd",)])
        P.add("act", lambda e: e.activation(out=rstd[:], in_=rstd[:], func=AF.Ln), reads=[("rstd",)], writes=[("rstd",)])
        P.add("act", lambda e: e.activation(out=rstd[:], in_=rstd[:], func=AF.Exp, scale=-0.5),
              reads=[("rstd",)], writes=[("rstd",)])
    outs = []
    for i in range(NT):
        b = i % 2
        if final_norm:
            P.add("dve", lambda e, i=i, b=b: e.scalar_tensor_tensor(out=ob[b][:], in0=H[:, i, :], scalar=rstd[:, i:i + 1],
                                                                    in1=gb[:], op0=ALU.mult, op1=ALU.mult),
                  reads=[("H", i), ("rstd",), ("gb",)], writes=[("ob", b)])
        else:
            P.add("dve", lambda e, i=i, b=b: e.tensor_copy(out=ob[b][:], in_=H[:, i, :]),
                  reads=[("H", i)], writes=[("ob", b)])
        outs.append(dma("sp", y_d[i * 128:(i + 1) * 128, :], ob[b][:], reads=[("ob", b)], writes=[("y", i)]))
    P.add("sp", None, reads=[("y", i) for i in range(NT)], writes=[("done",)])

    EM.flush()
    st.close()
    gstack.close()
    return nc


class Emitter:
    def __init__(self, nc, P, sems, dsems):
        self.nc, self.P, self.sems, self.dsems = nc, P, sems, dsems
        self.cnt = {e: 0 for e in sems}
        self.dq = {q: 0 for q in dsems}
        self.dslot = {q: [0] * ND for q in dsems}
        self.dlast = {q: [None] * ND for q in dsems}
        self.waited = {}
        self.done = 0

    def flush(self):
        P = self.P
        ops = P.ops[self.done:]
        self.done = len(P.ops)
        for op in P.lastw.values():
            op.needed = True
        for rd in P.readers.values():
            for d in rd[0].values():
                d.needed = True
        for op in ops:
            if op.is_dma:
                q = op.eng
                s_ = self.dq[q] % ND
                self.dq[q] += 1
                self.dslot[q][s_] += 1
                op.sem = self.dsems[q][s_]
                op.val = 16 * self.dslot[q][s_]
                op.prev = self.dlast[q][s_]
                self.dlast[q][s_] = op
            elif op.needed and op.fn is not None:
                self.cnt[op.eng] += 1
                op.sem = self.sems[op.eng]
                op.val = self.cnt[op.eng]
        engmap = {"pe": "tensor", "act": "scalar", "dve": "vector", "pool": "gpsimd", "sp": "sync"}
        byeng = dict((e, []) for e in engmap)
        for op in ops:
            byeng[op.eng].append(op)
        waited = self.waited
        with self.nc.Block() as block:
            for eng, bname in engmap.items():
                eops = byeng[eng]
                if not eops:
                    continue

                def body(e, eops=eops, eng=eng):
                    w = waited.setdefault(eng, {})

                    def wait(sem, val):
                        key = id(sem)
                        if w.get(key, (None, 0))[1] >= val:
                            return
                        w[key] = (sem, val)
                        e.wait_ge(sem, val)

                    for op in eops:
                        if op.prev is not None:
                            wait(op.prev.sem, op.prev.val)
                        for d in op.deps:
                            wait(d.sem, d.val)
                        if op.fn is None:
                            continue
                        ins = op.fn(e)
                        if op.is_dma:
                            ins.then_inc(op.sem, 16)
                        elif op.needed:
                            ins.then_inc(op.sem, 1)

                getattr(block, bname)(body)


def _host_consts():
    bf = ml_dtypes.bfloat16
    ident = np.eye(128, dtype=np.float32).astype(bf)
    j = np.arange(128)[:, None]
    s = np.arange(128)[None, :]
    triA = np.where(j >= s, -1.0, 0.0).astype(np.float32).astype(bf)
    triB = np.where(j < s, -1.0, 0.0).astype(np.float32).astype(bf)
    maskb = np.where(j >= s, NEG, 0.0).astype(np.float32).astype(bf)
    return ident, triA, triB, maskb


_CACHE = {}


def kernel(x, mem, g_ffn1, w_ffn1_gu, w_ffn1_down, g_mix, w_in, b_gate, conv_w, w_conv_out, w_attn_out, w_o,
           g_cross, g_mem, w_cq, w_ckv, w_co, g_ffn2, w_ffn2_gu, w_ffn2_down, g_final, _stages=5, _final_norm=True):
    f = lambda a: np.ascontiguousarray(np.asarray(a, dtype=np.float32))
    x = f(x)
    mem = f(mem)
    n = 8
    key = (_stages, _final_norm)
    if key not in _CACHE:
        _CACHE[key] = build_program(_stages, _final_norm)
    nc = _CACHE[key]
    ident, triA, triB, maskb = _host_consts()
    gv = np.stack([np.broadcast_to(f(g)[None, :], (128, D)) for g in (g_ffn1, g_mix, g_cross, g_mem, g_ffn2, g_final)])
    gv = np.ascontiguousarray(gv)
    bg = np.ascontiguousarray(f(b_gate).reshape(16, 128).T)
    cw = np.ascontiguousarray(f(conv_w).T.reshape(8, 128, 3).transpose(1, 0, 2))
    shared = {
        "gv": gv, "w_ffn1_gu": f(w_ffn1_gu), "w_ffn1_down": f(w_ffn1_down), "w_in": f(w_in),
        "w_conv_out": f(w_conv_out), "w_attn_out": f(w_attn_out), "w_o": f(w_o), "w_cq": f(w_cq),
        "w_ckv": f(w_ckv), "w_co": f(w_co), "w_ffn2_gu": f(w_ffn2_gu), "w_ffn2_down": f(w_ffn2_down),
        "bgate": bg, "convw": cw, "ident": ident, "triA": triA, "triB": triB, "maskb": maskb,
    }
    in_maps = []
    for c in range(n):
        m = dict(shared)
        m["x"] = x[c]
        m["mem"] = mem[c]
        in_maps.append(m)
    res = run_bass_kernel_spmd(nc, in_maps, core_ids=list(range(n)))
    return np.stack([r["y"] for r in res.results], axis=0)
```

```python
import os
import numpy as np
import ml_dtypes
import concourse.bass as bass
import concourse.mybir as mybir
from concourse.bass_utils import run_bass_kernel_spmd

F32, BF16 = mybir.dt.float32, mybir.dt.bfloat16
AF = mybir.ActivationFunctionType
ALU = mybir.AluOpType
AX = mybir.AxisListType

S = 2048
D = 1024
DFF = 2816
MEM = 256
NT = S // 128
EPS = 1e-6
ND = 8
NEG = -30000.0


class Op:
    __slots__ = ("eng", "fn", "deps", "needed", "sem", "val", "is_dma", "phase", "prev")


class Plan:
    def __init__(self):
        self.ops = []
        self.lastw = {}
        self.readers = {}
        self.phase = 0
        self.bank = 0
        self.reserved = set()

    def nextbank(self):
        while self.bank in self.reserved:
            self.bank = (self.bank + 1) % 8
        b = self.bank
        self.bank = (self.bank + 1) % 8
        return b

    def add(self, eng, fn, reads=(), writes=(), dma=False):
        op = Op()
        op.eng, op.fn, op.is_dma, op.needed, op.phase = eng, fn, dma, dma, self.phase
        op.prev = None
        deps, seen = [], set()

        def add_dep(d):
            if d is None or id(d) in seen:
                return
            if d.eng == "pe" and eng == "pe" and not d.is_dma and not dma:
                return
            seen.add(id(d))
            deps.append(d)
            d.needed = True

        for r in reads:
            add_dep(self.lastw.get(r))
        for w in writes:
            add_dep(self.lastw.get(w))
            rd = self.readers.get(w)
            if rd:
                for d in rd[0].values():
                    add_dep(d)
                for d in rd[1]:
                    add_dep(d)
        for w in writes:
            self.lastw[w] = op
            self.readers[w] = ({}, [])
        for r in reads:
            if r in writes:
                continue
            rd = self.readers.setdefault(r, ({}, []))
            if dma:
                rd[1].append(op)
            else:
                rd[0][eng] = op
        op.deps = deps
        self.ops.append(op)
        return op

    def next_phase(self):
        self.phase += 1


def build_program(stages=5, final_norm=True):
    nc = bass.Bass("TRN2", target_bir_lowering=False)
    P = Plan()

    def din(name, shape, dt=F32):
        return nc.dram_tensor(name, shape, dt, kind="ExternalInput").ap()

    x_d = din("x", [S, D])
    mem_d = din("mem", [MEM, D])
    gv_d = din("gv", [6, 128, D])
    w1gu_d = din("w_ffn1_gu", [D, 2 * DFF])
    w1d_d = din("w_ffn1_down", [DFF, D])
    win_d = din("w_in", [D, 8 * D])
    wco_d = din("w_conv_out", [D, D])
    wao_d = din("w_attn_out", [D, D])
    wo_d = din("w_o", [D, D])
    wcq_d = din("w_cq", [D, D])
    wckv_d = din("w_ckv", [D, 2 * D])
    wcout_d = din("w_co", [D, D])
    w2gu_d = din("w_ffn2_gu", [D, 2 * DFF])
    w2d_d = din("w_ffn2_down", [DFF, D])
    bg_d = din("bgate", [128, 16])
    cw_d = din("convw", [128, 8, 3])
    ident_d = din("ident", [128, 128], BF16)
    triA_d = din("triA", [128, 128], BF16)
    triB_d = din("triB", [128, 128], BF16)
    maskb_d = din("maskb", [128, 128], BF16)
    y_d = nc.dram_tensor("y", [S, D], F32, kind="ExternalOutput").ap()

    import contextlib

    gstack = contextlib.ExitStack()

    def salloc(stack, name, shape, dt):
        return stack.enter_context(nc.sbuf_tensor(name, shape, dt))

    H = salloc(gstack, "H", [128, NT, D], F32)
    H_ = H
    ident = salloc(gstack, "ident_s", [128, 128], BF16)
    triA = salloc(gstack, "triA_s", [128, 128], BF16)
    triB = salloc(gstack, "triB_s", [128, 128], BF16)
    maskb = salloc(gstack, "maskb_s", [128, 128], BF16)
    bgate = salloc(gstack, "bgate_s", [128, 16], F32)
    convw = salloc(gstack, "convw_s", [128, 8, 3], F32)
    gb = salloc(gstack, "gb", [128, D], F32)
    ss = salloc(gstack, "ss", [128, NT], F32)
    rstd = salloc(gstack, "rstd", [128, NT], F32)
    xn0 = salloc(gstack, "xn0", [128, D], BF16)
    XNB = [xn0]
    ps = [gstack.enter_context(nc.psum_tensor(f"ps{i}", [128, 512], F32)) for i in range(8)]

    sems = {}
    for e in ("pe", "act", "dve", "pool"):
        sems[e] = gstack.enter_context(nc.semaphore(f"s_{e}"))
    dsems = {}
    for q in ("sp", "pool", "act"):
        dsems[q] = [gstack.enter_context(nc.semaphore(f"d_{q}{i}")) for i in range(ND)]

    EM = Emitter(nc, P, sems, dsems)

    def dma(q, out, in_, reads=(), writes=()):
        return P.add(q, lambda e: e.dma_start(out=out, in_=in_), reads=reads, writes=writes, dma=True)

    def load_gb(idx):
        dma("sp", gb[:], gv_d[idx], writes=[("gb",)])

    BLK = {}

    def defblk(key, parts):
        BLK[key] = (len(BLK), parts)

    for blk in range(2):
        defblk(("k", blk), [(win_d[:, 4096 + blk * 512:4096 + (blk + 1) * 512], 0, 512)])
    for blk in range(2):
        defblk(("q", blk), [(win_d[:, 3072 + blk * 512:3072 + (blk + 1) * 512], 0, 512)])
    for blk in range(2):
        defblk(("v", blk), [(win_d[:, 5120 + blk * 512:5120 + (blk + 1) * 512], 0, 512)])
    for c in range(8):
        defblk(("conv", c), [(win_d[:, j * 1024 + c * 128:j * 1024 + (c + 1) * 128], j * 128, 128) for j in range(3)])
    for fc in range(8):
        defblk(("op", fc), [(wco_d[:, fc * 128:(fc + 1) * 128], 0, 128), (wao_d[:, fc * 128:(fc + 1) * 128], 128, 128),
                            (win_d[:, 6144 + fc * 128:6144 + (fc + 1) * 128], 256, 128),
                            (win_d[:, 7168 + fc * 128:7168 + (fc + 1) * 128], 384, 128)])
    for dh in range(2):
        defblk(("wo", dh), [(wo_d[:, dh * 512:(dh + 1) * 512], 0, 512)])
    wsc = nc.dram_tensor("wscratch", [len(BLK), 128, 8, 512], BF16).ap()
    conv_dmas = []
    for key, (b, parts) in BLK.items():
        for (src, c0, n) in parts:
            conv_dmas.append(lambda b=b, src=src, c0=c0, n=n: dma(
                "pool", wsc[b][:, :, c0:c0 + n], src.rearrange("(k p) n -> p k n", p=128),
                writes=[("wsc", b, c0 // 128 + j) for j in range(n // 128)]))

    xn_ctr = [0]

    def norm_pieces(tiles, XT, col_of_tile, evac_eng="act", SRC=None, skey="H", xkey="XT"):
        H = SRC if SRC is not None else H_
        xk = (lambda c: xkey + (c,)) if isinstance(xkey, tuple) else (lambda c: (xkey, c))
        lo, hi = min(tiles), max(tiles) + 1

        def stats():
            for i in tiles:
                P.add("act", lambda e, i=i: e.activation(out=XNB[0][:], in_=H[:, i, :], func=AF.Square,
                                                         accum_out=ss[:, i:i + 1]),
                      reads=[(skey, i)], writes=[("ss", i), ("xn", 0)])
            P.add("dve", lambda e: e.tensor_scalar(out=rstd[:, lo:hi], in0=ss[:, lo:hi], scalar1=1.0 / D, scalar2=EPS,
                                                   op0=ALU.mult, op1=ALU.add),
                  reads=[("ss", i) for i in tiles], writes=[("rstd",)])
            P.add("act", lambda e: e.activation(out=rstd[:, lo:hi], in_=rstd[:, lo:hi], func=AF.Ln),
                  reads=[("rstd",)], writes=[("rstd",)])
            P.add("act", lambda e: e.activation(out=rstd[:, lo:hi], in_=rstd[:, lo:hi], func=AF.Exp, scale=-0.5),
                  reads=[("rstd",)], writes=[("rstd",)])

        bsel = {}

        def mult(i):
            b = xn_ctr[0] % len(XNB)
            xn_ctr[0] += 1
            bsel[i] = b
            xnb = XNB[b]
            P.add("dve", lambda e: e.scalar_tensor_tensor(out=xnb[:], in0=H[:, i, :], scalar=rstd[:, i:i + 1], in1=gb[:],
                                                          op0=ALU.mult, op1=ALU.mult),
                  reads=[(skey, i), ("rstd",), ("gb",)], writes=[("xn", b)])

        def trn(i):
            b = bsel[i]
            xnb = XNB[b]
            bank = P.nextbank()
            pview = ps[bank][:].bitcast(BF16)

            def tr(e):
                for k in range(8):
                    ins = e.transpose(pview[:, k * 128:(k + 1) * 128], xnb[:, k * 128:(k + 1) * 128], ident[:])
                return ins

            P.add("pe", tr, reads=[("xn", b), ("ident",)], writes=[("ps", bank)])
            c0 = col_of_tile(i)
            src = pview.rearrange("p (k c) -> p k c", k=8)
            dst = XT[:, :, c0:c0 + 128]
            if evac_eng == "act":
                P.add("act", lambda e: e.copy(out=dst, in_=src), reads=[("ps", bank)], writes=[xk(c0 // 128)])
            else:
                P.add("dve", lambda e: e.tensor_copy(out=dst, in_=src), reads=[("ps", bank)], writes=[xk(c0 // 128)])

        if len(XNB) == 1:
            return [stats] + [(lambda i=i: (mult(i), trn(i))) for i in tiles]
        pieces = [stats, (lambda: mult(tiles[0]))]
        for j in range(1, len(tiles)):
            pieces.append(lambda j=j: (mult(tiles[j]), trn(tiles[j - 1])))
        pieces.append(lambda: trn(tiles[-1]))
        return pieces

    def norm_tiles(*a, **kw):
        for p_ in norm_pieces(*a, **kw):
            p_()

    for a in range(4):
        dma("sp", H[:, 4 * a:4 * a + 4, :], x_d[512 * a:512 * (a + 1), :].rearrange("(i p) d -> p i d", p=128),
            writes=[("H", 4 * a + j) for j in range(4)])
    dma("sp", ident[:], ident_d, writes=[("ident",)])
    dma("sp", triA[:], triA_d, writes=[("triA",)])
    dma("sp", triB[:], triB_d, writes=[("triB",)])
    dma("sp", maskb[:], maskb_d, writes=[("maskb",)])
    dma("sp", bgate[:], bg_d, writes=[("bgate",)])
    dma("sp", convw[:], cw_d, writes=[("convw",)])

    def ffn_phase(wgu_d, wd_d, gidx, tag, fin=False, extra=None):
        extra = list(extra or [])
        per_group = (len(extra) + 11) // 12
        EM.flush()
        st = contextlib.ExitStack()
        if fin:
            gb2 = salloc(st, "gb2", [128, D], F32)
            ob = [salloc(st, f"obf{i}", [128, D], F32) for i in range(2)]
            ss2 = salloc(st, "ss2", [128, NT], F32)
            rstd2 = salloc(st, "rstd2", [128, NT], F32)
            dma("sp", gb2[:], gv_d[5], writes=[("gb2",)])
            epsT = salloc(st, "epsT", [128, 1], F32)
            P.add("dve", lambda e: e.memset(epsT[:], EPS), writes=[("epsT",)])

            def fin_stats(t):
                P.add("act", lambda e: e.activation(out=XNB[0][:], in_=H[:, t, :], func=AF.Square, accum_out=ss2[:, t:t + 1]),
                      reads=[("H", t)], writes=[("ss2", t), ("xn", 0)])
                P.add("act", lambda e: e.activation(out=rstd2[:, t:t + 1], in_=ss2[:, t:t + 1], func=AF.Ln, scale=1.0 / D,
                                                    bias=epsT[:, 0:1]),
                      reads=[("ss2", t), ("epsT",)], writes=[("rstd2", t)])
                P.add("act", lambda e: e.activation(out=rstd2[:, t:t + 1], in_=rstd2[:, t:t + 1], func=AF.Exp, scale=-0.5),
                      reads=[("rstd2", t)], writes=[("rstd2", t)])

            def fin_out(t):
                b = t % 2
                P.add("dve", lambda e: e.scalar_tensor_tensor(out=ob[b][:], in0=H[:, t, :], scalar=rstd2[:, t:t + 1], in1=gb2[:],
                                                              op0=ALU.mult, op1=ALU.mult),
                      reads=[("H", t), ("rstd2", t), ("gb2",)], writes=[("obf", b)])
                dma("sp", y_d[t * 128:(t + 1) * 128, :], ob[b][:], reads=[("obf", b)], writes=[("y", t)])

            def final_tiles(tiles):
                lo, hi = min(tiles), max(tiles) + 1
                for i in tiles:
                    P.add("act", lambda e, i=i: e.activation(out=XNB[0][:], in_=H[:, i, :], func=AF.Square,
                                                             accum_out=ss2[:, i:i + 1]),
                          reads=[("H", i)], writes=[("ss2", i), ("xn", 0)])
                P.add("dve", lambda e: e.tensor_scalar(out=rstd2[:, lo:hi], in0=ss2[:, lo:hi], scalar1=1.0 / D, scalar2=EPS,
                                                       op0=ALU.mult, op1=ALU.add),
                      reads=[("ss2", i) for i in tiles], writes=[("rstd2", i) for i in tiles])
                P.add("act", lambda e: e.activation(out=rstd2[:, lo:hi], in_=rstd2[:, lo:hi], func=AF.Ln),
                      reads=[("rstd2", i) for i in tiles], writes=[("rstd2", i) for i in tiles])
                P.add("act", lambda e: e.activation(out=rstd2[:, lo:hi], in_=rstd2[:, lo:hi], func=AF.Exp, scale=-0.5),
                      reads=[("rstd2", i) for i in tiles], writes=[("rstd2", i) for i in tiles])
                for i in tiles:
                    b = i % 2
                    P.add("dve", lambda e, i=i, b=b: e.scalar_tensor_tensor(out=ob[b][:], in0=H[:, i, :],
                                                                            scalar=rstd2[:, i:i + 1], in1=gb2[:],
                                                                            op0=ALU.mult, op1=ALU.mult),
                          reads=[("H", i), ("rstd2", i), ("gb2",)], writes=[("obf", b)])
                    dma("sp", y_d[i * 128:(i + 1) * 128, :], ob[b][:], reads=[("obf", b)], writes=[("y", i)])
        XNB[:] = [xn0, salloc(st, f"xn1_{tag}", [128, D], BF16)]
        XTs = [salloc(st, f"XT{i}_{tag}", [128, 8, 1024], BF16) for i in range(2)]
        actT = salloc(st, f"actT_{tag}", [128, 22, 1024], BF16)
        wgu = [salloc(st, f"wgu{i}_{tag}", [128, 8, 2, 512], BF16) for i in range(2)]
        NWD = 4
        wd = [salloc(st, f"wd{i}_{tag}", [128, 2, 512], BF16) for i in range(NWD)]
        sg = [salloc(st, f"sg{i}_{tag}", [128, 512], F32) for i in range(2)]
        load_gb(gidx)
        gcnt = 0
        dcnt = 0
        scnt = 0
        nfill = []
        for hf in range(2):
            XT = XTs[hf]
            if hf == 0:
                norm_tiles(list(range(0, 8)), XT, lambda i: i * 128, xkey=("XT", 0))
                nfill = norm_pieces(list(range(8, 16)), XTs[1], lambda i: (i - 8) * 128, xkey=("XT", 1))
            if fin and hf == 1:
                final_tiles(list(range(0, 8)))
            for gi in range(6):
                ncols = 512 if gi < 5 else 256
                slot = gcnt % 2
                gcnt += 1
                for gu in range(2):
                    c0 = gu * DFF + gi * 512
                    dma("pool", wgu[slot][:, :, gu, 0:ncols],
                        wgu_d[:, c0:c0 + ncols].rearrange("(k p) n -> p k n", p=128),
                        writes=[("wgu", slot, gu)])
                if gcnt > 2:
                    for _ in range(2):
                        if extra:
                            extra.pop(0)()
                for jj in range(ncols // 128):
                    j = gi * 4 + jj
                    for nh in range(2):
                        banks = (P.nextbank(), P.nextbank())
                        for gu in range(2):
                            def mm(e, slot=slot, gu=gu, jj=jj, nh=nh, bank=banks[gu], XT=XT):
                                for k in range(8):
                                    ins = e.matmul(ps[bank][:], lhsT=wgu[slot][:, k, gu, jj * 128:(jj + 1) * 128],
                                                   rhs=XT[:, k, nh * 512:(nh + 1) * 512], start=(k == 0), stop=(k == 7))
                                return ins
                            P.add("pe", mm, reads=[("wgu", slot, gu)] + [("XT", hf, nh * 4 + c) for c in range(4)],
                                  writes=[("ps", banks[gu])])
                        s_ = scnt % 2
                        scnt += 1
                        P.add("act", lambda e, s_=s_, b=banks[0]: e.activation(out=sg[s_][:], in_=ps[b][:], func=AF.Silu),
                              reads=[("ps", banks[0])], writes=[("sg", s_)])
                        P.add("dve", lambda e, s_=s_, b=banks[1], j=j, nh=nh: e.tensor_tensor(
                            out=actT[:, j, nh * 512:(nh + 1) * 512], in0=sg[s_][:], in1=ps[b][:], op=ALU.mult),
                            reads=[("sg", s_), ("ps", banks[1])], writes=[("actT", j, nh)])
                    if hf == 0 and j >= 2 and nfill:
                        nfill.pop(0)()
            while hf == 0 and nfill:
                nfill.pop(0)()
            for dh in range(2):
                for kg in range(11):
                    slot = dcnt % NWD
                    dcnt += 1
                    dma("pool", wd[slot][:],
                        wd_d[kg * 256:(kg + 1) * 256, dh * 512:(dh + 1) * 512].rearrange("(k p) n -> p k n", p=128),
                        writes=[("wd", slot)])
                    if extra:
                        extra.pop(0)()
                    for ts in range(8):
                        def mm(e, slot=slot, kg=kg, ts=ts):
                            for kk in range(2):
                                ins = e.matmul(ps[ts][:], lhsT=actT[:, 2 * kg + kk, ts * 128:(ts + 1) * 128],
                                               rhs=wd[slot][:, kk, :], start=(kg == 0 and kk == 0),
                                               stop=(kg == 10 and kk == 1))
                            return ins
                        P.add("pe", mm, reads=[("wd", slot), ("actT", 2 * kg, ts // 4), ("actT", 2 * kg + 1, ts // 4)],
                              writes=[("ps", ts)])
                for ts in range(8):
                    t = 8 * hf + ts
                    P.add("dve", lambda e, ts=ts, t=t, dh=dh: e.scalar_tensor_tensor(
                        out=H[:, t, dh * 512:(dh + 1) * 512], in0=ps[ts][:], scalar=0.5,
                        in1=H[:, t, dh * 512:(dh + 1) * 512], op0=ALU.mult, op1=ALU.add),
                        reads=[("ps", ts), ("H", t)], writes=[("H", t)])
                    if fin and hf == 1 and dh == 1:
                        fin_stats(t)
                        if ts > 0:
                            fin_out(t - 1)
        if fin:
            fin_out(15)
            P.add("sp", None, reads=[("y", i) for i in range(NT)], writes=[("done",)])
        EM.flush()
        st.close()
        XNB[:] = [xn0]

    if stages >= 1:
        ffn_phase(w1gu_d, w1d_d, 0, "f1", extra=conv_dmas if stages >= 2 else None)


    def mixer_phase():
        EM.flush()
        st = contextlib.ExitStack()
        kT = salloc(st, "kT", [128, 8, S], BF16)
        V = salloc(st, "V", [128, NT, D], BF16)
        XTs = [salloc(st, f"XTm{i}", [128, 8, 512], BF16) for i in range(2)]
        qT = salloc(st, "qTm", [128, 8, 512], BF16)
        mT = salloc(st, "mTm", [128, 8, 512], BF16)
        ysT = qT
        ycT = salloc(st, "ycT", [128, 8, 512], BF16)
        wb = [salloc(st, f"wbm{i}", [128, 8, 512], BF16) for i in range(2)]
        ebuf = [salloc(st, f"e{i}", [128, 512], F32) for i in range(2)]
        ebuf += [mT[:, 2 * j:2 * j + 2, :].rearrange("p a b -> p (a b)").bitcast(F32) for j in range(2)]
        ekeys = {0: [("e", 0)], 1: [("e", 1)], 2: [("mT", 0), ("mT", 1)], 3: [("mT", 2), ("mT", 3)]}
        spb = [salloc(st, f"sp{i}", [128, 512], BF16) for i in range(2)]
        E1 = [salloc(st, f"E1{i}", [128, 512], F32) for i in range(2)]
        Ab = [salloc(st, f"A{i}", [128, 512], BF16) for i in range(2)]
        pbuf = salloc(st, "pbuf", [128, 514], F32)
        phalo = salloc(st, "phalo", [128, 8, 2], F32)
        tmpa = salloc(st, "tmpa", [128, 512], F32)
        load_gb(1)
        P.add("dve", lambda e: e.memset(phalo[:].rearrange("p a b -> p (a b)"), 0.0), writes=[("phalo", c) for c in range(8)])
        wcnt = [0]
        SCALE = float(128 ** -0.5)

        def wload(key):
            b, parts = BLK[key]
            ncols = max(c0 + n for (_, c0, n) in parts)
            slot = wcnt[0] % 2
            wcnt[0] += 1
            dma("sp", wb[slot][:, :, 0:ncols], wsc[b][:, :, 0:ncols],
                reads=[("wsc", b, j) for j in range(ncols // 128)],
                writes=[("wb", slot, j) for j in range(ncols // 128)])
            return slot

        def proj_fm(slot, cb, dst_fn, rkeys, XTsrc, xk):
            bank = P.nextbank()

            def mm(e):
                for k in range(8):
                    ins = e.matmul(ps[bank][:], lhsT=wb[slot][:, k, cb * 128:(cb + 1) * 128], rhs=XTsrc[:, k, :],
                                   start=(k == 0), stop=(k == 7))
                return ins
            P.add("pe", mm, reads=[("wb", slot, cb)] + rkeys, writes=[("ps", bank)])
            return bank

        def xkeys_of(T):
            return [("XTm", T % 2, c) for c in range(4)]

        def proj_fm_g(slot, cb, rkeys, XTsrc):
            bank = P.nextbank()
            P.reserved.add(bank)
            for kk in range(4):
                def mm(e, kk=kk):
                    for k in (2 * kk, 2 * kk + 1):
                        ins = e.matmul(ps[bank][:], lhsT=wb[slot][:, k, cb * 128:(cb + 1) * 128], rhs=XTsrc[:, k, :],
                                       start=(k == 0), stop=(k == 7))
                    return ins
                P.add("pe", mm, reads=[("wb", slot, cb)] + rkeys, writes=[("ps", bank)])
                if kk < 3:
                    yield "pe"
            return bank

        def norm_gen(T):
            XT = XTs[T % 2]
            tiles = list(range(4 * T, 4 * T + 4))
            lo, hi = tiles[0], tiles[-1] + 1
            for i in tiles:
                P.add("act", lambda e, i=i: e.activation(out=xn0[:], in_=H[:, i, :], func=AF.Square, accum_out=ss[:, i:i + 1]),
                      reads=[("H", i)], writes=[("ss", i), ("xn", 0)])
            P.add("dve", lambda e: e.tensor_scalar(out=rstd[:, lo:hi], in0=ss[:, lo:hi], scalar1=1.0 / D, scalar2=EPS,
                                                   op0=ALU.mult, op1=ALU.add),
                  reads=[("ss", i) for i in tiles], writes=[("rstd",)])
            P.add("act", lambda e: e.activation(out=rstd[:, lo:hi], in_=rstd[:, lo:hi], func=AF.Ln),
                  reads=[("rstd",)], writes=[("rstd",)])
            P.add("act", lambda e: e.activation(out=rstd[:, lo:hi], in_=rstd[:, lo:hi], func=AF.Exp, scale=-0.5),
                  reads=[("rstd",)], writes=[("rstd",)])
            yield "dve"
            for i in tiles:
                P.add("dve", lambda e, i=i: e.scalar_tensor_tensor(out=xn0[:], in0=H[:, i, :], scalar=rstd[:, i:i + 1], in1=gb[:],
                                                                   op0=ALU.mult, op1=ALU.mult),
                      reads=[("H", i), ("rstd",), ("gb",)], writes=[("xn", 0)])
                bank = P.nextbank()
                P.reserved.add(bank)
                pview = ps[bank][:].bitcast(BF16)
                for kk in range(4):
                    def tr(e, kk=kk, pview=pview):
                        for k in (2 * kk, 2 * kk + 1):
                            ins = e.transpose(pview[:, k * 128:(k + 1) * 128], xn0[:, k * 128:(k + 1) * 128], ident[:])
                        return ins
                    P.add("pe", tr, reads=[("xn", 0), ("ident",)], writes=[("ps", bank)])
                    if kk < 3:
                        yield "pe"
                yield "dve"
                c0 = (i - 4 * T) * 128
                P.add("dve", lambda e, c0=c0, pview=pview: e.tensor_copy(out=XT[:, :, c0:c0 + 128],
                                                                        in_=pview.rearrange("p (k c) -> p k c", k=8)),
                      reads=[("ps", bank)], writes=[("XTm", T % 2, c0 // 128)])
                P.reserved.discard(bank)
                yield "dve"

        def k_gen(T):
            XT, xkeys = XTs[T % 2], xkeys_of(T)
            for blk in range(2):
                slot = wload(("k", blk))
                for cb in range(4):
                    fc = blk * 4 + cb
                    bank = yield from proj_fm_g(slot, cb, xkeys, XT)
                    yield "dve"
                    P.add("dve", lambda e, fc=fc, bank=bank: e.tensor_copy(out=kT[:, fc, T * 512:(T + 1) * 512], in_=ps[bank][:]),
                          reads=[("ps", bank)], writes=[("kT", fc, T)])
                    P.reserved.discard(bank)
                    yield "pe"

        def v_gen(T):
            XT = XTs[T % 2]
            for blk in range(2):
                slot = wload(("v", blk))
                for ts in range(4):
                    bank = P.nextbank()
                    P.reserved.add(bank)
                    for kk in range(4):
                        def mm(e, kk=kk, slot=slot, ts=ts, bank=bank):
                            for k in (2 * kk, 2 * kk + 1):
                                ins = e.matmul(ps[bank][:], lhsT=XT[:, k, ts * 128:(ts + 1) * 128], rhs=wb[slot][:, k, :],
                                               start=(k == 0), stop=(k == 7))
                            return ins
                        P.add("pe", mm, reads=[("wb", slot, j) for j in range(4)] + [("XTm", T % 2, ts)], writes=[("ps", bank)])
                        if kk < 3:
                            yield "pe"
                    yield "dve"
                    P.add("dve", lambda e, bank=bank, ts=ts, blk=blk: e.tensor_copy(
                        out=V[:, 4 * T + ts, blk * 512:(blk + 1) * 512], in_=ps[bank][:]),
                        reads=[("ps", bank)], writes=[("V", 4 * T + ts, blk)])
                    P.reserved.discard(bank)
                    yield "pe"

        def drain(gens):
            for g in list(gens):
                for _ in g:
                    pass
            del gens[:]

        drain([norm_gen(0), k_gen(0), v_gen(0)])
        for T in range(4):
            XT, xkeys = XTs[T % 2], xkeys_of(T)
            for blk in range(2):
                slot = wload(("q", blk))
                for cb in range(4):
                    fc = blk * 4 + cb
                    bank = proj_fm(slot, cb, None, xkeys, XT, "XTm")
                    P.add("act", lambda e, fc=fc, bank=bank: e.activation(out=qT[:, fc, :], in_=ps[bank][:], func=AF.Copy,
                                                                         scale=SCALE),
                          reads=[("ps", bank)], writes=[("qT", fc)])
            def conv_gen(c, XT=XT, xkeys=xkeys):
                slot = wload(("conv", c))
                b_cc = yield from proj_fm_g(slot, 1, xkeys, XT)
                yield "dve"
                P.add("dve", lambda e: e.tensor_copy(out=tmpa[:], in_=ps[b_cc][:]), reads=[("ps", b_cc)], writes=[("tmpa",)])
                P.reserved.discard(b_cc)
                yield "pe"
                b_cx = yield from proj_fm_g(slot, 2, xkeys, XT)
                yield "dve"
                P.add("dve", lambda e: e.tensor_copy(out=pbuf[:, 0:2], in_=phalo[:, c, :]),
                      reads=[("phalo", c)], writes=[("pbuf",)])
                P.add("dve", lambda e: e.tensor_tensor(out=pbuf[:, 2:514], in0=tmpa[:], in1=ps[b_cx][:], op=ALU.mult),
                      reads=[("tmpa",), ("ps", b_cx), ("pbuf",)], writes=[("pbuf",)])
                P.reserved.discard(b_cx)
                P.add("dve", lambda e: e.tensor_copy(out=phalo[:, c, :], in_=pbuf[:, 512:514]),
                      reads=[("pbuf",)], writes=[("phalo", c)])
                yield "dve"
                P.add("dve", lambda e: e.tensor_scalar(out=tmpa[:], in0=pbuf[:, 2:514], scalar1=convw[:, c, 2:3], scalar2=None,
                                                       op0=ALU.mult),
                      reads=[("pbuf",), ("convw",)], writes=[("tmpa",)])
                P.add("dve", lambda e: e.scalar_tensor_tensor(out=tmpa[:], in0=pbuf[:, 1:513], scalar=convw[:, c, 1:2],
                                                              in1=tmpa[:], op0=ALU.mult, op1=ALU.add),
                      reads=[("pbuf",), ("tmpa",), ("convw",)], writes=[("tmpa",)])
                yield "dve"
                P.add("dve", lambda e: e.scalar_tensor_tensor(out=tmpa[:], in0=pbuf[:, 0:512], scalar=convw[:, c, 0:1],
                                                              in1=tmpa[:], op0=ALU.mult, op1=ALU.add),
                      reads=[("pbuf",), ("tmpa",), ("convw",)], writes=[("tmpa",)])
                yield "pe"
                b_cb = yield from proj_fm_g(slot, 0, xkeys, XT)
                yield "dve"
                P.add("dve", lambda e: e.tensor_tensor(out=ycT[:, c, :], in0=tmpa[:], in1=ps[b_cb][:], op=ALU.mult),
                      reads=[("tmpa",), ("ps", b_cb)], writes=[("ycT", c)])
                P.reserved.discard(b_cb)

            gens = [[conv_gen(c), "pe"] for c in range(8)]
            nmicro = 8 * 17
            if T < 3:
                gens += [[norm_gen(T + 1), "dve"], [k_gen(T + 1), "pe"], [v_gen(T + 1), "pe"]]
                nmicro += 21 + 40 + 40
            nch = 4 * T + 4
            npoints = 4 * 4 * nch
            fstate = [0, 0]

            def pump_n(n, allow_dve):
                while n > 0 and gens:
                    g, tag = gens[0]
                    if tag == "dve" and not allow_dve:
                        break
                    try:
                        gens[0][1] = next(g)
                        n -= 1
                        fstate[1] += 1
                    except StopIteration:
                        gens.pop(0)

            def step_done(allow_dve=False):
                fstate[0] += 1
                target = (nmicro * fstate[0] + npoints - 1) // npoints
                pump_n(target - fstate[1], allow_dve)

            tcnt = [0]
            for hp in range(4):
                heads = (2 * hp, 2 * hp + 1)
                Rb = [P.nextbank(), P.nextbank()]
                P.reserved.update(Rb)
                Ob = [P.nextbank(), P.nextbank()]
                P.reserved.update(Ob)
                zb = {}

                def emit_z(h, c):
                    bank = P.nextbank()
                    P.reserved.add(bank)
                    dd = c - 4 * T
                    c0 = max(dd, 0) * 128

                    def mm(e, h=h, c=c, bank=bank, dd=dd, c0=c0):
                        ins = e.matmul(ps[bank][:, c0:512], lhsT=kT[:, h, c * 128:(c + 1) * 128], rhs=qT[:, h, c0:512],
                                       start=True, stop=(dd < 0))
                        if dd >= 0:
                            ins = e.matmul(ps[bank][:, c0:c0 + 128], lhsT=ident[:], rhs=maskb[:], start=False, stop=True)
                        return ins
                    P.add("pe", mm, reads=[("kT", h, c // 4), ("qT", h), ("ident",), ("maskb",)], writes=[("ps", bank)])
                    zb[(h, c)] = bank

                def split_mm(e, out_bank, lhsT, rhs_buf, dd, c0, last):
                    if dd >= 0:
                        ins = e.matmul(ps[out_bank][:, c0:c0 + 128], lhsT=lhsT, rhs=rhs_buf[:, c0:c0 + 128], start=(dd == 3),
                                       stop=last, skip_group_check=True)
                        if dd < 3:
                            ins = e.matmul(ps[out_bank][:, c0 + 128:512], lhsT=lhsT, rhs=rhs_buf[:, c0 + 128:512],
                                           start=False, stop=last, skip_group_check=True)
                    else:
                        ins = e.matmul(ps[out_bank][:], lhsT=lhsT, rhs=rhs_buf[:], start=False, stop=last, skip_group_check=True)
                    return ins

                for h in heads:
                    emit_z(h, nch - 1)
                for c in range(nch - 1, -1, -1):
                    dd = c - 4 * T
                    c0 = max(dd, 0) * 128
                    slots = {}
                    eslot = {}
                    for i, h in enumerate(heads):
                        s_ = i
                        se = 2 * (tcnt[0] % 2) + i
                        slots[h] = s_
                        eslot[h] = se
                        bank = zb[(h, c)]
                        P.add("act", lambda e, se=se, bank=bank, c0=c0: e.activation(out=ebuf[se][:, c0:512], in_=ps[bank][:, c0:512],
                                                                                     func=AF.Exp),
                              reads=[("ps", bank)], writes=ekeys[se])
                        P.reserved.discard(bank)
                    tcnt[0] += 1
                    for h in heads:
                        s_ = slots[h]
                        se = eslot[h]
                        P.add("act", lambda e, s_=s_, se=se, c0=c0: e.activation(out=spb[s_][:, c0:512], in_=ebuf[se][:, c0:512],
                                                                                 func=AF.Ln, bias=1.0),
                              reads=ekeys[se], writes=[("sp", s_)])
                    for i, h in enumerate(heads):
                        s_ = slots[h]
                        P.add("pe", lambda e, s_=s_, rb=Rb[i], dd=dd, c0=c0: split_mm(e, rb, triA[:], spb[s_], dd, c0, True),
                              reads=[("triA",), ("sp", s_)], writes=[("ps", Rb[i])])
                    step_done(True)
                    if c > 0:
                        for h in heads:
                            emit_z(h, c - 1)
                    step_done()
                    for i, h in enumerate(heads):
                        s_ = slots[h]
                        P.add("act", lambda e, s_=s_, rb=Rb[i], c0=c0: e.activation(out=E1[s_][:, c0:512], in_=ps[rb][:, c0:512],
                                                                                    func=AF.Exp),
                              reads=[("ps", Rb[i])], writes=[("E1", s_)])
                        if c > 0:
                            P.add("pe", lambda e, s_=s_, rb=Rb[i], c0=c0: e.matmul(ps[rb][:, c0:512], lhsT=triB[:], rhs=spb[s_][:, c0:512],
                                                                                start=False, stop=True, skip_group_check=True),
                                  reads=[("triB",), ("sp", s_)], writes=[("ps", Rb[i])])
                        se = eslot[h]
                        P.add("dve", lambda e, s_=s_, se=se, c0=c0: e.tensor_tensor(out=Ab[s_][:, c0:512], in0=ebuf[se][:, c0:512],
                                                                                    in1=E1[s_][:, c0:512], op=ALU.mult),
                              reads=ekeys[se] + [("E1", s_)], writes=[("A", s_)])
                        P.add("pe", lambda e, s_=s_, ob=Ob[i], h=h, c=c, dd=dd, c0=c0: split_mm(
                            e, ob, V[:, c, h * 128:(h + 1) * 128], Ab[s_], dd, c0, (c == 0)),
                            reads=[("V", c, h // 4), ("A", s_)], writes=[("ps", Ob[i])])
                        if i == 0:
                            step_done()
                    step_done(True)
                for i, h in enumerate(heads):
                    P.add("dve", lambda e, h=h, ob=Ob[i]: e.tensor_copy(out=ysT[:, h, :], in_=ps[ob][:]),
                          reads=[("ps", Ob[i])], writes=[("qT", h)])
                P.reserved.difference_update(Rb)
                P.reserved.difference_update(Ob)
            drain([g for g, _ in gens])
            del gens[:]
            for fc in range(8):
                slot = wload(("op", fc))
                bA = proj_fm(slot, 0, None, [("ycT", k) for k in range(8)], ycT, "ycT")
                bB = proj_fm(slot, 1, None, [("qT", k) for k in range(8)], ysT, "ysT")
                bGc = proj_fm(slot, 2, None, xkeys, XT, "XTm")
                bGs = proj_fm(slot, 3, None, xkeys, XT, "XTm")
                P.add("act", lambda e, b=bGc, fc=fc: e.activation(out=tmpa[:], in_=ps[b][:], func=AF.Sigmoid,
                                                                 bias=bgate[:, fc:fc + 1]),
                      reads=[("ps", bGc), ("bgate",)], writes=[("tmpa",)])
                P.add("dve", lambda e, b=bA: e.tensor_tensor(out=tmpa[:], in0=tmpa[:], in1=ps[b][:], op=ALU.mult),
                      reads=[("tmpa",), ("ps", bA)], writes=[("tmpa",)])
                P.add("act", lambda e, b=bGs, fc=fc: e.activation(out=pbuf[:, 0:512], in_=ps[b][:], func=AF.Sigmoid,
                                                                 bias=bgate[:, 8 + fc:9 + fc]),
                      reads=[("ps", bGs), ("bgate",)], writes=[("pbuf",)])
                P.add("dve", lambda e, b=bB: e.tensor_tensor(out=pbuf[:, 0:512], in0=pbuf[:, 0:512], in1=ps[b][:], op=ALU.mult),
                      reads=[("pbuf",), ("ps", bB)], writes=[("pbuf",)])
                P.add("dve", lambda e, fc=fc: e.tensor_tensor(out=mT[:, fc, :], in0=tmpa[:], in1=pbuf[:, 0:512], op=ALU.add),
                      reads=[("tmpa",), ("pbuf",)], writes=[("mT", fc)])
            for dh in range(2):
                slot = wload(("wo", dh))
                for ts in range(4):
                    bank = P.nextbank()

                    def mm(e, slot=slot, ts=ts, bank=bank):
                        for k in range(8):
                            ins = e.matmul(ps[bank][:], lhsT=mT[:, k, ts * 128:(ts + 1) * 128], rhs=wb[slot][:, k, :],
                                           start=(k == 0), stop=(k == 7))
                        return ins
                    P.add("pe", mm, reads=[("wb", slot, j) for j in range(4)] + [("mT", k) for k in range(8)],
                          writes=[("ps", bank)])
                    t = 4 * T + ts
                    P.add("dve", lambda e, bank=bank, t=t, dh=dh: e.tensor_tensor(
                        out=H[:, t, dh * 512:(dh + 1) * 512], in0=ps[bank][:], in1=H[:, t, dh * 512:(dh + 1) * 512], op=ALU.add),
                        reads=[("ps", bank), ("H", t)], writes=[("H", t)])
        EM.flush()
        st.close()
        XNB[:] = [xn0]

    def cross_phase():
        EM.flush()
        st = contextlib.ExitStack()
        XNB[:] = [xn0, salloc(st, "xn1_c", [128, D], BF16)]
        memt = salloc(st, "memt", [128, 2, D], F32)
        mnT = salloc(st, "mnT", [128, 8, 256], BF16)
        kcT = salloc(st, "kcT", [128, 8, 256], BF16)
        Vc = salloc(st, "Vc", [128, 2, D], BF16)
        XT = [salloc(st, f"XTc{i}", [128, 8, 512], BF16) for i in range(2)]
        qcT = [salloc(st, f"qcT{i}", [128, 8, 512], BF16) for i in range(2)]
        oT = [salloc(st, f"oT{i}", [128, 8, 512], BF16) for i in range(2)]
        wq = [salloc(st, f"wq{i}", [128, 8, 512], BF16) for i in range(2)]
        wo = [salloc(st, f"wo{i}", [128, 8, 512], BF16) for i in range(2)]
        wkv = [salloc(st, f"wkv{i}", [128, 8, 512], BF16) for i in range(2)]
        Pf = [salloc(st, f"Pf{i}", [128, 4, 256], F32) for i in range(2)]
        Pn = [salloc(st, f"Pn{i}", [128, 4, 256], BF16) for i in range(2)]
        PnT = [salloc(st, f"PnT{i}", [128, 8, 128], BF16) for i in range(2)]
        mx = [salloc(st, f"mx{i}", [128, 4], F32) for i in range(2)]
        sm = [salloc(st, f"sm{i}", [128, 4], F32) for i in range(2)]
        rs = [salloc(st, f"rs{i}", [128, 4], F32) for i in range(2)]

        dma("sp", memt[:], mem_d.rearrange("(i p) d -> p i d", p=128), writes=[("memt", 0), ("memt", 1)])
        load_gb(3)
        for blk in range(2):
            dma("pool", wkv[blk][:], wckv_d[:, blk * 512:(blk + 1) * 512].rearrange("(k p) n -> p k n", p=128),
                writes=[("wkv", blk)])
        norm_tiles([0, 1], mnT, lambda i: i * 128, SRC=memt, skey="memt", xkey="mnT")
        load_gb(2)
        for blk in range(2):
            for cb in range(4):
                fc = blk * 4 + cb
                bank = P.nextbank()

                def mm(e, blk=blk, cb=cb, bank=bank):
                    for k in range(8):
                        ins = e.matmul(ps[bank][:, 0:256], lhsT=wkv[blk][:, k, cb * 128:(cb + 1) * 128], rhs=mnT[:, k, :],
                                       start=(k == 0), stop=(k == 7))
                    return ins
                P.add("pe", mm, reads=[("wkv", blk), ("mnT", 0), ("mnT", 1)], writes=[("ps", bank)])
                P.add("act", lambda e, fc=fc, bank=bank: e.copy(out=kcT[:, fc, :], in_=ps[bank][:, 0:256]),
                      reads=[("ps", bank)], writes=[("kcT", fc)])
        for blk in range(2):
            dma("pool", wkv[blk][:], wckv_d[:, 1024 + blk * 512:1024 + (blk + 1) * 512].rearrange("(k p) n -> p k n", p=128),
                writes=[("wkv", blk)])
        for blk in range(2):
            dma("pool", wq[blk][:], wcq_d[:, blk * 512:(blk + 1) * 512].rearrange("(k p) n -> p k n", p=128),
                writes=[("wq", blk)])
        for blk in range(2):
            dma("pool", wo[blk][:], wcout_d[:, blk * 512:(blk + 1) * 512].rearrange("(k p) n -> p k n", p=128),
                writes=[("wo", blk)])
        for blk in range(2):
            for mc in range(2):
                bank = P.nextbank()

                def mm(e, blk=blk, mc=mc, bank=bank):
                    for k in range(8):
                        ins = e.matmul(ps[bank][:], lhsT=mnT[:, k, mc * 128:(mc + 1) * 128], rhs=wkv[blk][:, k, :],
                                       start=(k == 0), stop=(k == 7))
                    return ins
                P.add("pe", mm, reads=[("wkv", blk), ("mnT", mc)], writes=[("ps", bank)])
                P.add("dve", lambda e, mc=mc, blk=blk, bank=bank: e.tensor_copy(out=Vc[:, mc, blk * 512:(blk + 1) * 512],
                                                                                in_=ps[bank][:]),
                      reads=[("ps", bank)], writes=[("Vc", mc, blk)])

        def qproj(T, fc):
            par = T % 2
            blk, cb = fc // 4, fc % 4
            bank = P.nextbank()

            def mm(e):
                for k in range(8):
                    ins = e.matmul(ps[bank][:], lhsT=wq[blk][:, k, cb * 128:(cb + 1) * 128], rhs=XT[par][:, k, :],
                                   start=(k == 0), stop=(k == 7))
                return ins
            P.add("pe", mm, reads=[("wq", blk)] + [("XTc", par, c) for c in range(4)], writes=[("ps", bank)])
            P.add("act", lambda e: e.activation(out=qcT[par][:, fc, :], in_=ps[bank][:], func=AF.Copy, scale=1.0 / 16.0),
                  reads=[("ps", bank)], writes=[("qcT", par, fc)])

        def outproj(T, ts):
            par = T % 2
            for dh in range(2):
                bank = P.nextbank()

                def mm(e, dh=dh, bank=bank):
                    for k in range(8):
                        ins = e.matmul(ps[bank][:], lhsT=oT[par][:, k, ts * 128:(ts + 1) * 128], rhs=wo[dh][:, k, :],
                                       start=(k == 0), stop=(k == 7))
                    return ins
                P.add("pe", mm, reads=[("wo", dh), ("oT", par, ts, 0), ("oT", par, ts, 1)], writes=[("ps", bank)])
                t = 4 * T + ts
                P.add("dve", lambda e, bank=bank, t=t, dh=dh: e.tensor_tensor(
                    out=H[:, t, dh * 512:(dh + 1) * 512], in0=ps[bank][:], in1=H[:, t, dh * 512:(dh + 1) * 512], op=ALU.add),
                    reads=[("ps", bank), ("H", t)], writes=[("H", t)])

        def cnorm(T):
            par = T % 2
            norm_tiles(list(range(4 * T, 4 * T + 4)), XT[par], lambda i: (i - 4 * T) * 128, xkey=("XTc", par), evac_eng="dve")

        def A(n):
            T, ts = divmod(n, 4)
            par = T % 2
            sbk = [P.nextbank(), P.nextbank()]
            P.reserved.update(sbk)

            def mm(e):
                for h in range(4):
                    for c in range(2):
                        ins = e.matmul(ps[sbk[h // 2]][:, (h % 2) * 256:(h % 2 + 1) * 256],
                                       lhsT=qcT[par][:, 2 * h + c, ts * 128:(ts + 1) * 128],
                                       rhs=kcT[:, 2 * h + c, :], start=(c == 0), stop=(c == 1))
                return ins
            P.add("pe", mm, reads=[("qcT", par, k) for k in range(8)] + [("kcT", k) for k in range(8)],
                  writes=[("ps", sbk[0]), ("ps", sbk[1])])
            return sbk

        def B(n, sbk):
            b_ = n % 2
            for i in range(2):
                pv = ps[sbk[i]][:].rearrange("p (h m) -> p h m", h=2)
                P.add("dve", lambda e, pv=pv, i=i: e.tensor_reduce(out=mx[b_][:, 2 * i:2 * i + 2], in_=pv, axis=AX.X, op=ALU.max),
                      reads=[("ps", sbk[i])], writes=[("mx", b_, i)])
            P.add("dve", lambda e: e.tensor_scalar(out=mx[b_][:], in0=mx[b_][:], scalar1=-1.0, scalar2=None, op0=ALU.mult),
                  reads=[("mx", b_, 0), ("mx", b_, 1)], writes=[("mx", b_, 0), ("mx", b_, 1)])
            for h in range(4):
                P.add("act", lambda e, h=h: e.activation(
                    out=Pf[b_][:, h, :], in_=ps[sbk[h // 2]][:, (h % 2) * 256:(h % 2 + 1) * 256],
                    func=AF.Exp, bias=mx[b_][:, h:h + 1], accum_out=sm[b_][:, h:h + 1]),
                    reads=[("ps", sbk[h // 2]), ("mx", b_, h // 2)], writes=[("Pf", b_, h), ("sm", b_, h)])
            P.reserved.difference_update(sbk)
            P.add("dve", lambda e: e.reciprocal(out=rs[b_][:], in_=sm[b_][:]),
                  reads=[("sm", b_, h) for h in range(4)], writes=[("rs", b_)])
            P.add("dve", lambda e: e.tensor_tensor(out=Pn[b_][:], in0=Pf[b_][:],
                                                   in1=rs[b_][:].unsqueeze(2).to_broadcast([128, 4, 256]), op=ALU.mult),
                  reads=[("Pf", b_, h) for h in range(4)] + [("rs", b_)], writes=[("Pn", b_)])

        def C(n):
            b_ = n % 2
            bank2 = P.nextbank()
            pview = ps[bank2][:].bitcast(BF16)

            def tr(e):
                for h in range(4):
                    for mc in range(2):
                        j = h * 2 + mc
                        ins = e.transpose(pview[:, j * 128:(j + 1) * 128], Pn[b_][:, h, mc * 128:(mc + 1) * 128], ident[:])
                return ins
            P.add("pe", tr, reads=[("Pn", b_), ("ident",)], writes=[("ps", bank2)])
            P.add("act", lambda e: e.copy(out=PnT[b_][:], in_=pview.rearrange("p (j c) -> p j c", j=8)),
                  reads=[("ps", bank2)], writes=[("PnT", b_)])

        def Dd(n):
            T, ts = divmod(n, 4)
            par = T % 2
            b_ = n % 2
            obk = [P.nextbank(), P.nextbank()]

            def mm2(e):
                for h in range(4):
                    for c in range(2):
                        f = 2 * h + c
                        for mc in range(2):
                            ins = e.matmul(ps[obk[f // 4]][:, (f % 4) * 128:(f % 4 + 1) * 128],
                                           lhsT=Vc[:, mc, f * 128:(f + 1) * 128],
                                           rhs=PnT[b_][:, h * 2 + mc, :], start=(mc == 0), stop=(mc == 1))
                return ins
            P.add("pe", mm2, reads=[("PnT", b_)] + [("Vc", mc, b) for mc in range(2) for b in range(2)],
                  writes=[("ps", obk[0]), ("ps", obk[1])])
            for i in range(2):
                P.add("dve", lambda e, i=i: e.tensor_copy(
                    out=oT[par][:, 4 * i:4 * i + 4, ts * 128:(ts + 1) * 128],
                    in_=ps[obk[i]][:].rearrange("p (f t) -> p f t", f=4)),
                    reads=[("ps", obk[i])], writes=[("oT", par, ts, i)])

        cnorm(0)
        for fc in range(8):
            qproj(0, fc)
        cur = A(0)
        B(0, cur)
        qsched = {0: [0, 1, 2], 1: [3, 4, 5], 2: [6, 7], 3: []}
        for n in range(16):
            T, ts = divmod(n, 4)
            if ts == 0 and T < 3:
                cnorm(T + 1)
            nxt = A(n + 1) if n + 1 < 16 and (n + 1) % 4 != 0 else None
            C(n)
            if nxt is not None:
                B(n + 1, nxt)
            if T < 3:
                for fc in qsched[ts]:
                    qproj(T + 1, fc)
            if n + 1 < 16 and (n + 1) % 4 == 0:
                nxt = A(n + 1)
                B(n + 1, nxt)
            if n >= 1:
                outproj(*divmod(n - 1, 4))
            Dd(n)
        prev = (3, 3)
        outproj(*prev)
        EM.flush()
        st.close()
        XNB[:] = [xn0]

    if stages >= 2:
        mixer_phase()
    if stages >= 3:
        cross_phase()
    if stages >= 4:
        ffn_phase(w2gu_d, w2d_d, 4, "f2", fin=final_norm)
    if stages >= 4 and final_norm:
        EM.flush()
        gstack.close()
        return nc

    EM.flush()
    st = contextlib.ExitStack()
    ob = [salloc(st, f"ob{i}", [128, D], F32) for i in range(2)]
    if final_norm:
        load_gb(5)
        for i in range(NT):
            P.add("act", lambda e, i=i: e.activation(out=XNB[0][:], in_=H[:, i, :], func=AF.Square,
                                                     accum_out=ss[:, i:i + 1]),
                  reads=[("H", i)], writes=[("ss", i), ("xn", 0)])
        P.add("dve", lambda e: e.tensor_scalar(out=rstd[:], in0=ss[:], scalar1=1.0 / D, scalar2=EPS,
                                               op0=ALU.mult, op1=ALU.add),
              reads=[("ss", i) for i in range(NT)], writes=[("rstd",)])
        P.add("act", lambda e: e.activation(out=rstd[:], in_=rstd[:], func=AF.Ln), reads=[("rstd",)], writes=[("rstd",)])
        P.add("act", lambda e: e.activation(out=rstd[:], in_=rstd[:], func=AF.Exp, scale=-0.5),
              reads=[("rstd",)], writes=[("rstd",)])
    outs = []
    for i in range(NT):
        b = i % 2
        if final_norm:
            P.add("dve", lambda e, i=i, b=b: e.scalar_tensor_tensor(out=ob[b][:], in0=H[:, i, :], scalar=rstd[:, i:i + 1],
                                                                    in1=gb[:], op0=ALU.mult, op1=ALU.mult),
                  reads=[("H", i), ("rstd",), ("gb",)], writes=[("ob", b)])
        else:
            P.add("dve", lambda e, i=i, b=b: e.tensor_copy(out=ob[b][:], in_=H[:, i, :]),
                  reads=[("H", i)], writes=[("ob", b)])
        outs.append(dma("sp", y_d[i * 128:(i + 1) * 128, :], ob[b][:], reads=[("ob", b)], writes=[("y", i)]))
    P.add("sp", None, reads=[("y", i) for i in range(NT)], writes=[("done",)])

    EM.flush()
    st.close()
    gstack.close()
    return nc


class Emitter:
    def __init__(self, nc, P, sems, dsems):
        self.nc, self.P, self.sems, self.dsems = nc, P, sems, dsems
        self.cnt = {e: 0 for e in sems}
        self.dq = {q: 0 for q in dsems}
        self.dslot = {q: [0] * ND for q in dsems}
        self.dlast = {q: [None] * ND for q in dsems}
        self.waited = {}
        self.done = 0

    def flush(self):
        P = self.P
        ops = P.ops[self.done:]
        self.done = len(P.ops)
        for op in P.lastw.values():
            op.needed = True
        for rd in P.readers.values():
            for d in rd[0].values():
                d.needed = True
        for op in ops:
            if op.is_dma:
                q = op.eng
                s_ = self.dq[q] % ND
                self.dq[q] += 1
                self.dslot[q][s_] += 1
                op.sem = self.dsems[q][s_]
                op.val = 16 * self.dslot[q][s_]
                op.prev = self.dlast[q][s_]
                self.dlast[q][s_] = op
            elif op.needed and op.fn is not None:
                self.cnt[op.eng] += 1
                op.sem = self.sems[op.eng]
                op.val = self.cnt[op.eng]
        engmap = {"pe": "tensor", "act": "scalar", "dve": "vector", "pool": "gpsimd", "sp": "sync"}
        byeng = dict((e, []) for e in engmap)
        for op in ops:
            byeng[op.eng].append(op)
        waited = self.waited
        with self.nc.Block() as block:
            for eng, bname in engmap.items():
                eops = byeng[eng]
                if not eops:
                    continue

                def body(e, eops=eops, eng=eng):
                    w = waited.setdefault(eng, {})

                    def wait(sem, val):
                        key = id(sem)
                        if w.get(key, (None, 0))[1] >= val:
                            return
                        w[key] = (sem, val)
                        e.wait_ge(sem, val)

                    for op in eops:
                        if op.prev is not None:
                            wait(op.prev.sem, op.prev.val)
                        for d in op.deps:
                            wait(d.sem, d.val)
                        if op.fn is None:
                            continue
                        ins = op.fn(e)
                        if op.is_dma:
                            ins.then_inc(op.sem, 16)
                        elif op.needed:
                            ins.then_inc(op.sem, 1)

                getattr(block, bname)(body)


def _host_consts():
    bf = ml_dtypes.bfloat16
    ident = np.eye(128, dtype=np.float32).astype(bf)
    j = np.arange(128)[:, None]
    s = np.arange(128)[None, :]
    triA = np.where(j >= s, -1.0, 0.0).astype(np.float32).astype(bf)
    triB = np.where(j < s, -1.0, 0.0).astype(np.float32).astype(bf)
    maskb = np.where(j >= s, NEG, 0.0).astype(np.float32).astype(bf)
    return ident, triA, triB, maskb


_CACHE = {}


def kernel(x, mem, g_ffn1, w_ffn1_gu, w_ffn1_down, g_mix, w_in, b_gate, conv_w, w_conv_out, w_attn_out, w_o,
           g_cross, g_mem, w_cq, w_ckv, w_co, g_ffn2, w_ffn2_gu, w_ffn2_down, g_final, _stages=5, _final_norm=True):
    f = lambda a: np.ascontiguousarray(np.asarray(a, dtype=np.float32))
    x = f(x)
    mem = f(mem)
    n = 8
    key = (_stages, _final_norm)
    if key not in _CACHE:
        _CACHE[key] = build_program(_stages, _final_norm)
    nc = _CACHE[key]
    ident, triA, triB, maskb = _host_consts()
    gv = np.stack([np.broadcast_to(f(g)[None, :], (128, D)) for g in (g_ffn1, g_mix, g_cross, g_mem, g_ffn2, g_final)])
    gv = np.ascontiguousarray(gv)
    bg = np.ascontiguousarray(f(b_gate).reshape(16, 128).T)
    cw = np.ascontiguousarray(f(conv_w).T.reshape(8, 128, 3).transpose(1, 0, 2))
    shared = {
        "gv": gv, "w_ffn1_gu": f(w_ffn1_gu), "w_ffn1_down": f(w_ffn1_down), "w_in": f(w_in),
        "w_conv_out": f(w_conv_out), "w_attn_out": f(w_attn_out), "w_o": f(w_o), "w_cq": f(w_cq),
        "w_ckv": f(w_ckv), "w_co": f(w_co), "w_ffn2_gu": f(w_ffn2_gu), "w_ffn2_down": f(w_ffn2_down),
        "bgate": bg, "convw": cw, "ident": ident, "triA": triA, "triB": triB, "maskb": maskb,
    }
    in_maps = []
    for c in range(n):
        m = dict(shared)
        m["x"] = x[c]
        m["mem"] = mem[c]
        in_maps.append(m)
    res = run_bass_kernel_spmd(nc, in_maps, core_ids=list(range(n)))
    return np.stack([r["y"] for r in res.results], axis=0)
```

```python
import os
import numpy as np
import ml_dtypes
import concourse.bass as bass
import concourse.mybir as mybir
from concourse.bass_utils import run_bass_kernel_spmd

F32, BF16 = mybir.dt.float32, mybir.dt.bfloat16
AF = mybir.ActivationFunctionType
ALU = mybir.AluOpType
AX = mybir.AxisListType

S = 2048
D = 1024
DFF = 2816
MEM = 256
NT = S // 128
EPS = 1e-6
ND = 8
NEG = -30000.0


class Op:
    __slots__ = ("eng", "fn", "deps", "needed", "sem", "val", "is_dma", "phase", "prev")


class Plan:
    def __init__(self):
        self.ops = []
        self.lastw = {}
        self.readers = {}
        self.phase = 0
        self.bank = 0
        self.reserved = set()

    def nextbank(self):
        while self.bank in self.reserved:
            self.bank = (self.bank + 1) % 8
        b = self.bank
        self.bank = (self.bank + 1) % 8
        return b

    def add(self, eng, fn, reads=(), writes=(), dma=False):
        op = Op()
        op.eng, op.fn, op.is_dma, op.needed, op.phase = eng, fn, dma, dma, self.phase
        op.prev = None
        deps, seen = [], set()

        def add_dep(d):
            if d is None or id(d) in seen:
                return
            if d.eng == "pe" and eng == "pe" and not d.is_dma and not dma:
                return
            seen.add(id(d))
            deps.append(d)
            d.needed = True

        for r in reads:
            add_dep(self.lastw.get(r))
        for w in writes:
            add_dep(self.lastw.get(w))
            rd = self.readers.get(w)
            if rd:
                for d in rd[0].values():
                    add_dep(d)
                for d in rd[1]:
                    add_dep(d)
        for w in writes:
            self.lastw[w] = op
            self.readers[w] = ({}, [])
        for r in reads:
            if r in writes:
                continue
            rd = self.readers.setdefault(r, ({}, []))
            if dma:
                rd[1].append(op)
            else:
                rd[0][eng] = op
        op.deps = deps
        self.ops.append(op)
        return op

    def next_phase(self):
        self.phase += 1


def build_program(stages=5, final_norm=True):
    nc = bass.Bass("TRN2", target_bir_lowering=False)
    P = Plan()

    def din(name, shape, dt=F32):
        return nc.dram_tensor(name, shape, dt, kind="ExternalInput").ap()

    x_d = din("x", [S, D])
    mem_d = din("mem", [MEM, D])
    gv_d = din("gv", [6, 128, D])
    w1gu_d = din("w_ffn1_gu", [D, 2 * DFF])
    w1d_d = din("w_ffn1_down", [DFF, D])
    win_d = din("w_in", [D, 8 * D])
    wco_d = din("w_conv_out", [D, D])
    wao_d = din("w_attn_out", [D, D])
    wo_d = din("w_o", [D, D])
    wcq_d = din("w_cq", [D, D])
    wckv_d = din("w_ckv", [D, 2 * D])
    wcout_d = din("w_co", [D, D])
    w2gu_d = din("w_ffn2_gu", [D, 2 * DFF])
    w2d_d = din("w_ffn2_down", [DFF, D])
    bg_d = din("bgate", [128, 16])
    cw_d = din("convw", [128, 8, 3])
    ident_d = din("ident", [128, 128], BF16)
    triA_d = din("triA", [128, 128], BF16)
    triB_d = din("triB", [128, 128], BF16)
    maskb_d = din("maskb", [128, 128], BF16)
    y_d = nc.dram_tensor("y", [S, D], F32, kind="ExternalOutput").ap()

    import contextlib

    gstack = contextlib.ExitStack()

    def salloc(stack, name, shape, dt):
        return stack.enter_context(nc.sbuf_tensor(name, shape, dt))

    H = salloc(gstack, "H", [128, NT, D], F32)
    H_ = H
    ident = salloc(gstack, "ident_s", [128, 128], BF16)
    triA = salloc(gstack, "triA_s", [128, 128], BF16)
    triB = salloc(gstack, "triB_s", [128, 128], BF16)
    maskb = salloc(gstack, "maskb_s", [128, 128], BF16)
    bgate = salloc(gstack, "bgate_s", [128, 16], F32)
    convw = salloc(gstack, "convw_s", [128, 8, 3], F32)
    gb = salloc(gstack, "gb", [128, D], F32)
    ss = salloc(gstack, "ss", [128, NT], F32)
    rstd = salloc(gstack, "rstd", [128, NT], F32)
    xn0 = salloc(gstack, "xn0", [128, D], BF16)
    XNB = [xn0]
    ps = [gstack.enter_context(nc.psum_tensor(f"ps{i}", [128, 512], F32)) for i in range(8)]

    sems = {}
    for e in ("pe", "act", "dve", "pool"):
        sems[e] = gstack.enter_context(nc.semaphore(f"s_{e}"))
    dsems = {}
    for q in ("sp", "pool", "act"):
        dsems[q] = [gstack.enter_context(nc.semaphore(f"d_{q}{i}")) for i in range(ND)]

    EM = Emitter(nc, P, sems, dsems)

    def dma(q, out, in_, reads=(), writes=()):
        return P.add(q, lambda e: e.dma_start(out=out, in_=in_), reads=reads, writes=writes, dma=True)

    def load_gb(idx):
        dma("sp", gb[:], gv_d[idx], writes=[("gb",)])

    BLK = {}

    def defblk(key, parts):
        BLK[key] = (len(BLK), parts)

    for blk in range(2):
        defblk(("k", blk), [(win_d[:, 4096 + blk * 512:4096 + (blk + 1) * 512], 0, 512)])
    for blk in range(2):
        defblk(("q", blk), [(win_d[:, 3072 + blk * 512:3072 + (blk + 1) * 512], 0, 512)])
    for blk in range(2):
        defblk(("v", blk), [(win_d[:, 5120 + blk * 512:5120 + (blk + 1) * 512], 0, 512)])
    for c in range(8):
        defblk(("conv", c), [(win_d[:, j * 1024 + c * 128:j * 1024 + (c + 1) * 128], j * 128, 128) for j in range(3)])
    for fc in range(8):
        defblk(("op", fc), [(wco_d[:, fc * 128:(fc + 1) * 128], 0, 128), (wao_d[:, fc * 128:(fc + 1) * 128], 128, 128),
                            (win_d[:, 6144 + fc * 128:6144 + (fc + 1) * 128], 256, 128),
                            (win_d[:, 7168 + fc * 128:7168 + (fc + 1) * 128], 384, 128)])
    for dh in range(2):
        defblk(("wo", dh), [(wo_d[:, dh * 512:(dh + 1) * 512], 0, 512)])
    wsc = nc.dram_tensor("wscratch", [len(BLK), 128, 8, 512], BF16).ap()
    conv_dmas = []
    for key, (b, parts) in BLK.items():
        for (src, c0, n) in parts:
            conv_dmas.append(lambda b=b, src=src, c0=c0, n=n: dma(
                "pool", wsc[b][:, :, c0:c0 + n], src.rearrange("(k p) n -> p k n", p=128),
                writes=[("wsc", b, c0 // 128 + j) for j in range(n // 128)]))

    xn_ctr = [0]

    def norm_pieces(tiles, XT, col_of_tile, evac_eng="act", SRC=None, skey="H", xkey="XT"):
        H = SRC if SRC is not None else H_
        xk = (lambda c: xkey + (c,)) if isinstance(xkey, tuple) else (lambda c: (xkey, c))
        lo, hi = min(tiles), max(tiles) + 1

        def stats():
            for i in tiles:
                P.add("act", lambda e, i=i: e.activation(out=XNB[0][:], in_=H[:, i, :], func=AF.Square,
                                                         accum_out=ss[:, i:i + 1]),
                      reads=[(skey, i)], writes=[("ss", i), ("xn", 0)])
            P.add("dve", lambda e: e.tensor_scalar(out=rstd[:, lo:hi], in0=ss[:, lo:hi], scalar1=1.0 / D, scalar2=EPS,
                                                   op0=ALU.mult, op1=ALU.add),
                  reads=[("ss", i) for i in tiles], writes=[("rstd",)])
            P.add("act", lambda e: e.activation(out=rstd[:, lo:hi], in_=rstd[:, lo:hi], func=AF.Ln),
                  reads=[("rstd",)], writes=[("rstd",)])
            P.add("act", lambda e: e.activation(out=rstd[:, lo:hi], in_=rstd[:, lo:hi], func=AF.Exp, scale=-0.5),
                  reads=[("rstd",)], writes=[("rstd",)])

        bsel = {}

        def mult(i):
            b = xn_ctr[0] % len(XNB)
            xn_ctr[0] += 1
            bsel[i] = b
            xnb = XNB[b]
            P.add("dve", lambda e: e.scalar_tensor_tensor(out=xnb[:], in0=H[:, i, :], scalar=rstd[:, i:i + 1], in1=gb[:],
                                                          op0=ALU.mult, op1=ALU.mult),
                  reads=[(skey, i), ("rstd",), ("gb",)], writes=[("xn", b)])

        def trn(i):
            b = bsel[i]
            xnb = XNB[b]
            bank = P.nextbank()
            pview = ps[bank][:].bitcast(BF16)

            def tr(e):
                for k in range(8):
                    ins = e.transpose(pview[:, k * 128:(k + 1) * 128], xnb[:, k * 128:(k + 1) * 128], ident[:])
                return ins

            P.add("pe", tr, reads=[("xn", b), ("ident",)], writes=[("ps", bank)])
            c0 = col_of_tile(i)
            src = pview.rearrange("p (k c) -> p k c", k=8)
            dst = XT[:, :, c0:c0 + 128]
            if evac_eng == "act":
                P.add("act", lambda e: e.copy(out=dst, in_=src), reads=[("ps", bank)], writes=[xk(c0 // 128)])
            else:
                P.add("dve", lambda e: e.tensor_copy(out=dst, in_=src), reads=[("ps", bank)], writes=[xk(c0 // 128)])

        if len(XNB) == 1:
            return [stats] + [(lambda i=i: (mult(i), trn(i))) for i in tiles]
        pieces = [stats, (lambda: mult(tiles[0]))]
        for j in range(1, len(tiles)):
            pieces.append(lambda j=j: (mult(tiles[j]), trn(tiles[j - 1])))
        pieces.append(lambda: trn(tiles[-1]))
        return pieces

    def norm_tiles(*a, **kw):
        for p_ in norm_pieces(*a, **kw):
            p_()

    for a in range(4):
        dma("sp", H[:, 4 * a:4 * a + 4, :], x_d[512 * a:512 * (a + 1), :].rearrange("(i p) d -> p i d", p=128),
            writes=[("H", 4 * a + j) for j in range(4)])
    dma("sp", ident[:], ident_d, writes=[("ident",)])
    dma("sp", triA[:], triA_d, writes=[("triA",)])
    dma("sp", triB[:], triB_d, writes=[("triB",)])
    dma("sp", maskb[:], maskb_d, writes=[("maskb",)])
    dma("sp", bgate[:], bg_d, writes=[("bgate",)])
    dma("sp", convw[:], cw_d, writes=[("convw",)])

    def ffn_phase(wgu_d, wd_d, gidx, tag, fin=False, extra=None):
        extra = list(extra or [])
        per_group = (len(extra) + 11) // 12
        EM.flush()
        st = contextlib.ExitStack()
        if fin:
            gb2 = salloc(st, "gb2", [128, D], F32)
            ob = [salloc(st, f"obf{i}", [128, D], F32) for i in range(2)]
            ss2 = salloc(st, "ss2", [128, NT], F32)
            rstd2 = salloc(st, "rstd2", [128, NT], F32)
            dma("sp", gb2[:], gv_d[5], writes=[("gb2",)])
            epsT = salloc(st, "epsT", [128, 1], F32)
            P.add("dve", lambda e: e.memset(epsT[:], EPS), writes=[("epsT",)])

            def fin_stats(t):
                P.add("act", lambda e: e.activation(out=XNB[0][:], in_=H[:, t, :], func=AF.Square, accum_out=ss2[:, t:t + 1]),
                      reads=[("H", t)], writes=[("ss2", t), ("xn", 0)])
                P.add("act", lambda e: e.activation(out=rstd2[:, t:t + 1], in_=ss2[:, t:t + 1], func=AF.Ln, scale=1.0 / D,
                                                    bias=epsT[:, 0:1]),
                      reads=[("ss2", t), ("epsT",)], writes=[("rstd2", t)])
                P.add("act", lambda e: e.activation(out=rstd2[:, t:t + 1], in_=rstd2[:, t:t + 1], func=AF.Exp, scale=-0.5),
                      reads=[("rstd2", t)], writes=[("rstd2", t)])

            def fin_out(t):
                b = t % 2
                P.add("dve", lambda e: e.scalar_tensor_tensor(out=ob[b][:], in0=H[:, t, :], scalar=rstd2[:, t:t + 1], in1=gb2[:],
                                                              op0=ALU.mult, op1=ALU.mult),
                      reads=[("H", t), ("rstd2", t), ("gb2",)], writes=[("obf", b)])
                dma("sp", y_d[t * 128:(t + 1) * 128, :], ob[b][:], reads=[("obf", b)], writes=[("y", t)])

            def final_tiles(tiles):
                lo, hi = min(tiles), max(tiles) + 1
                for i in tiles:
                    P.add("act", lambda e, i=i: e.activation(out=XNB[0][:], in_=H[:, i, :], func=AF.Square,
                                                             accum_out=ss2[:, i:i + 1]),
                          reads=[("H", i)], writes=[("ss2", i), ("xn", 0)])
                P.add("dve", lambda e: e.tensor_scalar(out=rstd2[:, lo:hi], in0=ss2[:, lo:hi], scalar1=1.0 / D, scalar2=EPS,
                                                       op0=ALU.mult, op1=ALU.add),
                      reads=[("ss2", i) for i in tiles], writes=[("rstd2", i) for i in tiles])
                P.add("act", lambda e: e.activation(out=rstd2[:, lo:hi], in_=rstd2[:, lo:hi], func=AF.Ln),
                      reads=[("rstd2", i) for i in tiles], writes=[("rstd2", i) for i in tiles])
                P.add("act", lambda e: e.activation(out=rstd2[:, lo:hi], in_=rstd2[:, lo:hi], func=AF.Exp, scale=-0.5),
                      reads=[("rstd2", i) for i in tiles], writes=[("rstd2", i) for i in tiles])
                for i in tiles:
                    b = i % 2
                    P.add("dve", lambda e, i=i, b=b: e.scalar_tensor_tensor(out=ob[b][:], in0=H[:, i, :],
                                                                            scalar=rstd2[:, i:i + 1], in1=gb2[:],
                                                                            op0=ALU.mult, op1=ALU.mult),
                          reads=[("H", i), ("rstd2", i), ("gb2",)], writes=[("obf", b)])
                    dma("sp", y_d[i * 128:(i + 1) * 128, :], ob[b][:], reads=[("obf", b)], writes=[("y", i)])
        XNB[:] = [xn0, salloc(st, f"xn1_{tag}", [128, D], BF16)]
        XTs = [salloc(st, f"XT{i}_{tag}", [128, 8, 1024], BF16) for i in range(2)]
        actT = salloc(st, f"actT_{tag}", [128, 22, 1024], BF16)
        wgu = [salloc(st, f"wgu{i}_{tag}", [128, 8, 2, 512], BF16) for i in range(2)]
        NWD = 4
        wd = [salloc(st, f"wd{i}_{tag}", [128, 2, 512], BF16) for i in range(NWD)]
        sg = [salloc(st, f"sg{i}_{tag}", [128, 512], F32) for i in range(2)]
        load_gb(gidx)
        gcnt = 0
        dcnt = 0
        scnt = 0
        nfill = []
        for hf in range(2):
            XT = XTs[hf]
            if hf == 0:
                norm_tiles(list(range(0, 8)), XT, lambda i: i * 128, xkey=("XT", 0))
                nfill = norm_pieces(list(range(8, 16)), XTs[1], lambda i: (i - 8) * 128, xkey=("XT", 1))
            if fin and hf == 1:
                final_tiles(list(range(0, 8)))
            for gi in range(6):
                ncols = 512 if gi < 5 else 256
                slot = gcnt % 2
                gcnt += 1
                for gu in range(2):
                    c0 = gu * DFF + gi * 512
                    dma("pool", wgu[slot][:, :, gu, 0:ncols],
                        wgu_d[:, c0:c0 + ncols].rearrange("(k p) n -> p k n", p=128),
                        writes=[("wgu", slot, gu)])
                if gcnt > 2:
                    for _ in range(2):
                        if extra:
                            extra.pop(0)()
                for jj in range(ncols // 128):
                    j = gi * 4 + jj
                    for nh in range(2):
                        banks = (P.nextbank(), P.nextbank())
                        for gu in range(2):
                            def mm(e, slot=slot, gu=gu, jj=jj, nh=nh, bank=banks[gu], XT=XT):
                                for k in range(8):
                                    ins = e.matmul(ps[bank][:], lhsT=wgu[slot][:, k, gu, jj * 128:(jj + 1) * 128],
                                                   rhs=XT[:, k, nh * 512:(nh + 1) * 512], start=(k == 0), stop=(k == 7))
                                return ins
                            P.add("pe", mm, reads=[("wgu", slot, gu)] + [("XT", hf, nh * 4 + c) for c in range(4)],
                                  writes=[("ps", banks[gu])])
                        s_ = scnt % 2
                        scnt += 1
                        P.add("act", lambda e, s_=s_, b=banks[0]: e.activation(out=sg[s_][:], in_=ps[b][:], func=AF.Silu),
                              reads=[("ps", banks[0])], writes=[("sg", s_)])
                        P.add("dve", lambda e, s_=s_, b=banks[1], j=j, nh=nh: e.tensor_tensor(
                            out=actT[:, j, nh * 512:(nh + 1) * 512], in0=sg[s_][:], in1=ps[b][:], op=ALU.mult),
                            reads=[("sg", s_), ("ps", banks[1])], writes=[("actT", j, nh)])
                    if hf == 0 and j >= 2 and nfill:
                        nfill.pop(0)()
            while hf == 0 and nfill:
                nfill.pop(0)()
            for dh in range(2):
                for kg in range(11):
                    slot = dcnt % NWD
                    dcnt += 1
                    dma("pool", wd[slot][:],
                        wd_d[kg * 256:(kg + 1) * 256, dh * 512:(dh + 1) * 512].rearrange("(k p) n -> p k n", p=128),
                        writes=[("wd", slot)])
                    if extra:
                        extra.pop(0)()
                    for ts in range(8):
                        def mm(e, slot=slot, kg=kg, ts=ts):
                            for kk in range(2):
                                ins = e.matmul(ps[ts][:], lhsT=actT[:, 2 * kg + kk, ts * 128:(ts + 1) * 128],
                                               rhs=wd[slot][:, kk, :], start=(kg == 0 and kk == 0),
                                               stop=(kg == 10 and kk == 1))
                            return ins
                        P.add("pe", mm, reads=[("wd", slot), ("actT", 2 * kg, ts // 4), ("actT", 2 * kg + 1, ts // 4)],
                              writes=[("ps", ts)])
                for ts in range(8):
                    t = 8 * hf + ts
                    P.add("dve", lambda e, ts=ts, t=t, dh=dh: e.scalar_tensor_tensor(
                        out=H[:, t, dh * 512:(dh + 1) * 512], in0=ps[ts][:], scalar=0.5,
                        in1=H[:, t, dh * 512:(dh + 1) * 512], op0=ALU.mult, op1=ALU.add),
                        reads=[("ps", ts), ("H", t)], writes=[("H", t)])
                    if fin and hf == 1 and dh == 1:
                        fin_stats(t)
                        if ts > 0:
                            fin_out(t - 1)
        if fin:
            fin_out(15)
            P.add("sp", None, reads=[("y", i) for i in range(NT)], writes=[("done",)])
        EM.flush()
        st.close()
        XNB[:] = [xn0]

    if stages >= 1:
        ffn_phase(w1gu_d, w1d_d, 0, "f1", extra=conv_dmas if stages >= 2 else None)


    def mixer_phase():
        EM.flush()
        st = contextlib.ExitStack()
        kT = salloc(st, "kT", [128, 8, S], BF16)
        V = salloc(st, "V", [128, NT, D], BF16)
        XTs = [salloc(st, f"XTm{i}", [128, 8, 512], BF16) for i in range(2)]
        qT = salloc(st, "qTm", [128, 8, 512], BF16)
        mT = salloc(st, "mTm", [128, 8, 512], BF16)
        ysT = qT
        ycT = salloc(st, "ycT", [128, 8, 512], BF16)
        wb = [salloc(st, f"wbm{i}", [128, 8, 512], BF16) for i in range(2)]
        ebuf = [salloc(st, f"e{i}", [128, 512], F32) for i in range(2)]
        ebuf += [mT[:, 2 * j:2 * j + 2, :].rearrange("p a b -> p (a b)").bitcast(F32) for j in range(2)]
        ekeys = {0: [("e", 0)], 1: [("e", 1)], 2: [("mT", 0), ("mT", 1)], 3: [("mT", 2), ("mT", 3)]}
        spb = [salloc(st, f"sp{i}", [128, 512], BF16) for i in range(2)]
        E1 = [salloc(st, f"E1{i}", [128, 512], F32) for i in range(2)]
        Ab = [salloc(st, f"A{i}", [128, 512], BF16) for i in range(2)]
        pbuf = salloc(st, "pbuf", [128, 514], F32)
        phalo = salloc(st, "phalo", [128, 8, 2], F32)
        tmpa = salloc(st, "tmpa", [128, 512], F32)
        load_gb(1)
        P.add("dve", lambda e: e.memset(phalo[:].rearrange("p a b -> p (a b)"), 0.0), writes=[("phalo", c) for c in range(8)])
        wcnt = [0]
        SCALE = float(128 ** -0.5)

        def wload(key):
            b, parts = BLK[key]
            ncols = max(c0 + n for (_, c0, n) in parts)
            slot = wcnt[0] % 2
            wcnt[0] += 1
            dma("sp", wb[slot][:, :, 0:ncols], wsc[b][:, :, 0:ncols],
                reads=[("wsc", b, j) for j in range(ncols // 128)],
                writes=[("wb", slot, j) for j in range(ncols // 128)])
            return slot

        def proj_fm(slot, cb, dst_fn, rkeys, XTsrc, xk):
            bank = P.nextbank()

            def mm(e):
                for k in range(8):
                    ins = e.matmul(ps[bank][:], lhsT=wb[slot][:, k, cb * 128:(cb + 1) * 128], rhs=XTsrc[:, k, :],
                                   start=(k == 0), stop=(k == 7))
                return ins
            P.add("pe", mm, reads=[("wb", slot, cb)] + rkeys, writes=[("ps", bank)])
            return bank

        def xkeys_of(T):
            return [("XTm", T % 2, c) for c in range(4)]

        def proj_fm_g(slot, cb, rkeys, XTsrc):
            bank = P.nextbank()
            P.reserved.add(bank)
            for kk in range(4):
                def mm(e, kk=kk):
                    for k in (2 * kk, 2 * kk + 1):
                        ins = e.matmul(ps[bank][:], lhsT=wb[slot][:, k, cb * 128:(cb + 1) * 128], rhs=XTsrc[:, k, :],
                                       start=(k == 0), stop=(k == 7))
                    return ins
                P.add("pe", mm, reads=[("wb", slot, cb)] + rkeys, writes=[("ps", bank)])
                if kk < 3:
                    yield "pe"
            return bank

        def norm_gen(T):
            XT = XTs[T % 2]
            tiles = list(range(4 * T, 4 * T + 4))
            lo, hi = tiles[0], tiles[-1] + 1
            for i in tiles:
                P.add("act", lambda e, i=i: e.activation(out=xn0[:], in_=H[:, i, :], func=AF.Square, accum_out=ss[:, i:i + 1]),
                      reads=[("H", i)], writes=[("ss", i), ("xn", 0)])
            P.add("dve", lambda e: e.tensor_scalar(out=rstd[:, lo:hi], in0=ss[:, lo:hi], scalar1=1.0 / D, scalar2=EPS,
                                                   op0=ALU.mult, op1=ALU.add),
                  reads=[("ss", i) for i in tiles], writes=[("rstd",)])
            P.add("act", lambda e: e.activation(out=rstd[:, lo:hi], in_=rstd[:, lo:hi], func=AF.Ln),
                  reads=[("rstd",)], writes=[("rstd",)])
            P.add("act", lambda e: e.activation(out=rstd[:, lo:hi], in_=rstd[:, lo:hi], func=AF.Exp, scale=-0.5),
                  reads=[("rstd",)], writes=[("rstd",)])
            yield "dve"
            for i in tiles:
                P.add("dve", lambda e, i=i: e.scalar_tensor_tensor(out=xn0[:], in0=H[:, i, :], scalar=rstd[:, i:i + 1], in1=gb[:],
                                                                   op0=ALU.mult, op1=ALU.mult),
                      reads=[("H", i), ("rstd",), ("gb",)], writes=[("xn", 0)])
                bank = P.nextbank()
                P.reserved.add(bank)
                pview = ps[bank][:].bitcast(BF16)
                for kk in range(4):
                    def tr(e, kk=kk, pview=pview):
                        for k in (2 * kk, 2 * kk + 1):
                            ins = e.transpose(pview[:, k * 128:(k + 1) * 128], xn0[:, k * 128:(k + 1) * 128], ident[:])
                        return ins
                    P.add("pe", tr, reads=[("xn", 0), ("ident",)], writes=[("ps", bank)])
                    if kk < 3:
                        yield "pe"
                yield "dve"
                c0 = (i - 4 * T) * 128
                P.add("dve", lambda e, c0=c0, pview=pview: e.tensor_copy(out=XT[:, :, c0:c0 + 128],
                                                                        in_=pview.rearrange("p (k c) -> p k c", k=8)),
                      reads=[("ps", bank)], writes=[("XTm", T % 2, c0 // 128)])
                P.reserved.discard(bank)
                yield "dve"

        def k_gen(T):
            XT, xkeys = XTs[T % 2], xkeys_of(T)
            for blk in range(2):
                slot = wload(("k", blk))
                for cb in range(4):
                    fc = blk * 4 + cb
                    bank = yield from proj_fm_g(slot, cb, xkeys, XT)
                    yield "dve"
                    P.add("dve", lambda e, fc=fc, bank=bank: e.tensor_copy(out=kT[:, fc, T * 512:(T + 1) * 512], in_=ps[bank][:]),
                          reads=[("ps", bank)], writes=[("kT", fc, T)])
                    P.reserved.discard(bank)
                    yield "pe"

        def v_gen(T):
            XT = XTs[T % 2]
            for blk in range(2):
                slot = wload(("v", blk))
                for ts in range(4):
                    bank = P.nextbank()
                    P.reserved.add(bank)
                    for kk in range(4):
                        def mm(e, kk=kk, slot=slot, ts=ts, bank=bank):
                            for k in (2 * kk, 2 * kk + 1):
                                ins = e.matmul(ps[bank][:], lhsT=XT[:, k, ts * 128:(ts + 1) * 128], rhs=wb[slot][:, k, :],
                                               start=(k == 0), stop=(k == 7))
                            return ins
                        P.add("pe", mm, reads=[("wb", slot, j) for j in range(4)] + [("XTm", T % 2, ts)], writes=[("ps", bank)])
                        if kk < 3:
                            yield "pe"
                    yield "dve"
                    P.add("dve", lambda e, bank=bank, ts=ts, blk=blk: e.tensor_copy(
                        out=V[:, 4 * T + ts, blk * 512:(blk + 1) * 512], in_=ps[bank][:]),
                        reads=[("ps", bank)], writes=[("V", 4 * T + ts, blk)])
                    P.reserved.discard(bank)
                    yield "pe"

        def drain(gens):
            for g in list(gens):
                for _ in g:
                    pass
            del gens[:]

        drain([norm_gen(0), k_gen(0), v_gen(0)])
        for T in range(4):
            XT, xkeys = XTs[T % 2], xkeys_of(T)
            for blk in range(2):
                slot = wload(("q", blk))
                for cb in range(4):
                    fc = blk * 4 + cb
                    bank = proj_fm(slot, cb, None, xkeys, XT, "XTm")
                    P.add("act", lambda e, fc=fc, bank=bank: e.activation(out=qT[:, fc, :], in_=ps[bank][:], func=AF.Copy,
                                                                         scale=SCALE),
                          reads=[("ps", bank)], writes=[("qT", fc)])
            def conv_gen(c, XT=XT, xkeys=xkeys):
                slot = wload(("conv", c))
                b_cc = yield from proj_fm_g(slot, 1, xkeys, XT)
                yield "dve"
                P.add("dve", lambda e: e.tensor_copy(out=tmpa[:], in_=ps[b_cc][:]), reads=[("ps", b_cc)], writes=[("tmpa",)])
                P.reserved.discard(b_cc)
                yield "pe"
                b_cx = yield from proj_fm_g(slot, 2, xkeys, XT)
                yield "dve"
                P.add("dve", lambda e: e.tensor_copy(out=pbuf[:, 0:2], in_=phalo[:, c, :]),
                      reads=[("phalo", c)], writes=[("pbuf",)])
                P.add("dve", lambda e: e.tensor_tensor(out=pbuf[:, 2:514], in0=tmpa[:], in1=ps[b_cx][:], op=ALU.mult),
                      reads=[("tmpa",), ("ps", b_cx), ("pbuf",)], writes=[("pbuf",)])
                P.reserved.discard(b_cx)
                P.add("dve", lambda e: e.tensor_copy(out=phalo[:, c, :], in_=pbuf[:, 512:514]),
                      reads=[("pbuf",)], writes=[("phalo", c)])
                yield "dve"
                P.add("dve", lambda e: e.tensor_scalar(out=tmpa[:], in0=pbuf[:, 2:514], scalar1=convw[:, c, 2:3], scalar2=None,
                                                       op0=ALU.mult),
                      reads=[("pbuf",), ("convw",)], writes=[("tmpa",)])
                P.add("dve", lambda e: e.scalar_tensor_tensor(out=tmpa[:], in0=pbuf[:, 1:513], scalar=convw[:, c, 1:2],
                                                              in1=tmpa[:], op0=ALU.mult, op1=ALU.add),
                      reads=[("pbuf",), ("tmpa",), ("convw",)], writes=[("tmpa",)])
                yield "dve"
                P.add("dve", lambda e: e.scalar_tensor_tensor(out=tmpa[:], in0=pbuf[:, 0:512], scalar=convw[:, c, 0:1],
                                                              in1=tmpa[:], op0=ALU.mult, op1=ALU.add),
                      reads=[("pbuf",), ("tmpa",), ("convw",)], writes=[("tmpa",)])
                yield "pe"
                b_cb = yield from proj_fm_g(slot, 0, xkeys, XT)
                yield "dve"
                P.add("dve", lambda e: e.tensor_tensor(out=ycT[:, c, :], in0=tmpa[:], in1=ps[b_cb][:], op=ALU.mult),
                      reads=[("tmpa",), ("ps", b_cb)], writes=[("ycT", c)])
                P.reserved.discard(b_cb)

            gens = [[conv_gen(c), "pe"] for c in range(8)]
            nmicro = 8 * 17
            if T < 3:
                gens += [[norm_gen(T + 1), "dve"], [k_gen(T + 1), "pe"], [v_gen(T + 1), "pe"]]
                nmicro += 21 + 40 + 40
            nch = 4 * T + 4
            npoints = 3 * 4 * nch
            fstate = [0, 0]

            def pump_n(n, allow_dve):
                while n > 0 and gens:
                    g, tag = gens[0]
                    if tag == "dve" and not allow_dve:
                        break
                    try:
                        gens[0][1] = next(g)
                        n -= 1
                        fstate[1] += 1
                    except StopIteration:
                        gens.pop(0)

            def step_done(allow_dve=False):
                fstate[0] += 1
                target = (nmicro * fstate[0] + npoints - 1) // npoints
                pump_n(target - fstate[1], allow_dve)

            tcnt = [0]
            zb = {}
            for hp in range(4):
                heads = (2 * hp, 2 * hp + 1)
                Rb = [P.nextbank(), P.nextbank()]
                P.reserved.update(Rb)
                Ob = [P.nextbank(), P.nextbank()]
                P.reserved.update(Ob)

                def emit_z(h, c):
                    bank = P.nextbank()
                    P.reserved.add(bank)
                    dd = c - 4 * T
                    c0 = max(dd, 0) * 128

                    def mm(e, h=h, c=c, bank=bank, dd=dd, c0=c0):
                        ins = e.matmul(ps[bank][:, c0:512], lhsT=kT[:, h, c * 128:(c + 1) * 128], rhs=qT[:, h, c0:512],
                                       start=True, stop=(dd < 0))
                        if dd >= 0:
                            ins = e.matmul(ps[bank][:, c0:c0 + 128], lhsT=ident[:], rhs=maskb[:], start=False, stop=True)
                        return ins
                    P.add("pe", mm, reads=[("kT", h, c // 4), ("qT", h), ("ident",), ("maskb",)], writes=[("ps", bank)])
                    zb[(h, c)] = bank

                def split_mm(e, out_bank, lhsT, rhs_buf, dd, c0, last):
                    if dd >= 0:
                        ins = e.matmul(ps[out_bank][:, c0:c0 + 128], lhsT=lhsT, rhs=rhs_buf[:, c0:c0 + 128], start=(dd == 3),
                                       stop=last, skip_group_check=True)
                        if dd < 3:
                            ins = e.matmul(ps[out_bank][:, c0 + 128:512], lhsT=lhsT, rhs=rhs_buf[:, c0 + 128:512],
                                           start=False, stop=last, skip_group_check=True)
                    else:
                        ins = e.matmul(ps[out_bank][:], lhsT=lhsT, rhs=rhs_buf[:], start=False, stop=last, skip_group_check=True)
                    return ins

                for h in heads:
                    if (h, nch - 1) not in zb:
                        emit_z(h, nch - 1)
                for c in range(nch - 1, -1, -1):
                    dd = c - 4 * T
                    c0 = max(dd, 0) * 128
                    slots = {}
                    eslot = {}
                    for i, h in enumerate(heads):
                        s_ = i
                        se = 2 * (tcnt[0] % 2) + i
                        slots[h] = s_
                        eslot[h] = se
                        bank = zb[(h, c)]
                        P.add("act", lambda e, se=se, bank=bank, c0=c0: e.activation(out=ebuf[se][:, c0:512], in_=ps[bank][:, c0:512],
                                                                                     func=AF.Exp),
                              reads=[("ps", bank)], writes=ekeys[se])
                        P.reserved.discard(bank)
                    tcnt[0] += 1
                    for h in heads:
                        s_ = slots[h]
                        se = eslot[h]
                        P.add("act", lambda e, s_=s_, se=se, c0=c0: e.activation(out=spb[s_][:, c0:512], in_=ebuf[se][:, c0:512],
                                                                                 func=AF.Ln, bias=1.0),
                              reads=ekeys[se], writes=[("sp", s_)])
                    for i, h in enumerate(heads):
                        s_ = slots[h]
                        P.add("pe", lambda e, s_=s_, rb=Rb[i], dd=dd, c0=c0: split_mm(e, rb, triA[:], spb[s_], dd, c0, True),
                              reads=[("triA",), ("sp", s_)], writes=[("ps", Rb[i])])
                    step_done(True)
                    if c > 0:
                        for h in heads:
                            emit_z(h, c - 1)
                    elif hp < 3:
                        for h2 in (2 * hp + 2, 2 * hp + 3):
                            emit_z(h2, nch - 1)
                    step_done()
                    for i, h in enumerate(heads):
                        s_ = slots[h]
                        P.add("act", lambda e, s_=s_, rb=Rb[i], c0=c0: e.activation(out=E1[s_][:, c0:512], in_=ps[rb][:, c0:512],
                                                                                    func=AF.Exp),
                              reads=[("ps", Rb[i])], writes=[("E1", s_)])
                        if c > 0:
                            P.add("pe", lambda e, s_=s_, rb=Rb[i], c0=c0: e.matmul(ps[rb][:, c0:512], lhsT=triB[:], rhs=spb[s_][:, c0:512],
                                                                                start=False, stop=True, skip_group_check=True),
                                  reads=[("triB",), ("sp", s_)], writes=[("ps", Rb[i])])
                        se = eslot[h]
                        P.add("dve", lambda e, s_=s_, se=se, c0=c0: e.tensor_tensor(out=Ab[s_][:, c0:512], in0=ebuf[se][:, c0:512],
                                                                                    in1=E1[s_][:, c0:512], op=ALU.mult),
                              reads=ekeys[se] + [("E1", s_)], writes=[("A", s_)])
                        P.add("pe", lambda e, s_=s_, ob=Ob[i], h=h, c=c, dd=dd, c0=c0: split_mm(
                            e, ob, V[:, c, h * 128:(h + 1) * 128], Ab[s_], dd, c0, (c == 0)),
                            reads=[("V", c, h // 4), ("A", s_)], writes=[("ps", Ob[i])])
                    step_done(True)
                for i, h in enumerate(heads):
                    P.add("dve", lambda e, h=h, ob=Ob[i]: e.tensor_copy(out=ysT[:, h, :], in_=ps[ob][:]),
                          reads=[("ps", Ob[i])], writes=[("qT", h)])
                P.reserved.difference_update(Rb)
                P.reserved.difference_update(Ob)
            drain([g for g, _ in gens])
            del gens[:]
            for fc in range(8):
                slot = wload(("op", fc))
                bA = proj_fm(slot, 0, None, [("ycT", k) for k in range(8)], ycT, "ycT")
                bB = proj_fm(slot, 1, None, [("qT", k) for k in range(8)], ysT, "ysT")
                bGc = proj_fm(slot, 2, None, xkeys, XT, "XTm")
                bGs = proj_fm(slot, 3, None, xkeys, XT, "XTm")
                P.add("act", lambda e, b=bGc, fc=fc: e.activation(out=tmpa[:], in_=ps[b][:], func=AF.Sigmoid,
                                                                 bias=bgate[:, fc:fc + 1]),
                      reads=[("ps", bGc), ("bgate",)], writes=[("tmpa",)])
                P.add("dve", lambda e, b=bA: e.tensor_tensor(out=tmpa[:], in0=tmpa[:], in1=ps[b][:], op=ALU.mult),
                      reads=[("tmpa",), ("ps", bA)], writes=[("tmpa",)])
                P.add("act", lambda e, b=bGs, fc=fc: e.activation(out=pbuf[:, 0:512], in_=ps[b][:], func=AF.Sigmoid,
                                                                 bias=bgate[:, 8 + fc:9 + fc]),
                      reads=[("ps", bGs), ("bgate",)], writes=[("pbuf",)])
                P.add("dve", lambda e, b=bB: e.tensor_tensor(out=pbuf[:, 0:512], in0=pbuf[:, 0:512], in1=ps[b][:], op=ALU.mult),
                      reads=[("pbuf",), ("ps", bB)], writes=[("pbuf",)])
                P.add("dve", lambda e, fc=fc: e.tensor_tensor(out=mT[:, fc, :], in0=tmpa[:], in1=pbuf[:, 0:512], op=ALU.add),
                      reads=[("tmpa",), ("pbuf",)], writes=[("mT", fc)])
            for dh in range(2):
                slot = wload(("wo", dh))
                for ts in range(4):
                    bank = P.nextbank()

                    def mm(e, slot=slot, ts=ts, bank=bank):
                        for k in range(8):
                            ins = e.matmul(ps[bank][:], lhsT=mT[:, k, ts * 128:(ts + 1) * 128], rhs=wb[slot][:, k, :],
                                           start=(k == 0), stop=(k == 7))
                        return ins
                    P.add("pe", mm, reads=[("wb", slot, j) for j in range(4)] + [("mT", k) for k in range(8)],
                          writes=[("ps", bank)])
                    t = 4 * T + ts
                    P.add("dve", lambda e, bank=bank, t=t, dh=dh: e.tensor_tensor(
                        out=H[:, t, dh * 512:(dh + 1) * 512], in0=ps[bank][:], in1=H[:, t, dh * 512:(dh + 1) * 512], op=ALU.add),
                        reads=[("ps", bank), ("H", t)], writes=[("H", t)])
        EM.flush()
        st.close()
        XNB[:] = [xn0]

    def cross_phase():
        EM.flush()
        st = contextlib.ExitStack()
        XNB[:] = [xn0, salloc(st, "xn1_c", [128, D], BF16)]
        memt = salloc(st, "memt", [128, 2, D], F32)
        mnT = salloc(st, "mnT", [128, 8, 256], BF16)
        kcT = salloc(st, "kcT", [128, 8, 256], BF16)
        Vc = salloc(st, "Vc", [128, 2, D], BF16)
        XT = [salloc(st, f"XTc{i}", [128, 8, 512], BF16) for i in range(2)]
        qcT = [salloc(st, f"qcT{i}", [128, 8, 512], BF16) for i in range(2)]
        oT = [salloc(st, f"oT{i}", [128, 8, 512], BF16) for i in range(2)]
        wq = [salloc(st, f"wq{i}", [128, 8, 512], BF16) for i in range(2)]
        wo = [salloc(st, f"wo{i}", [128, 8, 512], BF16) for i in range(2)]
        wkv = [salloc(st, f"wkv{i}", [128, 8, 512], BF16) for i in range(2)]
        Pf = [salloc(st, f"Pf{i}", [128, 4, 256], F32) for i in range(2)]
        Pn = [salloc(st, f"Pn{i}", [128, 4, 256], BF16) for i in range(2)]
        PnT = [salloc(st, f"PnT{i}", [128, 8, 128], BF16) for i in range(2)]
        mx = [salloc(st, f"mx{i}", [128, 4], F32) for i in range(2)]
        sm = [salloc(st, f"sm{i}", [128, 4], F32) for i in range(2)]
        rs = [salloc(st, f"rs{i}", [128, 4], F32) for i in range(2)]

        dma("sp", memt[:], mem_d.rearrange("(i p) d -> p i d", p=128), writes=[("memt", 0), ("memt", 1)])
        load_gb(3)
        for blk in range(2):
            dma("pool", wkv[blk][:], wckv_d[:, blk * 512:(blk + 1) * 512].rearrange("(k p) n -> p k n", p=128),
                writes=[("wkv", blk)])
        norm_tiles([0, 1], mnT, lambda i: i * 128, SRC=memt, skey="memt", xkey="mnT")
        load_gb(2)
        for blk in range(2):
            for cb in range(4):
                fc = blk * 4 + cb
                bank = P.nextbank()

                def mm(e, blk=blk, cb=cb, bank=bank):
                    for k in range(8):
                        ins = e.matmul(ps[bank][:, 0:256], lhsT=wkv[blk][:, k, cb * 128:(cb + 1) * 128], rhs=mnT[:, k, :],
                                       start=(k == 0), stop=(k == 7))
                    return ins
                P.add("pe", mm, reads=[("wkv", blk), ("mnT", 0), ("mnT", 1)], writes=[("ps", bank)])
                P.add("act", lambda e, fc=fc, bank=bank: e.copy(out=kcT[:, fc, :], in_=ps[bank][:, 0:256]),
                      reads=[("ps", bank)], writes=[("kcT", fc)])
        for blk in range(2):
            dma("pool", wkv[blk][:], wckv_d[:, 1024 + blk * 512:1024 + (blk + 1) * 512].rearrange("(k p) n -> p k n", p=128),
                writes=[("wkv", blk)])
        for blk in range(2):
            dma("pool", wq[blk][:], wcq_d[:, blk * 512:(blk + 1) * 512].rearrange("(k p) n -> p k n", p=128),
                writes=[("wq", blk)])
        for blk in range(2):
            dma("pool", wo[blk][:], wcout_d[:, blk * 512:(blk + 1) * 512].rearrange("(k p) n -> p k n", p=128),
                writes=[("wo", blk)])
        for blk in range(2):
            for mc in range(2):
                bank = P.nextbank()

                def mm(e, blk=blk, mc=mc, bank=bank):
                    for k in range(8):
                        ins = e.matmul(ps[bank][:], lhsT=mnT[:, k, mc * 128:(mc + 1) * 128], rhs=wkv[blk][:, k, :],
                                       start=(k == 0), stop=(k == 7))
                    return ins
                P.add("pe", mm, reads=[("wkv", blk), ("mnT", mc)], writes=[("ps", bank)])
                P.add("dve", lambda e, mc=mc, blk=blk, bank=bank: e.tensor_copy(out=Vc[:, mc, blk * 512:(blk + 1) * 512],
                                                                                in_=ps[bank][:]),
                      reads=[("ps", bank)], writes=[("Vc", mc, blk)])

        def qproj(T, fc):
            par = T % 2
            blk, cb = fc // 4, fc % 4
            bank = P.nextbank()

            def mm(e):
                for k in range(8):
                    ins = e.matmul(ps[bank][:], lhsT=wq[blk][:, k, cb * 128:(cb + 1) * 128], rhs=XT[par][:, k, :],
                                   start=(k == 0), stop=(k == 7))
                return ins
            P.add("pe", mm, reads=[("wq", blk)] + [("XTc", par, c) for c in range(4)], writes=[("ps", bank)])
            P.add("act", lambda e: e.activation(out=qcT[par][:, fc, :], in_=ps[bank][:], func=AF.Copy, scale=1.0 / 16.0),
                  reads=[("ps", bank)], writes=[("qcT", par, fc)])

        def outproj(T, ts):
            par = T % 2
            for dh in range(2):
                bank = P.nextbank()

                def mm(e, dh=dh, bank=bank):
                    for k in range(8):
                        ins = e.matmul(ps[bank][:], lhsT=oT[par][:, k, ts * 128:(ts + 1) * 128], rhs=wo[dh][:, k, :],
                                       start=(k == 0), stop=(k == 7))
                    return ins
                P.add("pe", mm, reads=[("wo", dh), ("oT", par, ts, 0), ("oT", par, ts, 1)], writes=[("ps", bank)])
                t = 4 * T + ts
                P.add("dve", lambda e, bank=bank, t=t, dh=dh: e.tensor_tensor(
                    out=H[:, t, dh * 512:(dh + 1) * 512], in0=ps[bank][:], in1=H[:, t, dh * 512:(dh + 1) * 512], op=ALU.add),
                    reads=[("ps", bank), ("H", t)], writes=[("H", t)])

        def cnorm(T):
            par = T % 2
            norm_tiles(list(range(4 * T, 4 * T + 4)), XT[par], lambda i: (i - 4 * T) * 128, xkey=("XTc", par), evac_eng="dve")

        def A(n):
            T, ts = divmod(n, 4)
            par = T % 2
            sbk = [P.nextbank(), P.nextbank()]
            P.reserved.update(sbk)

            def mm(e):
                for h in range(4):
                    for c in range(2):
                        ins = e.matmul(ps[sbk[h // 2]][:, (h % 2) * 256:(h % 2 + 1) * 256],
                                       lhsT=qcT[par][:, 2 * h + c, ts * 128:(ts + 1) * 128],
                                       rhs=kcT[:, 2 * h + c, :], start=(c == 0), stop=(c == 1))
                return ins
            P.add("pe", mm, reads=[("qcT", par, k) for k in range(8)] + [("kcT", k) for k in range(8)],
                  writes=[("ps", sbk[0]), ("ps", sbk[1])])
            return sbk

        def B(n, sbk):
            b_ = n % 2
            for i in range(2):
                pv = ps[sbk[i]][:].rearrange("p (h m) -> p h m", h=2)
                P.add("dve", lambda e, pv=pv, i=i: e.tensor_reduce(out=mx[b_][:, 2 * i:2 * i + 2], in_=pv, axis=AX.X, op=ALU.max),
                      reads=[("ps", sbk[i])], writes=[("mx", b_, i)])
            P.add("dve", lambda e: e.tensor_scalar(out=mx[b_][:], in0=mx[b_][:], scalar1=-1.0, scalar2=None, op0=ALU.mult),
                  reads=[("mx", b_, 0), ("mx", b_, 1)], writes=[("mx", b_, 0), ("mx", b_, 1)])
            for h in range(4):
                P.add("act", lambda e, h=h: e.activation(
                    out=Pf[b_][:, h, :], in_=ps[sbk[h // 2]][:, (h % 2) * 256:(h % 2 + 1) * 256],
                    func=AF.Exp, bias=mx[b_][:, h:h + 1], accum_out=sm[b_][:, h:h + 1]),
                    reads=[("ps", sbk[h // 2]), ("mx", b_, h // 2)], writes=[("Pf", b_, h), ("sm", b_, h)])
            P.reserved.difference_update(sbk)
            P.add("dve", lambda e: e.reciprocal(out=rs[b_][:], in_=sm[b_][:]),
                  reads=[("sm", b_, h) for h in range(4)], writes=[("rs", b_)])
            P.add("dve", lambda e: e.tensor_tensor(out=Pn[b_][:], in0=Pf[b_][:],
                                                   in1=rs[b_][:].unsqueeze(2).to_broadcast([128, 4, 256]), op=ALU.mult),
                  reads=[("Pf", b_, h) for h in range(4)] + [("rs", b_)], writes=[("Pn", b_)])

        def C(n):
            b_ = n % 2
            bank2 = P.nextbank()
            pview = ps[bank2][:].bitcast(BF16)

            def tr(e):
                for h in range(4):
                    for mc in range(2):
                        j = h * 2 + mc
                        ins = e.transpose(pview[:, j * 128:(j + 1) * 128], Pn[b_][:, h, mc * 128:(mc + 1) * 128], ident[:])
                return ins
            P.add("pe", tr, reads=[("Pn", b_), ("ident",)], writes=[("ps", bank2)])
            P.add("act", lambda e: e.copy(out=PnT[b_][:], in_=pview.rearrange("p (j c) -> p j c", j=8)),
                  reads=[("ps", bank2)], writes=[("PnT", b_)])

        def Dd(n):
            T, ts = divmod(n, 4)
            par = T % 2
            b_ = n % 2
            obk = [P.nextbank(), P.nextbank()]

            def mm2(e):
                for h in range(4):
                    for c in range(2):
                        f = 2 * h + c
                        for mc in range(2):
                            ins = e.matmul(ps[obk[f // 4]][:, (f % 4) * 128:(f % 4 + 1) * 128],
                                           lhsT=Vc[:, mc, f * 128:(f + 1) * 128],
                                           rhs=PnT[b_][:, h * 2 + mc, :], start=(mc == 0), stop=(mc == 1))
                return ins
            P.add("pe", mm2, reads=[("PnT", b_)] + [("Vc", mc, b) for mc in range(2) for b in range(2)],
                  writes=[("ps", obk[0]), ("ps", obk[1])])
            for i in range(2):
                P.add("dve", lambda e, i=i: e.tensor_copy(
                    out=oT[par][:, 4 * i:4 * i + 4, ts * 128:(ts + 1) * 128],
                    in_=ps[obk[i]][:].rearrange("p (f t) -> p f t", f=4)),
                    reads=[("ps", obk[i])], writes=[("oT", par, ts, i)])

        cnorm(0)
        for fc in range(8):
            qproj(0, fc)
        cur = A(0)
        B(0, cur)
        qsched = {0: [0, 1, 2], 1: [3, 4, 5], 2: [6, 7], 3: []}
        for n in range(16):
            T, ts = divmod(n, 4)
            if ts == 0 and T < 3:
                cnorm(T + 1)
            nxt = A(n + 1) if n + 1 < 16 and (n + 1) % 4 != 0 else None
            C(n)
            if nxt is not None:
                B(n + 1, nxt)
            if T < 3:
                for fc in qsched[ts]:
                    qproj(T + 1, fc)
            if n + 1 < 16 and (n + 1) % 4 == 0:
                nxt = A(n + 1)
                B(n + 1, nxt)
            if n >= 1:
                outproj(*divmod(n - 1, 4))
            Dd(n)
        prev = (3, 3)
        outproj(*prev)
        EM.flush()
        st.close()
        XNB[:] = [xn0]

    if stages >= 2:
        mixer_phase()
    if stages >= 3:
        cross_phase()
    if stages >= 4:
        ffn_phase(w2gu_d, w2d_d, 4, "f2", fin=final_norm)
    if stages >= 4 and final_norm:
        EM.flush()
        gstack.close()
        return nc

    EM.flush()
    st = contextlib.ExitStack()
    ob = [salloc(st, f"ob{i}", [128, D], F32) for i in range(2)]
    if final_norm:
        load_gb(5)
        for i in range(NT):
            P.add("act", lambda e, i=i: e.activation(out=XNB[0][:], in_=H[:, i, :], func=AF.Square,
                                                     accum_out=ss[:, i:i + 1]),
                  reads=[("H", i)], writes=[("ss", i), ("xn", 0)])
        P.add("dve", lambda e: e.tensor_scalar(out=rstd[:], in0=ss[:], scalar1=1.0 / D, scalar2=EPS,
                                               op0=ALU.mult, op1=ALU.add),
              reads=[("ss", i) for i in range(NT)], writes=[("rstd",)])
        P.add("act", lambda e: e.activation(out=rstd[:], in_=rstd[:], func=AF.Ln), reads=[("rstd",)], writes=[("rstd",)])
        P.add("act", lambda e: e.activation(out=rstd[:], in_=rstd[:], func=AF.Exp, scale=-0.5),
              reads=[("rstd",)], writes=[("rstd",)])
    outs = []
    for i in range(NT):
        b = i % 2
        if final_norm:
            P.add("dve", lambda e, i=i, b=b: e.scalar_tensor_tensor(out=ob[b][:], in0=H[:, i, :], scalar=rstd[:, i:i + 1],
                                                                    in1=gb[:], op0=ALU.mult, op1=ALU.mult),
                  reads=[("H", i), ("rstd",), ("gb",)], writes=[("ob", b)])
        else:
            P.add("dve", lambda e, i=i, b=b: e.tensor_copy(out=ob[b][:], in_=H[:, i, :]),
                  reads=[("H", i)], writes=[("ob", b)])
        outs.append(dma("sp", y_d[i * 128:(i + 1) * 128, :], ob[b][:], reads=[("ob", b)], writes=[("y", i)]))
    P.add("sp", None, reads=[("y", i) for i in range(NT)], writes=[("done",)])

    EM.flush()
    st.close()
    gstack.close()
    return nc


class Emitter:
    def __init__(self, nc, P, sems, dsems):
        self.nc, self.P, self.sems, self.dsems = nc, P, sems, dsems
        self.cnt = {e: 0 for e in sems}
        self.dq = {q: 0 for q in dsems}
        self.dslot = {q: [0] * ND for q in dsems}
        self.dlast = {q: [None] * ND for q in dsems}
        self.waited = {}
        self.done = 0

    def flush(self):
        P = self.P
        ops = P.ops[self.done:]
        self.done = len(P.ops)
        for op in P.lastw.values():
            op.needed = True
        for rd in P.readers.values():
            for d in rd[0].values():
                d.needed = True
        for op in ops:
            if op.is_dma:
                q = op.eng
                s_ = self.dq[q] % ND
                self.dq[q] += 1
                self.dslot[q][s_] += 1
                op.sem = self.dsems[q][s_]
                op.val = 16 * self.dslot[q][s_]
                op.prev = self.dlast[q][s_]
                self.dlast[q][s_] = op
            elif op.needed and op.fn is not None:
                self.cnt[op.eng] += 1
                op.sem = self.sems[op.eng]
                op.val = self.cnt[op.eng]
        engmap = {"pe": "tensor", "act": "scalar", "dve": "vector", "pool": "gpsimd", "sp": "sync"}
        byeng = dict((e, []) for e in engmap)
        for op in ops:
            byeng[op.eng].append(op)
        waited = self.waited
        with self.nc.Block() as block:
            for eng, bname in engmap.items():
                eops = byeng[eng]
                if not eops:
                    continue

                def body(e, eops=eops, eng=eng):
                    w = waited.setdefault(eng, {})

                    def wait(sem, val):
                        key = id(sem)
                        if w.get(key, (None, 0))[1] >= val:
                            return
                        w[key] = (sem, val)
                        e.wait_ge(sem, val)

                    for op in eops:
                        if op.prev is not None:
                            wait(op.prev.sem, op.prev.val)
                        for d in op.deps:
                            wait(d.sem, d.val)
                        if op.fn is None:
                            continue
                        ins = op.fn(e)
                        if op.is_dma:
                            ins.then_inc(op.sem, 16)
                        elif op.needed:
                            ins.then_inc(op.sem, 1)

                getattr(block, bname)(body)


def _host_consts():
    bf = ml_dtypes.bfloat16
    ident = np.eye(128, dtype=np.float32).astype(bf)
    j = np.arange(128)[:, None]
    s = np.arange(128)[None, :]
    triA = np.where(j >= s, -1.0, 0.0).astype(np.float32).astype(bf)
    triB = np.where(j < s, -1.0, 0.0).astype(np.float32).astype(bf)
    maskb = np.where(j >= s, NEG, 0.0).astype(np.float32).astype(bf)
    return ident, triA, triB, maskb


_CACHE = {}


def kernel(x, mem, g_ffn1, w_ffn1_gu, w_ffn1_down, g_mix, w_in, b_gate, conv_w, w_conv_out, w_attn_out, w_o,
           g_cross, g_mem, w_cq, w_ckv, w_co, g_ffn2, w_ffn2_gu, w_ffn2_down, g_final, _stages=5, _final_norm=True):
    f = lambda a: np.ascontiguousarray(np.asarray(a, dtype=np.float32))
    x = f(x)
    mem = f(mem)
    n = 8
    key = (_stages, _final_norm)
    if key not in _CACHE:
        _CACHE[key] = build_program(_stages, _final_norm)
    nc = _CACHE[key]
    ident, triA, triB, maskb = _host_consts()
    gv = np.stack([np.broadcast_to(f(g)[None, :], (128, D)) for g in (g_ffn1, g_mix, g_cross, g_mem, g_ffn2, g_final)])
    gv = np.ascontiguousarray(gv)
    bg = np.ascontiguousarray(f(b_gate).reshape(16, 128).T)
    cw = np.ascontiguousarray(f(conv_w).T.reshape(8, 128, 3).transpose(1, 0, 2))
    shared = {
        "gv": gv, "w_ffn1_gu": f(w_ffn1_gu), "w_ffn1_down": f(w_ffn1_down), "w_in": f(w_in),
        "w_conv_out": f(w_conv_out), "w_attn_out": f(w_attn_out), "w_o": f(w_o), "w_cq": f(w_cq),
        "w_ckv": f(w_ckv), "w_co": f(w_co), "w_ffn2_gu": f(w_ffn2_gu), "w_ffn2_down": f(w_ffn2_down),
        "bgate": bg, "convw": cw, "ident": ident, "triA": triA, "triB": triB, "maskb": maskb,
    }
    in_maps = []
    for c in range(n):
        m = dict(shared)
        m["x"] = x[c]
        m["mem"] = mem[c]
        in_maps.append(m)
    res = run_bass_kernel_spmd(nc, in_maps, core_ids=list(range(n)))
    return np.stack([r["y"] for r in res.results], axis=0)
```

```python
import os
import numpy as np
import ml_dtypes
import concourse.bass as bass
import concourse.mybir as mybir
from concourse.bass_utils import run_bass_kernel_spmd

F32, BF16 = mybir.dt.float32, mybir.dt.bfloat16
AF = mybir.ActivationFunctionType
ALU = mybir.AluOpType
AX = mybir.AxisListType

S = 2048
D = 1024
DFF = 2816
MEM = 256
NT = S // 128
EPS = 1e-6
ND = 8
NEG = -30000.0


class Op:
    __slots__ = ("eng", "fn", "deps", "needed", "sem", "val", "is_dma", "phase", "prev")


class Plan:
    def __init__(self):
        self.ops = []
        self.lastw = {}
        self.readers = {}
        self.phase = 0
        self.bank = 0
        self.reserved = set()

    def nextbank(self):
        while self.bank in self.reserved:
            self.bank = (self.bank + 1) % 8
        b = self.bank
        self.bank = (self.bank + 1) % 8
        return b

    def add(self, eng, fn, reads=(), writes=(), dma=False):
        op = Op()
        op.eng, op.fn, op.is_dma, op.needed, op.phase = eng, fn, dma, dma, self.phase
        op.prev = None
        deps, seen = [], set()

        def add_dep(d):
            if d is None or id(d) in seen:
                return
            if d.eng == "pe" and eng == "pe" and not d.is_dma and not dma:
                return
            seen.add(id(d))
            deps.append(d)
            d.needed = True

        for r in reads:
            add_dep(self.lastw.get(r))
        for w in writes:
            add_dep(self.lastw.get(w))
            rd = self.readers.get(w)
            if rd:
                for d in rd[0].values():
                    add_dep(d)
                for d in rd[1]:
                    add_dep(d)
        for w in writes:
            self.lastw[w] = op
            self.readers[w] = ({}, [])
        for r in reads:
            if r in writes:
                continue
            rd = self.readers.setdefault(r, ({}, []))
            if dma:
                rd[1].append(op)
            else:
                rd[0][eng] = op
        op.deps = deps
        self.ops.append(op)
        return op

    def next_phase(self):
        self.phase += 1


def build_program(stages=5, final_norm=True):
    nc = bass.Bass("TRN2", target_bir_lowering=False)
    P = Plan()

    def din(name, shape, dt=F32):
        return nc.dram_tensor(name, shape, dt, kind="ExternalInput").ap()

    x_d = din("x", [S, D])
    mem_d = din("mem", [MEM, D])
    gv_d = din("gv", [6, 128, D])
    w1gu_d = din("w_ffn1_gu", [D, 2 * DFF])
    w1d_d = din("w_ffn1_down", [DFF, D])
    win_d = din("w_in", [D, 8 * D])
    wco_d = din("w_conv_out", [D, D])
    wao_d = din("w_attn_out", [D, D])
    wo_d = din("w_o", [D, D])
    wcq_d = din("w_cq", [D, D])
    wckv_d = din("w_ckv", [D, 2 * D])
    wcout_d = din("w_co", [D, D])
    w2gu_d = din("w_ffn2_gu", [D, 2 * DFF])
    w2d_d = din("w_ffn2_down", [DFF, D])
    bg_d = din("bgate", [128, 16])
    cw_d = din("convw", [128, 8, 3])
    ident_d = din("ident", [128, 128], BF16)
    triA_d = din("triA", [128, 128], BF16)
    triB_d = din("triB", [128, 128], BF16)
    maskb_d = din("maskb", [128, 128], BF16)
    y_d = nc.dram_tensor("y", [S, D], F32, kind="ExternalOutput").ap()

    import contextlib

    gstack = contextlib.ExitStack()

    def salloc(stack, name, shape, dt):
        return stack.enter_context(nc.sbuf_tensor(name, shape, dt))

    H = salloc(gstack, "H", [128, NT, D], F32)
    H_ = H
    ident = salloc(gstack, "ident_s", [128, 128], BF16)
    triA = salloc(gstack, "triA_s", [128, 128], BF16)
    triB = salloc(gstack, "triB_s", [128, 128], BF16)
    maskb = salloc(gstack, "maskb_s", [128, 128], BF16)
    bgate = salloc(gstack, "bgate_s", [128, 16], F32)
    convw = salloc(gstack, "convw_s", [128, 8, 3], F32)
    gb = salloc(gstack, "gb", [128, D], F32)
    ss = salloc(gstack, "ss", [128, NT], F32)
    rstd = salloc(gstack, "rstd", [128, NT], F32)
    xn0 = salloc(gstack, "xn0", [128, D], BF16)
    XNB = [xn0]
    ps = [gstack.enter_context(nc.psum_tensor(f"ps{i}", [128, 512], F32)) for i in range(8)]

    sems = {}
    for e in ("pe", "act", "dve", "pool"):
        sems[e] = gstack.enter_context(nc.semaphore(f"s_{e}"))
    dsems = {}
    for q in ("sp", "pool", "act"):
        dsems[q] = [gstack.enter_context(nc.semaphore(f"d_{q}{i}")) for i in range(ND)]

    EM = Emitter(nc, P, sems, dsems)

    def dma(q, out, in_, reads=(), writes=()):
        return P.add(q, lambda e: e.dma_start(out=out, in_=in_), reads=reads, writes=writes, dma=True)

    def load_gb(idx):
        dma("sp", gb[:], gv_d[idx], writes=[("gb",)])

    BLK = {}

    def defblk(key, parts):
        BLK[key] = (len(BLK), parts)

    for blk in range(2):
        defblk(("k", blk), [(win_d[:, 4096 + blk * 512:4096 + (blk + 1) * 512], 0, 512)])
    for blk in range(2):
        defblk(("q", blk), [(win_d[:, 3072 + blk * 512:3072 + (blk + 1) * 512], 0, 512)])
    for blk in range(2):
        defblk(("v", blk), [(win_d[:, 5120 + blk * 512:5120 + (blk + 1) * 512], 0, 512)])
    for c in range(8):
        defblk(("conv", c), [(win_d[:, j * 1024 + c * 128:j * 1024 + (c + 1) * 128], j * 128, 128) for j in range(3)])
    for fc in range(8):
        defblk(("op", fc), [(wco_d[:, fc * 128:(fc + 1) * 128], 0, 128), (wao_d[:, fc * 128:(fc + 1) * 128], 128, 128),
                            (win_d[:, 6144 + fc * 128:6144 + (fc + 1) * 128], 256, 128),
                            (win_d[:, 7168 + fc * 128:7168 + (fc + 1) * 128], 384, 128)])
    for dh in range(2):
        defblk(("wo", dh), [(wo_d[:, dh * 512:(dh + 1) * 512], 0, 512)])
    wsc = nc.dram_tensor("wscratch", [len(BLK), 128, 8, 512], BF16).ap()
    conv_dmas = []
    for key, (b, parts) in BLK.items():
        for (src, c0, n) in parts:
            conv_dmas.append(lambda b=b, src=src, c0=c0, n=n: dma(
                "pool", wsc[b][:, :, c0:c0 + n], src.rearrange("(k p) n -> p k n", p=128),
                writes=[("wsc", b, c0 // 128 + j) for j in range(n // 128)]))

    xn_ctr = [0]

    def norm_pieces(tiles, XT, col_of_tile, evac_eng="act", SRC=None, skey="H", xkey="XT"):
        H = SRC if SRC is not None else H_
        xk = (lambda c: xkey + (c,)) if isinstance(xkey, tuple) else (lambda c: (xkey, c))
        lo, hi = min(tiles), max(tiles) + 1

        def stats():
            for i in tiles:
                P.add("act", lambda e, i=i: e.activation(out=XNB[0][:], in_=H[:, i, :], func=AF.Square,
                                                         accum_out=ss[:, i:i + 1]),
                      reads=[(skey, i)], writes=[("ss", i), ("xn", 0)])
            P.add("dve", lambda e: e.tensor_scalar(out=rstd[:, lo:hi], in0=ss[:, lo:hi], scalar1=1.0 / D, scalar2=EPS,
                                                   op0=ALU.mult, op1=ALU.add),
                  reads=[("ss", i) for i in tiles], writes=[("rstd",)])
            P.add("act", lambda e: e.activation(out=rstd[:, lo:hi], in_=rstd[:, lo:hi], func=AF.Ln),
                  reads=[("rstd",)], writes=[("rstd",)])
            P.add("act", lambda e: e.activation(out=rstd[:, lo:hi], in_=rstd[:, lo:hi], func=AF.Exp, scale=-0.5),
                  reads=[("rstd",)], writes=[("rstd",)])

        bsel = {}

        def mult(i):
            b = xn_ctr[0] % len(XNB)
            xn_ctr[0] += 1
            bsel[i] = b
            xnb = XNB[b]
            P.add("dve", lambda e: e.scalar_tensor_tensor(out=xnb[:], in0=H[:, i, :], scalar=rstd[:, i:i + 1], in1=gb[:],
                                                          op0=ALU.mult, op1=ALU.mult),
                  reads=[(skey, i), ("rstd",), ("gb",)], writes=[("xn", b)])

        def trn(i):
            b = bsel[i]
            xnb = XNB[b]
            bank = P.nextbank()
            pview = ps[bank][:].bitcast(BF16)

            def tr(e):
                for k in range(8):
                    ins = e.transpose(pview[:, k * 128:(k + 1) * 128], xnb[:, k * 128:(k + 1) * 128], ident[:])
                return ins

            P.add("pe", tr, reads=[("xn", b), ("ident",)], writes=[("ps", bank)])
            c0 = col_of_tile(i)
            src = pview.rearrange("p (k c) -> p k c", k=8)
            dst = XT[:, :, c0:c0 + 128]
            if evac_eng == "act":
                P.add("act", lambda e: e.copy(out=dst, in_=src), reads=[("ps", bank)], writes=[xk(c0 // 128)])
            else:
                P.add("dve", lambda e: e.tensor_copy(out=dst, in_=src), reads=[("ps", bank)], writes=[xk(c0 // 128)])

        if len(XNB) == 1:
            return [stats] + [(lambda i=i: (mult(i), trn(i))) for i in tiles]
        pieces = [stats, (lambda: mult(tiles[0]))]
        for j in range(1, len(tiles)):
            pieces.append(lambda j=j: (mult(tiles[j]), trn(tiles[j - 1])))
        pieces.append(lambda: trn(tiles[-1]))
        return pieces

    def norm_tiles(*a, **kw):
        for p_ in norm_pieces(*a, **kw):
            p_()

    for a in range(4):
        dma("sp", H[:, 4 * a:4 * a + 4, :], x_d[512 * a:512 * (a + 1), :].rearrange("(i p) d -> p i d", p=128),
            writes=[("H", 4 * a + j) for j in range(4)])
    dma("sp", ident[:], ident_d, writes=[("ident",)])
    dma("sp", triA[:], triA_d, writes=[("triA",)])
    dma("sp", triB[:], triB_d, writes=[("triB",)])
    dma("sp", maskb[:], maskb_d, writes=[("maskb",)])
    dma("sp", bgate[:], bg_d, writes=[("bgate",)])
    dma("sp", convw[:], cw_d, writes=[("convw",)])

    def ffn_phase(wgu_d, wd_d, gidx, tag, fin=False, extra=None):
        extra = list(extra or [])
        per_group = (len(extra) + 11) // 12
        EM.flush()
        st = contextlib.ExitStack()
        if fin:
            gb2 = salloc(st, "gb2", [128, D], F32)
            ob = [salloc(st, f"obf{i}", [128, D], F32) for i in range(2)]
            ss2 = salloc(st, "ss2", [128, NT], F32)
            rstd2 = salloc(st, "rstd2", [128, NT], F32)
            dma("sp", gb2[:], gv_d[5], writes=[("gb2",)])
            epsT = salloc(st, "epsT", [128, 1], F32)
            P.add("dve", lambda e: e.memset(epsT[:], EPS), writes=[("epsT",)])

            def fin_stats(t):
                P.add("act", lambda e: e.activation(out=XNB[0][:], in_=H[:, t, :], func=AF.Square, accum_out=ss2[:, t:t + 1]),
                      reads=[("H", t)], writes=[("ss2", t), ("xn", 0)])
                P.add("act", lambda e: e.activation(out=rstd2[:, t:t + 1], in_=ss2[:, t:t + 1], func=AF.Ln, scale=1.0 / D,
                                                    bias=epsT[:, 0:1]),
                      reads=[("ss2", t), ("epsT",)], writes=[("rstd2", t)])
                P.add("act", lambda e: e.activation(out=rstd2[:, t:t + 1], in_=rstd2[:, t:t + 1], func=AF.Exp, scale=-0.5),
                      reads=[("rstd2", t)], writes=[("rstd2", t)])

            def fin_out(t):
                b = t % 2
                P.add("dve", lambda e: e.scalar_tensor_tensor(out=ob[b][:], in0=H[:, t, :], scalar=rstd2[:, t:t + 1], in1=gb2[:],
                                                              op0=ALU.mult, op1=ALU.mult),
                      reads=[("H", t), ("rstd2", t), ("gb2",)], writes=[("obf", b)])
                dma("sp", y_d[t * 128:(t + 1) * 128, :], ob[b][:], reads=[("obf", b)], writes=[("y", t)])

            def final_tiles(tiles):
                lo, hi = min(tiles), max(tiles) + 1
                for i in tiles:
                    P.add("act", lambda e, i=i: e.activation(out=XNB[0][:], in_=H[:, i, :], func=AF.Square,
                                                             accum_out=ss2[:, i:i + 1]),
                          reads=[("H", i)], writes=[("ss2", i), ("xn", 0)])
                P.add("dve", lambda e: e.tensor_scalar(out=rstd2[:, lo:hi], in0=ss2[:, lo:hi], scalar1=1.0 / D, scalar2=EPS,
                                                       op0=ALU.mult, op1=ALU.add),
                      reads=[("ss2", i) for i in tiles], writes=[("rstd2", i) for i in tiles])
                P.add("act", lambda e: e.activation(out=rstd2[:, lo:hi], in_=rstd2[:, lo:hi], func=AF.Ln),
                      reads=[("rstd2", i) for i in tiles], writes=[("rstd2", i) for i in tiles])
                P.add("act", lambda e: e.activation(out=rstd2[:, lo:hi], in_=rstd2[:, lo:hi], func=AF.Exp, scale=-0.5),
                      reads=[("rstd2", i) for i in tiles], writes=[("rstd2", i) for i in tiles])
                for i in tiles:
                    b = i % 2
                    P.add("dve", lambda e, i=i, b=b: e.scalar_tensor_tensor(out=ob[b][:], in0=H[:, i, :],
                                                                            scalar=rstd2[:, i:i + 1], in1=gb2[:],
                                                                            op0=ALU.mult, op1=ALU.mult),
                          reads=[("H", i), ("rstd2", i), ("gb2",)], writes=[("obf", b)])
                    dma("sp", y_d[i * 128:(i + 1) * 128, :], ob[b][:], reads=[("obf", b)], writes=[("y", i)])
        XNB[:] = [xn0, salloc(st, f"xn1_{tag}", [128, D], BF16)]
        XTs = [salloc(st, f"XT{i}_{tag}", [128, 8, 1024], BF16) for i in range(2)]
        actT = salloc(st, f"actT_{tag}", [128, 22, 1024], BF16)
        wgu = [salloc(st, f"wgu{i}_{tag}", [128, 8, 2, 512], BF16) for i in range(2)]
        NWD = 4
        wd = [salloc(st, f"wd{i}_{tag}", [128, 2, 512], BF16) for i in range(NWD)]
        sg = [salloc(st, f"sg{i}_{tag}", [128, 512], F32) for i in range(2)]
        load_gb(gidx)
        gcnt = 0
        dcnt = 0
        scnt = 0
        nfill = []
        for hf in range(2):
            XT = XTs[hf]
            if hf == 0:
                norm_tiles(list(range(0, 8)), XT, lambda i: i * 128, xkey=("XT", 0))
                nfill = norm_pieces(list(range(8, 16)), XTs[1], lambda i: (i - 8) * 128, xkey=("XT", 1))
            if fin and hf == 1:
                final_tiles(list(range(0, 8)))
            for gi in range(6):
                ncols = 512 if gi < 5 else 256
                slot = gcnt % 2
                gcnt += 1
                for gu in range(2):
                    c0 = gu * DFF + gi * 512
                    dma("pool", wgu[slot][:, :, gu, 0:ncols],
                        wgu_d[:, c0:c0 + ncols].rearrange("(k p) n -> p k n", p=128),
                        writes=[("wgu", slot, gu)])
                if gcnt > 2:
                    for _ in range(2):
                        if extra:
                            extra.pop(0)()
                for jj in range(ncols // 128):
                    j = gi * 4 + jj
                    for nh in range(2):
                        banks = (P.nextbank(), P.nextbank())
                        for gu in range(2):
                            def mm(e, slot=slot, gu=gu, jj=jj, nh=nh, bank=banks[gu], XT=XT):
                                for k in range(8):
                                    ins = e.matmul(ps[bank][:], lhsT=wgu[slot][:, k, gu, jj * 128:(jj + 1) * 128],
                                                   rhs=XT[:, k, nh * 512:(nh + 1) * 512], start=(k == 0), stop=(k == 7))
                                return ins
                            P.add("pe", mm, reads=[("wgu", slot, gu)] + [("XT", hf, nh * 4 + c) for c in range(4)],
                                  writes=[("ps", banks[gu])])
                        s_ = scnt % 2
                        scnt += 1
                        P.add("act", lambda e, s_=s_, b=banks[0]: e.activation(out=sg[s_][:], in_=ps[b][:], func=AF.Silu),
                              reads=[("ps", banks[0])], writes=[("sg", s_)])
                        P.add("dve", lambda e, s_=s_, b=banks[1], j=j, nh=nh: e.tensor_tensor(
                            out=actT[:, j, nh * 512:(nh + 1) * 512], in0=sg[s_][:], in1=ps[b][:], op=ALU.mult),
                            reads=[("sg", s_), ("ps", banks[1])], writes=[("actT", j, nh)])
                    if hf == 0 and j >= 2 and nfill:
                        nfill.pop(0)()
            while hf == 0 and nfill:
                nfill.pop(0)()
            for dh in range(2):
                for kg in range(11):
                    slot = dcnt % NWD
                    dcnt += 1
                    dma("pool", wd[slot][:],
                        wd_d[kg * 256:(kg + 1) * 256, dh * 512:(dh + 1) * 512].rearrange("(k p) n -> p k n", p=128),
                        writes=[("wd", slot)])
                    if extra:
                        extra.pop(0)()
                    for ts in range(8):
                        def mm(e, slot=slot, kg=kg, ts=ts):
                            for kk in range(2):
                                ins = e.matmul(ps[ts][:], lhsT=actT[:, 2 * kg + kk, ts * 128:(ts + 1) * 128],
                                               rhs=wd[slot][:, kk, :], start=(kg == 0 and kk == 0),
                                               stop=(kg == 10 and kk == 1))
                            return ins
                        P.add("pe", mm, reads=[("wd", slot), ("actT", 2 * kg, ts // 4), ("actT", 2 * kg + 1, ts // 4)],
                              writes=[("ps", ts)])
                for ts in range(8):
                    t = 8 * hf + ts
                    P.add("dve", lambda e, ts=ts, t=t, dh=dh: e.scalar_tensor_tensor(
                        out=H[:, t, dh * 512:(dh + 1) * 512], in0=ps[ts][:], scalar=0.5,
                        in1=H[:, t, dh * 512:(dh + 1) * 512], op0=ALU.mult, op1=ALU.add),
                        reads=[("ps", ts), ("H", t)], writes=[("H", t)])
                    if fin and hf == 1 and dh == 1:
                        fin_stats(t)
                        if ts > 0:
                            fin_out(t - 1)
        if fin:
            fin_out(15)
            P.add("sp", None, reads=[("y", i) for i in range(NT)], writes=[("done",)])
        EM.flush()
        st.close()
        XNB[:] = [xn0]

    if stages >= 1:
        ffn_phase(w1gu_d, w1d_d, 0, "f1", extra=conv_dmas if stages >= 2 else None)


    def mixer_phase():
        EM.flush()
        st = contextlib.ExitStack()
        kT = salloc(st, "kT", [128, 8, S], BF16)
        V = salloc(st, "V", [128, NT, D], BF16)
        XTs = [salloc(st, f"XTm{i}", [128, 8, 512], BF16) for i in range(2)]
        qT = salloc(st, "qTm", [128, 8, 512], BF16)
        mT = salloc(st, "mTm", [128, 8, 512], BF16)
        ysT = qT
        ycT = salloc(st, "ycT", [128, 8, 512], BF16)
        wb = [salloc(st, f"wbm{i}", [128, 8, 512], BF16) for i in range(2)]
        ebuf = [salloc(st, f"e{i}", [128, 512], F32) for i in range(2)]
        ebuf += [mT[:, 2 * j:2 * j + 2, :].rearrange("p a b -> p (a b)").bitcast(F32) for j in range(2)]
        ekeys = {0: [("e", 0)], 1: [("e", 1)], 2: [("mT", 0), ("mT", 1)], 3: [("mT", 2), ("mT", 3)]}
        spb = [salloc(st, f"sp{i}", [128, 512], BF16) for i in range(2)]
        E1 = [salloc(st, f"E1{i}", [128, 512], F32) for i in range(2)]
        Ab = [salloc(st, f"A{i}", [128, 512], BF16) for i in range(2)]
        pbuf = salloc(st, "pbuf", [128, 514], F32)
        phalo = salloc(st, "phalo", [128, 8, 2], F32)
        tmpa = salloc(st, "tmpa", [128, 512], F32)
        load_gb(1)
        P.add("dve", lambda e: e.memset(phalo[:].rearrange("p a b -> p (a b)"), 0.0), writes=[("phalo", c) for c in range(8)])
        wcnt = [0]
        SCALE = float(128 ** -0.5)

        def wload(key):
            b, parts = BLK[key]
            ncols = max(c0 + n for (_, c0, n) in parts)
            slot = wcnt[0] % 2
            wcnt[0] += 1
            dma("sp", wb[slot][:, :, 0:ncols], wsc[b][:, :, 0:ncols],
                reads=[("wsc", b, j) for j in range(ncols // 128)],
                writes=[("wb", slot, j) for j in range(ncols // 128)])
            return slot

        def proj_fm(slot, cb, dst_fn, rkeys, XTsrc, xk):
            bank = P.nextbank()

            def mm(e):
                for k in range(8):
                    ins = e.matmul(ps[bank][:], lhsT=wb[slot][:, k, cb * 128:(cb + 1) * 128], rhs=XTsrc[:, k, :],
                                   start=(k == 0), stop=(k == 7))
                return ins
            P.add("pe", mm, reads=[("wb", slot, cb)] + rkeys, writes=[("ps", bank)])
            return bank

        def xkeys_of(T):
            return [("XTm", T % 2, c) for c in range(4)]

        def proj_fm_g(slot, cb, rkeys, XTsrc):
            bank = P.nextbank()
            P.reserved.add(bank)
            for kk in range(4):
                def mm(e, kk=kk):
                    for k in (2 * kk, 2 * kk + 1):
                        ins = e.matmul(ps[bank][:], lhsT=wb[slot][:, k, cb * 128:(cb + 1) * 128], rhs=XTsrc[:, k, :],
                                       start=(k == 0), stop=(k == 7))
                    return ins
                P.add("pe", mm, reads=[("wb", slot, cb)] + rkeys, writes=[("ps", bank)])
                if kk < 3:
                    yield "pe"
            return bank

        def norm_gen(T):
            XT = XTs[T % 2]
            tiles = list(range(4 * T, 4 * T + 4))
            lo, hi = tiles[0], tiles[-1] + 1
            for i in tiles:
                P.add("act", lambda e, i=i: e.activation(out=xn0[:], in_=H[:, i, :], func=AF.Square, accum_out=ss[:, i:i + 1]),
                      reads=[("H", i)], writes=[("ss", i), ("xn", 0)])
            P.add("dve", lambda e: e.tensor_scalar(out=rstd[:, lo:hi], in0=ss[:, lo:hi], scalar1=1.0 / D, scalar2=EPS,
                                                   op0=ALU.mult, op1=ALU.add),
                  reads=[("ss", i) for i in tiles], writes=[("rstd",)])
            P.add("act", lambda e: e.activation(out=rstd[:, lo:hi], in_=rstd[:, lo:hi], func=AF.Ln),
                  reads=[("rstd",)], writes=[("rstd",)])
            P.add("act", lambda e: e.activation(out=rstd[:, lo:hi], in_=rstd[:, lo:hi], func=AF.Exp, scale=-0.5),
                  reads=[("rstd",)], writes=[("rstd",)])
            yield "dve"
            for i in tiles:
                P.add("dve", lambda e, i=i: e.scalar_tensor_tensor(out=xn0[:], in0=H[:, i, :], scalar=rstd[:, i:i + 1], in1=gb[:],
                                                                   op0=ALU.mult, op1=ALU.mult),
                      reads=[("H", i), ("rstd",), ("gb",)], writes=[("xn", 0)])
                bank = P.nextbank()
                P.reserved.add(bank)
                pview = ps[bank][:].bitcast(BF16)
                for kk in range(4):
                    def tr(e, kk=kk, pview=pview):
                        for k in (2 * kk, 2 * kk + 1):
                            ins = e.transpose(pview[:, k * 128:(k + 1) * 128], xn0[:, k * 128:(k + 1) * 128], ident[:])
                        return ins
                    P.add("pe", tr, reads=[("xn", 0), ("ident",)], writes=[("ps", bank)])
                    if kk < 3:
                        yield "pe"
                yield "dve"
                c0 = (i - 4 * T) * 128
                P.add("dve", lambda e, c0=c0, pview=pview: e.tensor_copy(out=XT[:, :, c0:c0 + 128],
                                                                        in_=pview.rearrange("p (k c) -> p k c", k=8)),
                      reads=[("ps", bank)], writes=[("XTm", T % 2, c0 // 128)])
                P.reserved.discard(bank)
                yield "dve"

        def k_gen(T):
            XT, xkeys = XTs[T % 2], xkeys_of(T)
            for blk in range(2):
                slot = wload(("k", blk))
                for cb in range(4):
                    fc = blk * 4 + cb
                    bank = yield from proj_fm_g(slot, cb, xkeys, XT)
                    yield "dve"
                    P.add("dve", lambda e, fc=fc, bank=bank: e.tensor_copy(out=kT[:, fc, T * 512:(T + 1) * 512], in_=ps[bank][:]),
                          reads=[("ps", bank)], writes=[("kT", fc, T)])
                    P.reserved.discard(bank)
                    yield "pe"

        def v_gen(T):
            XT = XTs[T % 2]
            for blk in range(2):
                slot = wload(("v", blk))
                for ts in range(4):
                    bank = P.nextbank()
                    P.reserved.add(bank)
                    for kk in range(4):
                        def mm(e, kk=kk, slot=slot, ts=ts, bank=bank):
                            for k in (2 * kk, 2 * kk + 1):
                                ins = e.matmul(ps[bank][:], lhsT=XT[:, k, ts * 128:(ts + 1) * 128], rhs=wb[slot][:, k, :],
                                               start=(k == 0), stop=(k == 7))
                            return ins
                        P.add("pe", mm, reads=[("wb", slot, j) for j in range(4)] + [("XTm", T % 2, ts)], writes=[("ps", bank)])
                        if kk < 3:
                            yield "pe"
                    yield "dve"
                    P.add("dve", lambda e, bank=bank, ts=ts, blk=blk: e.tensor_copy(
                        out=V[:, 4 * T + ts, blk * 512:(blk + 1) * 512], in_=ps[bank][:]),
                        reads=[("ps", bank)], writes=[("V", 4 * T + ts, blk)])
                    P.reserved.discard(bank)
                    yield "pe"

        def drain(gens):
            for g in list(gens):
                for _ in g:
                    pass
            del gens[:]

        drain([norm_gen(0), k_gen(0), v_gen(0)])
        def q_tile(Tq):
            XTq, xkq = XTs[Tq % 2], xkeys_of(Tq)
            for blk in range(2):
                slot = wload(("q", blk))
                for cb in range(4):
                    fc = blk * 4 + cb
                    bank = proj_fm(slot, cb, None, xkq, XTq, "XTm")
                    P.add("act", lambda e, fc=fc, bank=bank: e.activation(out=qT[:, fc, :], in_=ps[bank][:], func=AF.Copy,
                                                                         scale=SCALE),
                          reads=[("ps", bank)], writes=[("qT", fc)])

        q_tile(0)
        for T in range(4):
            XT, xkeys = XTs[T % 2], xkeys_of(T)
            def conv_gen(c, XT=XT, xkeys=xkeys):
                slot = wload(("conv", c))
                b_cc = yield from proj_fm_g(slot, 1, xkeys, XT)
                yield "dve"
                P.add("dve", lambda e: e.tensor_copy(out=tmpa[:], in_=ps[b_cc][:]), reads=[("ps", b_cc)], writes=[("tmpa",)])
                P.reserved.discard(b_cc)
                yield "pe"
                b_cx = yield from proj_fm_g(slot, 2, xkeys, XT)
                yield "dve"
                P.add("dve", lambda e: e.tensor_copy(out=pbuf[:, 0:2], in_=phalo[:, c, :]),
                      reads=[("phalo", c)], writes=[("pbuf",)])
                P.add("dve", lambda e: e.tensor_tensor(out=pbuf[:, 2:514], in0=tmpa[:], in1=ps[b_cx][:], op=ALU.mult),
                      reads=[("tmpa",), ("ps", b_cx), ("pbuf",)], writes=[("pbuf",)])
                P.reserved.discard(b_cx)
                P.add("dve", lambda e: e.tensor_copy(out=phalo[:, c, :], in_=pbuf[:, 512:514]),
                      reads=[("pbuf",)], writes=[("phalo", c)])
                yield "dve"
                P.add("dve", lambda e: e.tensor_scalar(out=tmpa[:], in0=pbuf[:, 2:514], scalar1=convw[:, c, 2:3], scalar2=None,
                                                       op0=ALU.mult),
                      reads=[("pbuf",), ("convw",)], writes=[("tmpa",)])
                P.add("dve", lambda e: e.scalar_tensor_tensor(out=tmpa[:], in0=pbuf[:, 1:513], scalar=convw[:, c, 1:2],
                                                              in1=tmpa[:], op0=ALU.mult, op1=ALU.add),
                      reads=[("pbuf",), ("tmpa",), ("convw",)], writes=[("tmpa",)])
                yield "dve"
                P.add("dve", lambda e: e.scalar_tensor_tensor(out=tmpa[:], in0=pbuf[:, 0:512], scalar=convw[:, c, 0:1],
                                                              in1=tmpa[:], op0=ALU.mult, op1=ALU.add),
                      reads=[("pbuf",), ("tmpa",), ("convw",)], writes=[("tmpa",)])
                yield "pe"
                b_cb = yield from proj_fm_g(slot, 0, xkeys, XT)
                yield "dve"
                P.add("dve", lambda e: e.tensor_tensor(out=ycT[:, c, :], in0=tmpa[:], in1=ps[b_cb][:], op=ALU.mult),
                      reads=[("tmpa",), ("ps", b_cb)], writes=[("ycT", c)])
                P.reserved.discard(b_cb)

            gens = [[conv_gen(c), "pe"] for c in range(8)]
            nmicro = 8 * 17
            if T < 3:
                gens += [[norm_gen(T + 1), "dve"], [k_gen(T + 1), "pe"], [v_gen(T + 1), "pe"]]
                nmicro += 21 + 40 + 40
            nch = 4 * T + 4
            npoints = 3 * 4 * nch
            fstate = [0, 0]

            def pump_n(n, allow_dve):
                while n > 0 and gens:
                    g, tag = gens[0]
                    if tag == "dve" and not allow_dve:
                        break
                    try:
                        gens[0][1] = next(g)
                        n -= 1
                        fstate[1] += 1
                    except StopIteration:
                        gens.pop(0)

            def step_done(allow_dve=False):
                fstate[0] += 1
                target = (nmicro * fstate[0] + npoints - 1) // npoints
                pump_n(target - fstate[1], allow_dve)

            tcnt = [0]
            zb = {}
            for hp in range(4):
                heads = (2 * hp, 2 * hp + 1)
                Rb = [P.nextbank(), P.nextbank()]
                P.reserved.update(Rb)
                Ob = [P.nextbank(), P.nextbank()]
                P.reserved.update(Ob)

                def emit_z(h, c):
                    bank = P.nextbank()
                    P.reserved.add(bank)
                    dd = c - 4 * T
                    c0 = max(dd, 0) * 128

                    def mm(e, h=h, c=c, bank=bank, dd=dd, c0=c0):
                        ins = e.matmul(ps[bank][:, c0:512], lhsT=kT[:, h, c * 128:(c + 1) * 128], rhs=qT[:, h, c0:512],
                                       start=True, stop=(dd < 0))
                        if dd >= 0:
                            ins = e.matmul(ps[bank][:, c0:c0 + 128], lhsT=ident[:], rhs=maskb[:], start=False, stop=True)
                        return ins
                    P.add("pe", mm, reads=[("kT", h, c // 4), ("qT", h), ("ident",), ("maskb",)], writes=[("ps", bank)])
                    zb[(h, c)] = bank

                def split_mm(e, out_bank, lhsT, rhs_buf, dd, c0, last):
                    if dd >= 0:
                        ins = e.matmul(ps[out_bank][:, c0:c0 + 128], lhsT=lhsT, rhs=rhs_buf[:, c0:c0 + 128], start=(dd == 3),
                                       stop=last, skip_group_check=True)
                        if dd < 3:
                            ins = e.matmul(ps[out_bank][:, c0 + 128:512], lhsT=lhsT, rhs=rhs_buf[:, c0 + 128:512],
                                           start=False, stop=last, skip_group_check=True)
                    else:
                        ins = e.matmul(ps[out_bank][:], lhsT=lhsT, rhs=rhs_buf[:], start=False, stop=last, skip_group_check=True)
                    return ins

                for h in heads:
                    if (h, nch - 1) not in zb:
                        emit_z(h, nch - 1)
                for c in range(nch - 1, -1, -1):
                    dd = c - 4 * T
                    c0 = max(dd, 0) * 128
                    slots = {}
                    eslot = {}
                    for i, h in enumerate(heads):
                        s_ = i
                        se = 2 * (tcnt[0] % 2) + i
                        slots[h] = s_
                        eslot[h] = se
                        bank = zb[(h, c)]
                        P.add("act", lambda e, se=se, bank=bank, c0=c0: e.activation(out=ebuf[se][:, c0:512], in_=ps[bank][:, c0:512],
                                                                                     func=AF.Exp),
                              reads=[("ps", bank)], writes=ekeys[se])
                        P.reserved.discard(bank)
                    tcnt[0] += 1
                    for h in heads:
                        s_ = slots[h]
                        se = eslot[h]
                        P.add("act", lambda e, s_=s_, se=se, c0=c0: e.activation(out=spb[s_][:, c0:512], in_=ebuf[se][:, c0:512],
                                                                                 func=AF.Ln, bias=1.0),
                              reads=ekeys[se], writes=[("sp", s_)])
                    for i, h in enumerate(heads):
                        s_ = slots[h]
                        P.add("pe", lambda e, s_=s_, rb=Rb[i], dd=dd, c0=c0: split_mm(e, rb, triA[:], spb[s_], dd, c0, True),
                              reads=[("triA",), ("sp", s_)], writes=[("ps", Rb[i])])
                    step_done(True)
                    if c > 0:
                        for h in heads:
                            emit_z(h, c - 1)
                    elif hp < 3:
                        for h2 in (2 * hp + 2, 2 * hp + 3):
                            emit_z(h2, nch - 1)
                    step_done()
                    for i, h in enumerate(heads):
                        s_ = slots[h]
                        P.add("act", lambda e, s_=s_, rb=Rb[i], c0=c0: e.activation(out=E1[s_][:, c0:512], in_=ps[rb][:, c0:512],
                                                                                    func=AF.Exp),
                              reads=[("ps", Rb[i])], writes=[("E1", s_)])
                        if c > 0:
                            P.add("pe", lambda e, s_=s_, rb=Rb[i], c0=c0: e.matmul(ps[rb][:, c0:512], lhsT=triB[:], rhs=spb[s_][:, c0:512],
                                                                                start=False, stop=True, skip_group_check=True),
                                  reads=[("triB",), ("sp", s_)], writes=[("ps", Rb[i])])
                        se = eslot[h]
                        P.add("dve", lambda e, s_=s_, se=se, c0=c0: e.tensor_tensor(out=Ab[s_][:, c0:512], in0=ebuf[se][:, c0:512],
                                                                                    in1=E1[s_][:, c0:512], op=ALU.mult),
                              reads=ekeys[se] + [("E1", s_)], writes=[("A", s_)])
                        P.add("pe", lambda e, s_=s_, ob=Ob[i], h=h, c=c, dd=dd, c0=c0: split_mm(
                            e, ob, V[:, c, h * 128:(h + 1) * 128], Ab[s_], dd, c0, (c == 0)),
                            reads=[("V", c, h // 4), ("A", s_)], writes=[("ps", Ob[i])])
                    step_done(True)
                for i, h in enumerate(heads):
                    P.add("dve", lambda e, h=h, ob=Ob[i]: e.tensor_copy(out=ysT[:, h, :], in_=ps[ob][:]),
                          reads=[("ps", Ob[i])], writes=[("qT", h)])
                P.reserved.difference_update(Rb)
                P.reserved.difference_update(Ob)
            drain([g for g, _ in gens])
            del gens[:]
            for fc in range(8):
                slot = wload(("op", fc))
                bA = proj_fm(slot, 0, None, [("ycT", k) for k in range(8)], ycT, "ycT")
                bB = proj_fm(slot, 1, None, [("qT", k) for k in range(8)], ysT, "ysT")
                bGc = proj_fm(slot, 2, None, xkeys, XT, "XTm")
                bGs = proj_fm(slot, 3, None, xkeys, XT, "XTm")
                P.add("act", lambda e, b=bGc, fc=fc: e.activation(out=tmpa[:], in_=ps[b][:], func=AF.Sigmoid,
                                                                 bias=bgate[:, fc:fc + 1]),
                      reads=[("ps", bGc), ("bgate",)], writes=[("tmpa",)])
                P.add("dve", lambda e, b=bA: e.tensor_tensor(out=tmpa[:], in0=tmpa[:], in1=ps[b][:], op=ALU.mult),
                      reads=[("tmpa",), ("ps", bA)], writes=[("tmpa",)])
                P.add("act", lambda e, b=bGs, fc=fc: e.activation(out=pbuf[:, 0:512], in_=ps[b][:], func=AF.Sigmoid,
                                                                 bias=bgate[:, 8 + fc:9 + fc]),
                      reads=[("ps", bGs), ("bgate",)], writes=[("pbuf",)])
                P.add("dve", lambda e, b=bB: e.tensor_tensor(out=pbuf[:, 0:512], in0=pbuf[:, 0:512], in1=ps[b][:], op=ALU.mult),
                      reads=[("pbuf",), ("ps", bB)], writes=[("pbuf",)])
                P.add("dve", lambda e, fc=fc: e.tensor_tensor(out=mT[:, fc, :], in0=tmpa[:], in1=pbuf[:, 0:512], op=ALU.add),
                      reads=[("tmpa",), ("pbuf",)], writes=[("mT", fc)])
            if T < 3:
                q_tile(T + 1)
            for dh in range(2):
                slot = wload(("wo", dh))
                for ts in range(4):
                    bank = P.nextbank()

                    def mm(e, slot=slot, ts=ts, bank=bank):
                        for k in range(8):
                            ins = e.matmul(ps[bank][:], lhsT=mT[:, k, ts * 128:(ts + 1) * 128], rhs=wb[slot][:, k, :],
                                           start=(k == 0), stop=(k == 7))
                        return ins
                    P.add("pe", mm, reads=[("wb", slot, j) for j in range(4)] + [("mT", k) for k in range(8)],
                          writes=[("ps", bank)])
                    t = 4 * T + ts
                    P.add("dve", lambda e, bank=bank, t=t, dh=dh: e.tensor_tensor(
                        out=H[:, t, dh * 512:(dh + 1) * 512], in0=ps[bank][:], in1=H[:, t, dh * 512:(dh + 1) * 512], op=ALU.add),
                        reads=[("ps", bank), ("H", t)], writes=[("H", t)])
        EM.flush()
        st.close()
        XNB[:] = [xn0]

    def cross_phase():
        EM.flush()
        st = contextlib.ExitStack()
        XNB[:] = [xn0, salloc(st, "xn1_c", [128, D], BF16)]
        memt = salloc(st, "memt", [128, 2, D], F32)
        mnT = salloc(st, "mnT", [128, 8, 256], BF16)
        kcT = salloc(st, "kcT", [128, 8, 256], BF16)
        Vc = salloc(st, "Vc", [128, 2, D], BF16)
        XT = [salloc(st, f"XTc{i}", [128, 8, 512], BF16) for i in range(2)]
        qcT = [salloc(st, f"qcT{i}", [128, 8, 512], BF16) for i in range(2)]
        oT = [salloc(st, f"oT{i}", [128, 8, 512], BF16) for i in range(2)]
        wq = [salloc(st, f"wq{i}", [128, 8, 512], BF16) for i in range(2)]
        wo = [salloc(st, f"wo{i}", [128, 8, 512], BF16) for i in range(2)]
        wkv = [salloc(st, f"wkv{i}", [128, 8, 512], BF16) for i in range(2)]
        Pf = [salloc(st, f"Pf{i}", [128, 4, 256], F32) for i in range(2)]
        Pn = [salloc(st, f"Pn{i}", [128, 4, 256], BF16) for i in range(2)]
        PnT = [salloc(st, f"PnT{i}", [128, 8, 128], BF16) for i in range(2)]
        mx = [salloc(st, f"mx{i}", [128, 4], F32) for i in range(2)]
        sm = [salloc(st, f"sm{i}", [128, 4], F32) for i in range(2)]
        rs = [salloc(st, f"rs{i}", [128, 4], F32) for i in range(2)]

        dma("sp", memt[:], mem_d.rearrange("(i p) d -> p i d", p=128), writes=[("memt", 0), ("memt", 1)])
        load_gb(3)
        for blk in range(2):
            dma("pool", wkv[blk][:], wckv_d[:, blk * 512:(blk + 1) * 512].rearrange("(k p) n -> p k n", p=128),
                writes=[("wkv", blk)])
        norm_tiles([0, 1], mnT, lambda i: i * 128, SRC=memt, skey="memt", xkey="mnT")
        load_gb(2)
        for blk in range(2):
            for cb in range(4):
                fc = blk * 4 + cb
                bank = P.nextbank()

                def mm(e, blk=blk, cb=cb, bank=bank):
                    for k in range(8):
                        ins = e.matmul(ps[bank][:, 0:256], lhsT=wkv[blk][:, k, cb * 128:(cb + 1) * 128], rhs=mnT[:, k, :],
                                       start=(k == 0), stop=(k == 7))
                    return ins
                P.add("pe", mm, reads=[("wkv", blk), ("mnT", 0), ("mnT", 1)], writes=[("ps", bank)])
                P.add("act", lambda e, fc=fc, bank=bank: e.copy(out=kcT[:, fc, :], in_=ps[bank][:, 0:256]),
                      reads=[("ps", bank)], writes=[("kcT", fc)])
        for blk in range(2):
            dma("pool", wkv[blk][:], wckv_d[:, 1024 + blk * 512:1024 + (blk + 1) * 512].rearrange("(k p) n -> p k n", p=128),
                writes=[("wkv", blk)])
        for blk in range(2):
            dma("pool", wq[blk][:], wcq_d[:, blk * 512:(blk + 1) * 512].rearrange("(k p) n -> p k n", p=128),
                writes=[("wq", blk)])
        for blk in range(2):
            dma("pool", wo[blk][:], wcout_d[:, blk * 512:(blk + 1) * 512].rearrange("(k p) n -> p k n", p=128),
                writes=[("wo", blk)])
        for blk in range(2):
            for mc in range(2):
                bank = P.nextbank()

                def mm(e, blk=blk, mc=mc, bank=bank):
                    for k in range(8):
                        ins = e.matmul(ps[bank][:], lhsT=mnT[:, k, mc * 128:(mc + 1) * 128], rhs=wkv[blk][:, k, :],
                                       start=(k == 0), stop=(k == 7))
                    return ins
                P.add("pe", mm, reads=[("wkv", blk), ("mnT", mc)], writes=[("ps", bank)])
                P.add("dve", lambda e, mc=mc, blk=blk, bank=bank: e.tensor_copy(out=Vc[:, mc, blk * 512:(blk + 1) * 512],
                                                                                in_=ps[bank][:]),
                      reads=[("ps", bank)], writes=[("Vc", mc, blk)])

        def qproj(T, fc):
            par = T % 2
            blk, cb = fc // 4, fc % 4
            bank = P.nextbank()

            def mm(e):
                for k in range(8):
                    ins = e.matmul(ps[bank][:], lhsT=wq[blk][:, k, cb * 128:(cb + 1) * 128], rhs=XT[par][:, k, :],
                                   start=(k == 0), stop=(k == 7))
                return ins
            P.add("pe", mm, reads=[("wq", blk)] + [("XTc", par, c) for c in range(4)], writes=[("ps", bank)])
            P.add("act", lambda e: e.activation(out=qcT[par][:, fc, :], in_=ps[bank][:], func=AF.Copy, scale=1.0 / 16.0),
                  reads=[("ps", bank)], writes=[("qcT", par, fc)])

        def outproj(T, ts):
            par = T % 2
            for dh in range(2):
                bank = P.nextbank()

                def mm(e, dh=dh, bank=bank):
                    for k in range(8):
                        ins = e.matmul(ps[bank][:], lhsT=oT[par][:, k, ts * 128:(ts + 1) * 128], rhs=wo[dh][:, k, :],
                                       start=(k == 0), stop=(k == 7))
                    return ins
                P.add("pe", mm, reads=[("wo", dh), ("oT", par, ts, 0), ("oT", par, ts, 1)], writes=[("ps", bank)])
                t = 4 * T + ts
                P.add("dve", lambda e, bank=bank, t=t, dh=dh: e.tensor_tensor(
                    out=H[:, t, dh * 512:(dh + 1) * 512], in0=ps[bank][:], in1=H[:, t, dh * 512:(dh + 1) * 512], op=ALU.add),
                    reads=[("ps", bank), ("H", t)], writes=[("H", t)])

        def cnorm(T):
            par = T % 2
            norm_tiles(list(range(4 * T, 4 * T + 4)), XT[par], lambda i: (i - 4 * T) * 128, xkey=("XTc", par), evac_eng="dve")

        def A(n):
            T, ts = divmod(n, 4)
            par = T % 2
            sbk = [P.nextbank(), P.nextbank()]
            P.reserved.update(sbk)

            def mm(e):
                for h in range(4):
                    for c in range(2):
                        ins = e.matmul(ps[sbk[h // 2]][:, (h % 2) * 256:(h % 2 + 1) * 256],
                                       lhsT=qcT[par][:, 2 * h + c, ts * 128:(ts + 1) * 128],
                                       rhs=kcT[:, 2 * h + c, :], start=(c == 0), stop=(c == 1))
                return ins
            P.add("pe", mm, reads=[("qcT", par, k) for k in range(8)] + [("kcT", k) for k in range(8)],
                  writes=[("ps", sbk[0]), ("ps", sbk[1])])
            return sbk

        def B(n, sbk):
            b_ = n % 2
            for i in range(2):
                pv = ps[sbk[i]][:].rearrange("p (h m) -> p h m", h=2)
                P.add("dve", lambda e, pv=pv, i=i: e.tensor_reduce(out=mx[b_][:, 2 * i:2 * i + 2], in_=pv, axis=AX.X, op=ALU.max),
                      reads=[("ps", sbk[i])], writes=[("mx", b_, i)])
            P.add("dve", lambda e: e.tensor_scalar(out=mx[b_][:], in0=mx[b_][:], scalar1=-1.0, scalar2=None, op0=ALU.mult),
                  reads=[("mx", b_, 0), ("mx", b_, 1)], writes=[("mx", b_, 0), ("mx", b_, 1)])
            for h in range(4):
                P.add("act", lambda e, h=h: e.activation(
                    out=Pf[b_][:, h, :], in_=ps[sbk[h // 2]][:, (h % 2) * 256:(h % 2 + 1) * 256],
                    func=AF.Exp, bias=mx[b_][:, h:h + 1], accum_out=sm[b_][:, h:h + 1]),
                    reads=[("ps", sbk[h // 2]), ("mx", b_, h // 2)], writes=[("Pf", b_, h), ("sm", b_, h)])
            P.reserved.difference_update(sbk)
            P.add("dve", lambda e: e.reciprocal(out=rs[b_][:], in_=sm[b_][:]),
                  reads=[("sm", b_, h) for h in range(4)], writes=[("rs", b_)])
            P.add("dve", lambda e: e.tensor_tensor(out=Pn[b_][:], in0=Pf[b_][:],
                                                   in1=rs[b_][:].unsqueeze(2).to_broadcast([128, 4, 256]), op=ALU.mult),
                  reads=[("Pf", b_, h) for h in range(4)] + [("rs", b_)], writes=[("Pn", b_)])

        def C(n):
            b_ = n % 2
            bank2 = P.nextbank()
            pview = ps[bank2][:].bitcast(BF16)

            def tr(e):
                for h in range(4):
                    for mc in range(2):
                        j = h * 2 + mc
                        ins = e.transpose(pview[:, j * 128:(j + 1) * 128], Pn[b_][:, h, mc * 128:(mc + 1) * 128], ident[:])
                return ins
            P.add("pe", tr, reads=[("Pn", b_), ("ident",)], writes=[("ps", bank2)])
            P.add("act", lambda e: e.copy(out=PnT[b_][:], in_=pview.rearrange("p (j c) -> p j c", j=8)),
                  reads=[("ps", bank2)], writes=[("PnT", b_)])

        def Dd(n):
            T, ts = divmod(n, 4)
            par = T % 2
            b_ = n % 2
            obk = [P.nextbank(), P.nextbank()]

            def mm2(e):
                for h in range(4):
                    for c in range(2):
                        f = 2 * h + c
                        for mc in range(2):
                            ins = e.matmul(ps[obk[f // 4]][:, (f % 4) * 128:(f % 4 + 1) * 128],
                                           lhsT=Vc[:, mc, f * 128:(f + 1) * 128],
                                           rhs=PnT[b_][:, h * 2 + mc, :], start=(mc == 0), stop=(mc == 1))
                return ins
            P.add("pe", mm2, reads=[("PnT", b_)] + [("Vc", mc, b) for mc in range(2) for b in range(2)],
                  writes=[("ps", obk[0]), ("ps", obk[1])])
            for i in range(2):
                P.add("dve", lambda e, i=i: e.tensor_copy(
                    out=oT[par][:, 4 * i:4 * i + 4, ts * 128:(ts + 1) * 128],
                    in_=ps[obk[i]][:].rearrange("p (f t) -> p f t", f=4)),
                    reads=[("ps", obk[i])], writes=[("oT", par, ts, i)])

        cnorm(0)
        for fc in range(8):
            qproj(0, fc)
        cur = A(0)
        B(0, cur)
        qsched = {0: [0, 1, 2], 1: [3, 4, 5], 2: [6, 7], 3: []}
        for n in range(16):
            T, ts = divmod(n, 4)
            if ts == 0 and T < 3:
                cnorm(T + 1)
            nxt = A(n + 1) if n + 1 < 16 and (n + 1) % 4 != 0 else None
            C(n)
            if nxt is not None:
                B(n + 1, nxt)
            if T < 3:
                for fc in qsched[ts]:
                    qproj(T + 1, fc)
            if n + 1 < 16 and (n + 1) % 4 == 0:
                nxt = A(n + 1)
                B(n + 1, nxt)
            if n >= 1:
                outproj(*divmod(n - 1, 4))
            Dd(n)
        prev = (3, 3)
        outproj(*prev)
        EM.flush()
        st.close()
        XNB[:] = [xn0]

    if stages >= 2:
        mixer_phase()
    if stages >= 3:
        cross_phase()
    if stages >= 4:
        ffn_phase(w2gu_d, w2d_d, 4, "f2", fin=final_norm)
    if stages >= 4 and final_norm:
        EM.flush()
        gstack.close()
        return nc

    EM.flush()
    st = contextlib.ExitStack()
    ob = [salloc(st, f"ob{i}", [128, D], F32) for i in range(2)]
    if final_norm:
        load_gb(5)
        for i in range(NT):
            P.add("act", lambda e, i=i: e.activation(out=XNB[0][:], in_=H[:, i, :], func=AF.Square,
                                                     accum_out=ss[:, i:i + 1]),
                  reads=[("H", i)], writes=[("ss", i), ("xn", 0)])
        P.add("dve", lambda e: e.tensor_scalar(out=rstd[:], in0=ss[:], scalar1=1.0 / D, scalar2=EPS,
                                               op0=ALU.mult, op1=ALU.add),
              reads=[("ss", i) for i in range(NT)], writes=[("rstd",)])
        P.add("act", lambda e: e.activation(out=rstd[:], in_=rstd[:], func=AF.Ln), reads=[("rstd",)], writes=[("rstd",)])
        P.add("act", lambda e: e.activation(out=rstd[:], in_=rstd[:], func=AF.Exp, scale=-0.5),
              reads=[("rstd",)], writes=[("rstd",)])
    outs = []
    for i in range(NT):
        b = i % 2
        if final_norm:
            P.add("dve", lambda e, i=i, b=b: e.scalar_tensor_tensor(out=ob[b][:], in0=H[:, i, :], scalar=rstd[:, i:i + 1],
                                                                    in1=gb[:], op0=ALU.mult, op1=ALU.mult),
                  reads=[("H", i), ("rstd",), ("gb",)], writes=[("ob", b)])
        else:
            P.add("dve", lambda e, i=i, b=b: e.tensor_copy(out=ob[b][:], in_=H[:, i, :]),
                  reads=[("H", i)], writes=[("ob", b)])
        outs.append(dma("sp", y_d[i * 128:(i + 1) * 128, :], ob[b][:], reads=[("ob", b)], writes=[("y", i)]))
    P.add("sp", None, reads=[("y", i) for i in range(NT)], writes=[("done",)])

    EM.flush()
    st.close()
    gstack.close()
    return nc


class Emitter:
    def __init__(self, nc, P, sems, dsems):
        self.nc, self.P, self.sems, self.dsems = nc, P, sems, dsems
        self.cnt = {e: 0 for e in sems}
        self.dq = {q: 0 for q in dsems}
        self.dslot = {q: [0] * ND for q in dsems}
        self.dlast = {q: [None] * ND for q in dsems}
        self.waited = {}
        self.done = 0

    def flush(self):
        P = self.P
        ops = P.ops[self.done:]
        self.done = len(P.ops)
        for op in P.lastw.values():
            op.needed = True
        for rd in P.readers.values():
            for d in rd[0].values():
                d.needed = True
        for op in ops:
            if op.is_dma:
                q = op.eng
                s_ = self.dq[q] % ND
                self.dq[q] += 1
                self.dslot[q][s_] += 1
                op.sem = self.dsems[q][s_]
                op.val = 16 * self.dslot[q][s_]
                op.prev = self.dlast[q][s_]
                self.dlast[q][s_] = op
            elif op.needed and op.fn is not None:
                self.cnt[op.eng] += 1
                op.sem = self.sems[op.eng]
                op.val = self.cnt[op.eng]
        engmap = {"pe": "tensor", "act": "scalar", "dve": "vector", "pool": "gpsimd", "sp": "sync"}
        byeng = dict((e, []) for e in engmap)
        for op in ops:
            byeng[op.eng].append(op)
        waited = self.waited
        with self.nc.Block() as block:
            for eng, bname in engmap.items():
                eops = byeng[eng]
                if not eops:
                    continue

                def body(e, eops=eops, eng=eng):
                    w = waited.setdefault(eng, {})

                    def wait(sem, val):
                        key = id(sem)
                        if w.get(key, (None, 0))[1] >= val:
                            return
                        w[key] = (sem, val)
                        e.wait_ge(sem, val)

                    for op in eops:
                        if op.prev is not None:
                            wait(op.prev.sem, op.prev.val)
                        for d in op.deps:
                            wait(d.sem, d.val)
                        if op.fn is None:
                            continue
                        ins = op.fn(e)
                        if op.is_dma:
                            ins.then_inc(op.sem, 16)
                        elif op.needed:
                            ins.then_inc(op.sem, 1)

                getattr(block, bname)(body)


def _host_consts():
    bf = ml_dtypes.bfloat16
    ident = np.eye(128, dtype=np.float32).astype(bf)
    j = np.arange(128)[:, None]
    s = np.arange(128)[None, :]
    triA = np.where(j >= s, -1.0, 0.0).astype(np.float32).astype(bf)
    triB = np.where(j < s, -1.0, 0.0).astype(np.float32).astype(bf)
    maskb = np.where(j >= s, NEG, 0.0).astype(np.float32).astype(bf)
    return ident, triA, triB, maskb


_CACHE = {}


def kernel(x, mem, g_ffn1, w_ffn1_gu, w_ffn1_down, g_mix, w_in, b_gate, conv_w, w_conv_out, w_attn_out, w_o,
           g_cross, g_mem, w_cq, w_ckv, w_co, g_ffn2, w_ffn2_gu, w_ffn2_down, g_final, _stages=5, _final_norm=True):
    f = lambda a: np.ascontiguousarray(np.asarray(a, dtype=np.float32))
    x = f(x)
    mem = f(mem)
    n = 8
    key = (_stages, _final_norm)
    if key not in _CACHE:
        _CACHE[key] = build_program(_stages, _final_norm)
    nc = _CACHE[key]
    ident, triA, triB, maskb = _host_consts()
    gv = np.stack([np.broadcast_to(f(g)[None, :], (128, D)) for g in (g_ffn1, g_mix, g_cross, g_mem, g_ffn2, g_final)])
    gv = np.ascontiguousarray(gv)
    bg = np.ascontiguousarray(f(b_gate).reshape(16, 128).T)
    cw = np.ascontiguousarray(f(conv_w).T.reshape(8, 128, 3).transpose(1, 0, 2))
    shared = {
        "gv": gv, "w_ffn1_gu": f(w_ffn1_gu), "w_ffn1_down": f(w_ffn1_down), "w_in": f(w_in),
        "w_conv_out": f(w_conv_out), "w_attn_out": f(w_attn_out), "w_o": f(w_o), "w_cq": f(w_cq),
        "w_ckv": f(w_ckv), "w_co": f(w_co), "w_ffn2_gu": f(w_ffn2_gu), "w_ffn2_down": f(w_ffn2_down),
        "bgate": bg, "convw": cw, "ident": ident, "triA": triA, "triB": triB, "maskb": maskb,
    }
    in_maps = []
    for c in range(n):
        m = dict(shared)
        m["x"] = x[c]
        m["mem"] = mem[c]
        in_maps.append(m)
    res = run_bass_kernel_spmd(nc, in_maps, core_ids=list(range(n)))
    return np.stack([r["y"] for r in res.results], axis=0)
```
